# Optimizing a Trainium2 kernel written in Bass

```python
import math
import jax, jax.numpy as jnp
from jax import lax
import numpy as np

D_MODEL = 1024
BATCH = 8
SEQ = 4096
DEPTH = 1

GRID_W = 64
CTX_LEN = 256
N_HEADS = 8
HEAD_DIM = 64
ATTN_WIDTH = N_HEADS * HEAD_DIM
WIN_H_MAX = 8
WIN_W = 16
Q_COL_BLOCK = 16
BAND_W = 32
ROPE_BASE = 10000.0
SSM_WIDTH = 512
SSM_GROUP = 16
SSM_GROUPS = SSM_WIDTH // SSM_GROUP
SSM_STATE = 64
DT_MIN = 0.001
DT_MAX = 0.1
FFN_HIDDEN = ((8 * D_MODEL // 3 + 255) // 256) * 256
IN_COLS = 3 * ATTN_WIDTH + SSM_WIDTH + 2 * D_MODEL
N_MOD = 6
NORM_EPS = 1e-6
NEG_BIG = -1e30

kernel_name = 'hybrid_natten_s5_dit_block'


def rmsnorm(x, g):
    xf = x.astype(jnp.float32)
    xf = xf * lax.rsqrt(jnp.mean(xf * xf, axis=-1, keepdims=True) + NORM_EPS)
    return xf.astype(x.dtype) * g


def modulate(h, shift, scale):
    return h * (1.0 + scale) + shift


def heads(t):
    return t.reshape(t.shape[0], t.shape[1], N_HEADS, HEAD_DIM)


def axial_rope(x):
    L = x.shape[1]
    t = jnp.arange(L, dtype=jnp.int32)
    n_freq = HEAD_DIM // 4
    inv_freq = ROPE_BASE ** (-jnp.arange(n_freq, dtype=jnp.float32) / n_freq)

    def rotate(xa, pos):
        ang = pos.astype(jnp.float32)[:, None] * inv_freq[None, :]
        cos = jnp.cos(ang)[None, :, None, :].astype(x.dtype)
        sin = jnp.sin(ang)[None, :, None, :].astype(x.dtype)
        x1, x2 = xa[..., :n_freq], xa[..., n_freq:]
        return jnp.concatenate([x1 * cos - x2 * sin, x2 * cos + x1 * sin], axis=-1)

    half = HEAD_DIM // 2
    return jnp.concatenate([rotate(x[..., :half], t // GRID_W), rotate(x[..., half:], t % GRID_W)], axis=-1)


def neighbourhood_attention(q, k, v, k_ctx, v_ctx, rpb):
    B, L, H, d = q.shape
    rows = L // GRID_W
    kh = min(WIN_H_MAX, rows)
    n_cb = GRID_W // Q_COL_BLOCK
    qg = q.reshape(B, rows, n_cb, Q_COL_BLOCK, H, d)
    kg = k.reshape(B, rows, GRID_W, H, d)
    vg = v.reshape(B, rows, GRID_W, H, d)
    qcol = np.arange(GRID_W).reshape(n_cb, Q_COL_BLOCK)
    col_start = np.clip(qcol - WIN_W // 2, 0, GRID_W - WIN_W)
    band_start = np.clip(np.arange(n_cb) * Q_COL_BLOCK - WIN_W // 2, 0, GRID_W - BAND_W)
    band_idx = band_start[:, None] + np.arange(BAND_W)[None, :]
    kcol = band_idx[:, None, :]
    in_win = (kcol >= col_start[..., None]) & (kcol < col_start[..., None] + WIN_W)
    dc_idx = np.clip(kcol - qcol[..., None], -(WIN_W - 1), WIN_W - 1) + WIN_W - 1
    scale = HEAD_DIM ** -0.5
    n_loc = kh * BAND_W

    def one_row(r):
        rs = jnp.clip(r - kh // 2, 0, rows - kh)
        k_band = lax.dynamic_slice_in_dim(kg, rs, kh, axis=1)[:, :, band_idx]
        v_band = lax.dynamic_slice_in_dim(vg, rs, kh, axis=1)[:, :, band_idx]
        q_r = lax.dynamic_index_in_dim(qg, r, axis=1, keepdims=False)
        s_loc = jnp.einsum('bcqhd,bkcnhd->bhcqkn', q_r, k_band).astype(jnp.float32) * scale
        dr_idx = rs + jnp.arange(kh) - r + WIN_H_MAX - 1
        bias = rpb[:, dr_idx][:, :, dc_idx]
        bias = jnp.transpose(bias, (0, 2, 3, 1, 4)).astype(jnp.float32)
        s_loc = jnp.where(in_win[:, :, None, :], s_loc + bias, NEG_BIG)
        s_ctx = jnp.einsum('bcqhd,bmhd->bhcqm', q_r, k_ctx).astype(jnp.float32) * scale
        s = jnp.concatenate([s_loc.reshape(B, H, n_cb, Q_COL_BLOCK, n_loc), s_ctx], axis=-1)
        p = jax.nn.softmax(s, axis=-1).astype(q.dtype)
        p_loc = p[..., :n_loc].reshape(B, H, n_cb, Q_COL_BLOCK, kh, BAND_W)
        p_ctx = p[..., n_loc:]
        o = (jnp.einsum('bhcqkn,bkcnhd->bcqhd', p_loc, v_band)
             + jnp.einsum('bhcqm,bmhd->bcqhd', p_ctx, v_ctx))
        return o.reshape(B, GRID_W, H, d)

    out = lax.map(one_row, jnp.arange(rows, dtype=jnp.int32))
    return jnp.moveaxis(out, 0, 1).reshape(B, L, H * d)


def context_attention(q, k, v):
    s = jnp.einsum('bqhd,bkhd->bhqk', q, k).astype(jnp.float32) * (HEAD_DIM ** -0.5)
    p = jax.nn.softmax(s, axis=-1).astype(q.dtype)
    return jnp.einsum('bhqk,bkhd->bqhd', p, v).reshape(q.shape[0], q.shape[1], ATTN_WIDTH)


def s5_discretise(lam_re, lam_im, log_dt, b_re, b_im):
    dt = jnp.exp(log_dt)[:, None]
    mag = jnp.exp(lam_re * dt)
    ang = lam_im * dt
    lb_re, lb_im = mag * jnp.cos(ang), mag * jnp.sin(ang)
    den = lam_re * lam_re + lam_im * lam_im
    nr, ni = lb_re - 1.0, lb_im
    f_re = (nr * lam_re + ni * lam_im) / den
    f_im = (ni * lam_re - nr * lam_im) / den
    bb_re = f_re[..., None] * b_re - f_im[..., None] * b_im
    bb_im = f_re[..., None] * b_im + f_im[..., None] * b_re
    return lb_re, lb_im, bb_re, bb_im


def _linear_combine(e1, e2):
    a1r, a1i, b1r, b1i = e1
    a2r, a2i, b2r, b2i = e2
    return (a2r * a1r - a2i * a1i, a2r * a1i + a2i * a1r,
            a2r * b1r - a2i * b1i + b2r, a2r * b1i + a2i * b1r + b2i)


def diag_scan(lb_re, lb_im, bu_re, bu_im):
    L = bu_re.shape[1]
    a_re = jnp.broadcast_to(lb_re, (1, L) + lb_re.shape)
    a_im = jnp.broadcast_to(lb_im, (1, L) + lb_im.shape)
    return lax.associative_scan(_linear_combine, (a_re, a_im, bu_re, bu_im), axis=1)


def s5_direction(u_ctx, u_lat, lam_re, lam_im, log_dt, b_re, b_im, c_re, c_im, reverse, need_ctx):
    lb_re, lb_im, bb_re, bb_im = s5_discretise(lam_re, lam_im, log_dt, b_re, b_im)

    def drive(u):
        u = jnp.flip(u, axis=1) if reverse else u
        return (jnp.einsum('blgh,gph->blgp', u, bb_re), jnp.einsum('blgh,gph->blgp', u, bb_im))

    def readout(hr, hi):
        y = jnp.einsum('blgp,ghp->blgh', hr, c_re) - jnp.einsum('blgp,ghp->blgh', hi, c_im)
        return jnp.flip(y, axis=1) if reverse else y

    ur, ui = drive(u_ctx)
    _, _, hc_re, hc_im = diag_scan(lb_re, lb_im, ur, ui)
    h0_re, h0_im = hc_re[:, -1:], hc_im[:, -1:]
    ur, ui = drive(u_lat)
    ac_re, ac_im, hl_re, hl_im = diag_scan(lb_re, lb_im, ur, ui)
    hl_re = hl_re + ac_re * h0_re - ac_im * h0_im
    hl_im = hl_im + ac_re * h0_im + ac_im * h0_re
    y_ctx = readout(hc_re, hc_im) if need_ctx else None
    return y_ctx, readout(hl_re, hl_im)


def s5_branch(u_ctx, u_lat, lam_re, lam_im, log_dt, b_re, b_im, c_re, c_im, d_skip, w_glu, need_ctx):
    B, Lc, _ = u_ctx.shape
    L = u_lat.shape[1]
    ucg = u_ctx.reshape(B, Lc, SSM_GROUPS, SSM_GROUP)
    ulg = u_lat.reshape(B, L, SSM_GROUPS, SSM_GROUP)
    fwd = s5_direction(ucg, ulg, lam_re[0], lam_im[0], log_dt[0], b_re[0], b_im[0], c_re[0], c_im[0], False, need_ctx)
    bwd = s5_direction(ucg, ulg, lam_re[1], lam_im[1], log_dt[1], b_re[1], b_im[1], c_re[1], c_im[1], True, need_ctx)

    def finish(y_f, y_b, u):
        y = jax.nn.gelu((y_f + y_b).reshape(u.shape) + d_skip * u)
        val, gate = jnp.split(y @ w_glu, 2, axis=-1)
        return val * jax.nn.sigmoid(gate)

    y_lat = finish(fwd[1], bwd[1], u_lat)
    y_ctx = finish(fwd[0], bwd[0], u_ctx) if need_ctx else None
    return y_ctx, y_lat


def token_mixer(n_ctx, n_lat, w_in, rpb, lam_re, lam_im, log_dt, b_re, b_im, c_re, c_im, d_skip, w_glu,
                w_branch_attn, w_branch_ssm, w_out, need_ctx):
    cuts = [ATTN_WIDTH, 2 * ATTN_WIDTH, 3 * ATTN_WIDTH, 3 * ATTN_WIDTH + SSM_WIDTH,
            3 * ATTN_WIDTH + SSM_WIDTH + D_MODEL]
    qc, kc, vc, uc, gac, gsc = jnp.split(n_ctx @ w_in, cuts, axis=-1)
    ql, kl, vl, ul, gal, gsl = jnp.split(n_lat @ w_in, cuts, axis=-1)
    kc_h, vc_h = heads(kc), heads(vc)
    a_lat = neighbourhood_attention(axial_rope(heads(ql)), axial_rope(heads(kl)), heads(vl), kc_h, vc_h, rpb)
    s_ctx, s_lat = s5_branch(uc, ul, lam_re, lam_im, log_dt, b_re, b_im, c_re, c_im, d_skip, w_glu, need_ctx)

    def merge(a, s, ga, gs):
        return (jax.nn.sigmoid(ga) * (a @ w_branch_attn) + jax.nn.sigmoid(gs) * (s @ w_branch_ssm)) @ w_out

    out_lat = merge(a_lat, s_lat, gal, gsl)
    out_ctx = merge(context_attention(heads(qc), kc_h, vc_h), s_ctx, gac, gsc) if need_ctx else None
    return out_ctx, out_lat


def swiglu(h, w_ffn_in, w_ffn_out):
    a, b = jnp.split(h @ w_ffn_in, 2, axis=-1)
    return (jax.nn.silu(a) * b) @ w_ffn_out


def setup_inputs(seed: int = 0) -> dict:
    key = jax.random.key(seed)
    ks = jax.random.split(key, 26)
    f32 = jnp.float32

    def nrm(k, shape, s):
        return s * jax.random.normal(k, shape, f32)

    G, P, HG = SSM_GROUPS, SSM_STATE, SSM_GROUP
    n_idx = jnp.arange(P, dtype=f32)
    return {
        'x': nrm(ks[0], (BATCH, SEQ, D_MODEL), 1.0),
        'c': nrm(ks[1], (BATCH, D_MODEL), 1.0),
        'ctx': nrm(ks[2], (BATCH, CTX_LEN, D_MODEL), 1.0),
        'c_ctx': nrm(ks[3], (D_MODEL,), 1.0),
        'w_mod': nrm(ks[4], (DEPTH, D_MODEL, N_MOD * D_MODEL), D_MODEL ** -0.5),
        'b_mod': nrm(ks[5], (DEPTH, N_MOD * D_MODEL), 0.01),
        'attn_norm_g': 1.0 + nrm(ks[6], (DEPTH, D_MODEL), 0.01),
        'ffn_norm_g': 1.0 + nrm(ks[7], (DEPTH, D_MODEL), 0.01),
        'w_in': nrm(ks[8], (DEPTH, D_MODEL, IN_COLS), D_MODEL ** -0.5),
        'rel_pos_bias': nrm(ks[9], (DEPTH, N_HEADS, 2 * WIN_H_MAX - 1, 2 * WIN_W - 1), 0.1),
        'ssm_lambda_re': -0.5 + nrm(ks[10], (DEPTH, 2, G, P), 0.01),
        'ssm_lambda_im': math.pi * n_idx + nrm(ks[11], (DEPTH, 2, G, P), 0.01),
        'ssm_log_dt': jax.random.uniform(ks[12], (DEPTH, 2, G), f32, math.log(DT_MIN), math.log(DT_MAX)),
        'ssm_b_re': nrm(ks[13], (DEPTH, 2, G, P, HG), (2 * HG) ** -0.5),
        'ssm_b_im': nrm(ks[14], (DEPTH, 2, G, P, HG), (2 * HG) ** -0.5),
        'ssm_c_re': nrm(ks[15], (DEPTH, 2, G, HG, P), P ** -0.5),
        'ssm_c_im': nrm(ks[16], (DEPTH, 2, G, HG, P), P ** -0.5),
        'ssm_d': nrm(ks[17], (DEPTH, SSM_WIDTH), 1.0),
        'w_glu': nrm(ks[18], (DEPTH, SSM_WIDTH, 2 * SSM_WIDTH), SSM_WIDTH ** -0.5),
        'w_branch_attn': nrm(ks[19], (DEPTH, ATTN_WIDTH, D_MODEL), ATTN_WIDTH ** -0.5),
        'w_branch_ssm': nrm(ks[20], (DEPTH, SSM_WIDTH, D_MODEL), SSM_WIDTH ** -0.5),
        'w_out': nrm(ks[21], (DEPTH, D_MODEL, D_MODEL), D_MODEL ** -0.5),
        'w_ffn_in': nrm(ks[22], (DEPTH, D_MODEL, 2 * FFN_HIDDEN), D_MODEL ** -0.5),
        'w_ffn_out': nrm(ks[23], (DEPTH, FFN_HIDDEN, D_MODEL), FFN_HIDDEN ** -0.5),
        'final_norm_g': 1.0 + nrm(ks[24], (D_MODEL,), 0.01),
    }


def reference(x, c, ctx, c_ctx, w_mod, b_mod, attn_norm_g, ffn_norm_g, w_in, rel_pos_bias,
              ssm_lambda_re, ssm_lambda_im, ssm_log_dt, ssm_b_re, ssm_b_im, ssm_c_re, ssm_c_im, ssm_d,
              w_glu, w_branch_attn, w_branch_ssm, w_out, w_ffn_in, w_ffn_out, final_norm_g):
    h = x
    hc = ctx
    for layer in range(DEPTH):
        need_ctx = layer < DEPTH - 1
        sh1, sc1, g1, sh2, sc2, g2 = jnp.split((jax.nn.silu(c) @ w_mod[layer] + b_mod[layer])[:, None, :], N_MOD, axis=-1)
        csh1, csc1, cg1, csh2, csc2, cg2 = jnp.split(jax.nn.silu(c_ctx) @ w_mod[layer] + b_mod[layer], N_MOD, axis=-1)
        n_lat = modulate(rmsnorm(h, attn_norm_g[layer]), sh1, sc1)
        n_ctx = modulate(rmsnorm(hc, attn_norm_g[layer]), csh1, csc1)
        mix_ctx, mix_lat = token_mixer(n_ctx, n_lat, w_in[layer], rel_pos_bias[layer],
                                       ssm_lambda_re[layer], ssm_lambda_im[layer], ssm_log_dt[layer],
                                       ssm_b_re[layer], ssm_b_im[layer], ssm_c_re[layer], ssm_c_im[layer],
                                       ssm_d[layer], w_glu[layer], w_branch_attn[layer], w_branch_ssm[layer],
                                       w_out[layer], need_ctx)
        h = h + g1 * mix_lat
        h = h + g2 * swiglu(modulate(rmsnorm(h, ffn_norm_g[layer]), sh2, sc2), w_ffn_in[layer], w_ffn_out[layer])
        if need_ctx:
            hc = hc + cg1 * mix_ctx
            hc = hc + cg2 * swiglu(modulate(rmsnorm(hc, ffn_norm_g[layer]), csh2, csc2), w_ffn_in[layer], w_ffn_out[layer])
    return rmsnorm(h, final_norm_g)
```

```python
import math
from contextlib import ExitStack

import numpy as np
import concourse.bass as bass
import concourse.mybir as mybir
from concourse.bass_utils import run_bass_kernel_spmd

F32 = mybir.dt.float32
BF16 = mybir.dt.bfloat16
U8 = mybir.dt.uint8
I32 = mybir.dt.int32
AF = mybir.ActivationFunctionType
ALU = mybir.AluOpType
AX = mybir.AxisListType
DTSZ = {F32: 4, BF16: 2, U8: 1, I32: 4}

L = 4096
D = 1024
LC = 256
NTOK = L + LC
GW = 64
NH = 8
HD = 64
AW = 512
SW = 512
NG = 32
NP = 64
FH = 2816
EPS = 1e-6
TCH = 16
NCH = NTOK // TCH
UEXT = L + 2 * LC

ENGS = ("pe", "act", "dve", "pool", "sp")


class T:
    __slots__ = ("name", "w", "r")

    def __init__(self, name):
        self.name = name
        self.w = None
        self.r = {}


class Prog:
    def __init__(self, nc):
        self.nc = nc
        self.es = ExitStack()
        self.ops = {e: [] for e in ENGS}
        self.cnt = {}
        self.sems = {}
        self.seen = {e: {} for e in ENGS}
        self.marks = []
        for e in ENGS:
            self._sem("E_" + e)

    def _sem(self, key):
        if key not in self.sems:
            self.sems[key] = self.es.enter_context(self.nc.semaphore(key))
            self.cnt[key] = 0
        return self.sems[key]

    def sbuf(self, name, shape, dt):
        return self.es.enter_context(self.nc.sbuf_tensor(name, list(shape), dt))

    def psum(self, name, shape, dt):
        return self.es.enter_context(self.nc.psum_tensor(name, list(shape), dt))

    def emit(self, eng, fn, reads=(), writes=(), dma_sem=None):
        own = "E_" + eng
        deps = {}

        def add(ev, is_raw):
            if ev is None:
                return
            k, v = ev
            if k == own and (eng == "pe" or not is_raw):
                return
            if deps.get(k, 0) < v:
                deps[k] = v

        for t in reads:
            add(t.w, True)
        for t in writes:
            add(t.w, False)
            for k, v in t.r.items():
                add((k, v), False)
        waits = []
        seen = self.seen[eng]
        for k, v in deps.items():
            if seen.get(k, 0) < v:
                seen[k] = v
                waits.append((k, v))
        if dma_sem is not None:
            self._sem(dma_sem)
            self.cnt[dma_sem] += 16
            ev = (dma_sem, self.cnt[dma_sem])
            inc = (dma_sem, 16)
        else:
            self.cnt[own] += 1
            ev = (own, self.cnt[own])
            inc = (own, 1)
        self.ops[eng].append((waits, fn, inc))
        for t in reads:
            if t.r.get(ev[0], 0) < ev[1]:
                t.r[ev[0]] = ev[1]
        for t in writes:
            t.w = ev
            t.r = {}
        return ev

    def final_wait(self, eng, tiles):
        waits = []
        for t in tiles:
            if t.w is not None:
                waits.append(t.w)
            for k, v in t.r.items():
                waits.append((k, v))
        self.ops[eng].append((waits, None, None))

    def barrier(self):
        snap = dict(self.cnt)
        self.marks.append(snap)
        for e in ENGS:
            waits = []
            for k, v in snap.items():
                if v > 0 and k != "E_" + e and self.seen[e].get(k, 0) < v:
                    self.seen[e][k] = v
                    waits.append((k, v))
            if waits:
                self.ops[e].append((waits, None, None))

    def build(self):
        nc = self.nc
        needed = {}
        for e in ENGS:
            for waits, fn, inc in self.ops[e]:
                for k, v in waits:
                    if k.startswith("E_"):
                        needed.setdefault(k, set()).add(v)
        rank = {k: {v: i + 1 for i, v in enumerate(sorted(vs))} for k, vs in needed.items()}
        with nc.Block() as block:
            def run(e, handle):
                seq = 0
                own = "E_" + e
                myrank = rank.get(own, {})
                for waits, fn, inc in self.ops[e]:
                    for k, v in waits:
                        if k.startswith("E_"):
                            handle.wait_ge(self.sems[k], rank[k][v])
                        else:
                            handle.wait_ge(self.sems[k], v)
                    if fn is not None:
                        ins = fn(handle)
                        if inc[0] == own:
                            seq += 1
                            if seq in myrank:
                                ins.then_inc(self.sems[own], 1)
                        else:
                            ins.then_inc(self.sems[inc[0]], inc[1])

            @block.tensor
            def _(h):
                run("pe", h)

            @block.scalar
            def _(h):
                run("act", h)

            @block.vector
            def _(h):
                run("dve", h)

            @block.gpsimd
            def _(h):
                run("pool", h)

            @block.sync
            def _(h):
                run("sp", h)
        self.es.close()

    def mm(self, out, lhsT, rhs, start, stop, R, W, tp=None):
        if tp is None:
            self.emit("pe", lambda e: e.matmul(out, lhsT=lhsT, rhs=rhs, start=start, stop=stop), R, W)
        else:
            self.emit("pe", lambda e: e.matmul(out, lhsT=lhsT, rhs=rhs, start=start, stop=stop, tile_position=tp), R, W)

    def tr(self, out, in_, ident, R, W):
        self.emit("pe", lambda e: e.transpose(out=out, in_=in_, identity=ident), R, W)

    def act(self, out, in_, func, R, W, bias=None, scale=None, accum=None):
        kw = {}
        if bias is not None:
            kw["bias"] = bias
        if scale is not None:
            kw["scale"] = scale
        if accum is not None:
            kw["accum_out"] = accum
        self.emit("act", lambda e: e.activation(out=out, in_=in_, func=func, **kw), R, W)

    def tt(self, eng, out, in0, in1, op, R, W):
        self.emit(eng, lambda e: e.tensor_tensor(out=out, in0=in0, in1=in1, op=op), R, W)

    def ts(self, eng, out, in0, s1, s2, op0, op1, R, W, accum=None):
        if s2 is None:
            self.emit(eng, lambda e: e.tensor_scalar(out=out, in0=in0, scalar1=s1, scalar2=None, op0=op0), R, W)
        elif accum is None:
            self.emit(eng, lambda e: e.tensor_scalar(out=out, in0=in0, scalar1=s1, scalar2=s2, op0=op0, op1=op1), R, W)
        else:
            self.emit(eng, lambda e: e.tensor_scalar(out=out, in0=in0, scalar1=s1, scalar2=s2, op0=op0, op1=op1,
                                                     accum_out=accum), R, W)

    def stt(self, eng, out, in0, scalar, in1, op0, op1, R, W):
        self.emit(eng, lambda e: e.scalar_tensor_tensor(out=out, in0=in0, scalar=scalar, in1=in1, op0=op0, op1=op1), R, W)

    def copy(self, eng, out, in_, R, W):
        if eng == "act":
            self.emit("act", lambda e: e.activation(out=out, in_=in_, func=AF.Copy), R, W)
        else:
            self.emit(eng, lambda e: e.tensor_copy(out=out, in_=in_), R, W)

    def memset(self, eng, ap, val, W):
        self.emit(eng, lambda e: e.memset(ap, val), (), W)

    def dma(self, eng, out, in_, sem, R, W):
        self.emit(eng, lambda e: e.dma_start(out=out, in_=in_), R, W, dma_sem=sem)


class Arena:
    def __init__(self, P, nbytes):
        self.t = P.sbuf("arena", [128, nbytes], U8)
        self.n = nbytes
        self.off = 0

    def alloc(self, shape, dt):
        n = int(np.prod(shape)) * DTSZ[dt]
        n_al = (n + 63) // 64 * 64
        assert self.off + n_al <= self.n, f"arena overflow {self.off}+{n_al}>{self.n}"
        v = self.t[:, self.off:self.off + n].bitcast(dt)
        self.off += n_al
        if len(shape) == 2:
            v = v.rearrange("p (a b) -> p a b", a=shape[0])
        elif len(shape) == 3:
            v = v.rearrange("p (a b c) -> p a b c", a=shape[0], b=shape[1])
        elif len(shape) == 4:
            v = v.rearrange("p (a b c d) -> p a b c d", a=shape[0], b=shape[1], c=shape[2])
        return v

    def mark(self):
        return self.off

    def release(self, m):
        self.off = m


ARENA_BYTES = 207 * 1024


def build_program(debug=False, stop_after=None):
    nc = bass.Bass("TRN2", target_bir_lowering=False)
    P = Prog(nc)

    def din(name, shape, dt=F32):
        return nc.dram_tensor(name, list(shape), dt, kind="ExternalInput").ap()

    skind = "ExternalOutput" if debug else "Internal"

    def dscr(name, shape, dt):
        return nc.dram_tensor(name, list(shape), dt, kind=skind).ap()

    x_d = din("x", [L, D])
    ctx_d = din("ctx", [LC, D])
    cvec_d = din("cvec", [128, 8, 2])
    wmod_d = din("w_mod", [D, 6 * D])
    bmod_d = din("b_mod", [128, 48])
    gA_d = din("gA", [128, 8])
    gF_d = din("gF", [128, 8])
    gFin_d = din("gFin", [128, 8])
    win_d = din("w_in_ext", [D, 3072])
    wgate_d = din("w_gate", [D, 2048])
    cos_d = din("rope_cos", [128, L])
    sin_d = din("rope_sin", [128, L])
    tb_d = din("bias_tb", [128, 4, 15, 64])
    mk_d = din("bias_mask", [128, 64])
    lre_d = din("lam_re", [32, 128])
    lim_d = din("lam_im", [32, 128])
    ldt_d = din("log_dt", [32, 2])
    bre_d = din("b_re", [2, 32, 64, 16])
    bim_d = din("b_im", [2, 32, 64, 16])
    cre_d = din("c_re", [2, 32, 16, 64])
    cim_d = din("c_im", [2, 32, 16, 64])
    dsk_d = din("d_skip", [128, 4])
    wglu_d = din("w_glu", [SW, 2 * SW])
    wba_d = din("w_ba", [AW, D])
    wbs_d = din("w_bs", [SW, D])
    wout_d = din("w_out", [D, D])
    wfi_d = din("w_ffn_in", [D, 2 * FH])
    wfo_d = din("w_ffn_out", [FH, D])
    out_d = nc.dram_tensor("out", [L, D], F32, kind="ExternalOutput").ap()

    qT_s = dscr("qT_s", [4, 128, L], BF16)
    uT_s = dscr("uT_s", [4, 128, UEXT], BF16)
    aT_s = dscr("aT_s", [4, 128, L], BF16)
    h_s = dscr("h_s", [L, D], F32)
    y_o = dscr("y_o", [2, 4, 128, L], BF16)
    if debug:
        kT_o = dscr("kT_o", [4, 128, NTOK], BF16)
        v_o = dscr("v_o", [34, 128, 512], BF16)
        mod_o = dscr("mod_o", [128, 96], F32)
        gb_o = dscr("gb_o", [128, 3 * D], F32)

    A = Arena(P, ARENA_BYTES)
    ps = P.psum("ps", [128, 4096], F32)

    def bank(b, n=512, off=0):
        return ps[:, b * 512 + off:b * 512 + off + n]

    def bank_bf(b):
        return ps[:, b * 512:(b + 1) * 512].bitcast(BF16)

    Tb = [T(f"bank{b}") for b in range(8)]

    ident_f = A.alloc([128], F32)
    ident_b = A.alloc([128], BF16)
    ones_f = A.alloc([128], F32)
    modv = A.alloc([48, 2], F32)
    A1l = A.alloc([8], F32)
    A1c = A.alloc([8], F32)
    A2 = A.alloc([8], F32)
    gA = A.alloc([8], F32)
    gF = A.alloc([8], F32)
    gFin = A.alloc([8], F32)
    g1b = A.alloc([D], F32)
    g2b = A.alloc([D], F32)
    fgb = A.alloc([D], F32)
    Tconst = T("const")
    Tdbg = T("dbg")
    Tmod = T("mod")
    Tgb = T("gb")

    P.memset("pool", ident_f, 1.0, [Tconst])
    P.emit("pool", lambda e: e.affine_select(out=ident_f, in_=ident_f, pattern=[[-1, 128]], compare_op=ALU.is_equal,
                                             fill=0.0, base=0, channel_multiplier=1), [Tconst], [Tconst])
    P.copy("pool", ident_b, ident_f, [Tconst], [Tconst])
    P.memset("pool", ones_f, 1.0, [Tconst])

    m0 = A.mark()
    cv = A.alloc([8, 2], F32)
    cvb = A.alloc([8, 2], BF16)
    bmod = A.alloc([48], F32)
    wmb = [A.alloc([8, 1024], BF16) for _ in range(2)]
    Tcv, Tbm = T("cv"), T("bmod")
    Twm = [T("wm0"), T("wm1")]
    P.dma("sp", cv, cvec_d, "ld_cv", [], [Tcv])
    P.dma("sp", bmod, bmod_d, "ld_cv", [], [Tbm])
    P.dma("sp", gA, gA_d, "ld_cv", [], [Tmod])
    P.dma("sp", gF, gF_d, "ld_cv", [], [Tmod])
    P.dma("sp", gFin, gFin_d, "ld_cv", [], [Tmod])
    P.act(cvb, cv, AF.Silu, [Tcv], [Tcv])
    wm_v = wmod_d.rearrange("(k p) m -> p k m", p=128)
    for blk in range(6):
        bi = blk % 2
        P.dma("pool", wmb[bi], wm_v[:, :, blk * 1024:(blk + 1) * 1024], f"ld_wm{bi}", [], [Twm[bi]])
        for ml in range(8):
            m = blk * 8 + ml
            for k in range(8):
                P.mm(bank(0, 2, m * 2), wmb[bi][:, k, ml * 128:(ml + 1) * 128], cvb[:, k, :], k == 0, k == 7,
                     [Twm[bi], Tcv], [Tb[0]])
    P.tt("dve", modv, bank(0, 96).rearrange("p (m j) -> p m j", j=2), bmod.unsqueeze(2).to_broadcast([128, 48, 2]),
         ALU.add, [Tb[0], Tbm], [Tmod])
    for (dst, j) in ((A1l, 0), (A1c, 1)):
        P.ts("dve", dst, modv[:, 8:16, j], 1.0, None, ALU.add, None, [Tmod], [Tmod])
        P.tt("dve", dst, dst, gA, ALU.mult, [Tmod], [Tmod])
    P.ts("dve", A2, modv[:, 32:40, 0], 1.0, None, ALU.add, None, [Tmod], [Tmod])
    P.tt("dve", A2, A2, gF, ALU.mult, [Tmod], [Tmod])
    B1l = modv[:, 0:8, 0]
    B1c = modv[:, 0:8, 1]
    B2 = modv[:, 24:32, 0]
    diag = [A.alloc([128], F32) for _ in range(2)]
    Tdg = [T("dg0"), T("dg1")]
    n = 0
    for (dst, vec) in ((g1b, modv[:, 16:24, 0]), (g2b, modv[:, 40:48, 0]), (fgb, gFin)):
        for k in range(8):
            di = n % 2
            bk = 1 + (n % 2)
            P.ts("dve", diag[di], ident_f, vec[:, k:k + 1], None, ALU.mult, None, [Tmod, Tconst], [Tdg[di]])
            P.mm(bank(bk, 128), ones_f, diag[di], True, True, [Tconst, Tdg[di]], [Tb[bk]])
            P.copy("act", dst[:, k * 128:(k + 1) * 128], bank(bk, 128), [Tb[bk]], [Tgb])
            n += 1
    if debug:
        P.dma("sp", mod_o, modv.rearrange("p m j -> p (m j)"), "dbg", [Tmod], [Tdbg])
        P.dma("sp", gb_o[:, 0:D], g1b, "dbg", [Tgb], [Tdbg])
        P.dma("sp", gb_o[:, D:2 * D], g2b, "dbg", [Tgb], [Tdbg])
        P.dma("sp", gb_o[:, 2 * D:3 * D], fgb, "dbg", [Tgb], [Tdbg])
    P.barrier()
    A.release(m0)
    if stop_after == "p0":
        return finish(P, nc, [Tdbg])

    mA = A.mark()
    kT = A.alloc([4, NTOK], BF16)
    vtok = A.alloc([34, 512], BF16)
    TkT = [T(f"kT{g}") for g in range(9)]
    Tv = [T(f"v{g}") for g in range(9)]
    m1 = A.mark()
    wA = A.alloc([8, 3072], BF16)
    TwA = T("wA")
    win_v = win_d.rearrange("(k p) m -> p k m", p=128)
    for c6 in range(6):
        P.dma("pool", wA[:, :, c6 * 512:(c6 + 1) * 512], win_v[:, :, c6 * 512:(c6 + 1) * 512], "ld_wA", [], [TwA])
    xt = [A.alloc([D], F32) for _ in range(3)]
    Txt = [T(f"xt{i}") for i in range(3)]
    junk = A.alloc([D], BF16)
    Tjunk = T("junk")
    xn = [A.alloc([D], BF16) for _ in range(2)]
    Txn = [T(f"xn{i}") for i in range(2)]
    st = [A.alloc([4], F32) for _ in range(3)]
    Tst = [T(f"st{i}") for i in range(3)]
    nT = [A.alloc([8, 512], BF16) for _ in range(2)]
    TnT = [T("nT0"), T("nT1")]
    ropec = [A.alloc([512], F32) for _ in range(2)]
    ropes = [A.alloc([512], F32) for _ in range(2)]
    Trope = [T("rope0"), T("rope1")]
    rt = [A.alloc([512], F32) for _ in range(4)]
    Trt = [T(f"rt{i}") for i in range(4)]
    qst = [A.alloc([4, 512], BF16) for _ in range(2)]
    Tqst = [T("qst0"), T("qst1")]
    ust = [A.alloc([4, 512], BF16) for _ in range(2)]
    Tust = [T("ust0"), T("ust1")]
    TqS, TuS = T("qT_s"), T("uT_s")

    groups = [("ctx", 0, LC)] + [("lat", g * 512, 512) for g in range(8)]
    tile_no = 0
    mmbank = 0
    for gi, (kind, t0, ntok) in enumerate(groups):
        nb = gi % 2
        ntile = ntok // 128
        src = ctx_d if kind == "ctx" else x_d
        Asc, Bsc = (A1c, B1c) if kind == "ctx" else (A1l, B1l)
        if kind == "lat":
            P.dma("act", ropec[nb], cos_d[:, t0:t0 + 512], f"ld_rope{nb}", [], [Trope[nb]])
            P.dma("act", ropes[nb], sin_d[:, t0:t0 + 512], f"ld_rope{nb}", [], [Trope[nb]])
        for half in range((ntile + 1) // 2):
            tb_ = 0 if (gi * 2 + half) % 2 == 0 else 2
            pT = ps[:, tb_ * 512:(tb_ + 2) * 512].bitcast(BF16).rearrange("p (c t) -> p c t", c=8)
            Tp = Tb[tb_]
            for tl in range(2):
                ti = half * 2 + tl
                xi = tile_no % 3
                si = tile_no % 3
                ni = tile_no % 2
                tile_no += 1
                P.dma("sp", xt[xi], src[t0 + ti * 128:t0 + (ti + 1) * 128, :], f"ld_x{xi}", [], [Txt[xi]])
                P.act(junk, xt[xi], AF.Square, [Txt[xi]], [Tjunk, Tst[si]], accum=st[si][:, 0:1])
                rstd_ops(P, st[si], Tst[si])
                P.act(xn[ni], xt[xi], AF.Copy, [Txt[xi], Tst[si]], [Txn[ni]], scale=st[si][:, 2:3])
                for c in range(8):
                    P.tr(pT[:, c, tl * 128:(tl + 1) * 128], xn[ni][:, c * 128:(c + 1) * 128], ident_b,
                         [Txn[ni], Tconst], [Tp])
            for c in range(8):
                P.ts("dve", nT[nb][:, c, half * 256:(half + 1) * 256], pT[:, c, :], Asc[:, c:c + 1], Bsc[:, c:c + 1],
                     ALU.mult, ALU.add, [Tp, Tmod], [TnT[nb]])
        cols = slice(0, ntok)

        def proj(mcol, bk):
            for k in range(8):
                P.mm(bank(bk, ntok), wA[:, k, mcol:mcol + 128], nT[nb][:, k, cols], k == 0, k == 7,
                     [TwA, TnT[nb]], [Tb[bk]])

        kcol0 = L if kind == "ctx" else t0
        for j in range(4):
            if kind == "ctx":
                bk = 4 + mmbank % 4
                mmbank += 1
                proj(512 + j * 128, bk)
                P.copy("act", kT[:, j, kcol0:kcol0 + ntok], bank(bk, ntok), [Tb[bk]], [TkT[gi]])
            else:
                for (base, swb, dst, Td, isq) in ((0, 2048, qst[nb], Tqst[nb], True), (512, 2560, None, TkT[gi], False)):
                    bk = 4 + mmbank % 4
                    bk2 = 4 + (mmbank + 1) % 4
                    mmbank += 2
                    proj(base + j * 128, bk)
                    proj(swb + j * 128, bk2)
                    rsel = (mmbank // 2) % 2 * 2
                    r0, r1 = rt[rsel], rt[rsel + 1]
                    Tr0, Tr1 = Trt[rsel], Trt[rsel + 1]
                    P.tt("dve", r0, bank(bk), ropec[nb], ALU.mult, [Tb[bk], Trope[nb]], [Tr0])
                    P.tt("dve", r1, bank(bk2), ropes[nb], ALU.mult, [Tb[bk2], Trope[nb]], [Tr1])
                    o = qst[nb][:, j, :] if isq else kT[:, j, t0:t0 + 512]
                    P.tt("pool", o, r0, r1, ALU.add, [Tr0, Tr1], [Td])
            bk = 4 + mmbank % 4
            mmbank += 1
            proj(1536 + j * 128, bk)
            P.copy("act", ust[nb][:, j, cols], bank(bk, ntok), [Tb[bk]], [Tust[nb]])
        for ti in range(ntile):
            bk = 4 + mmbank % 4
            mmbank += 1
            for k in range(8):
                P.mm(bank(bk), nT[nb][:, k, ti * 128:(ti + 1) * 128], wA[:, k, 1024:1536], k == 0, k == 7,
                     [TwA, TnT[nb]], [Tb[bk]])
            vt = (32 + ti) if kind == "ctx" else (t0 // 128 + ti)
            P.copy("act", vtok[:, vt, :], bank(bk), [Tb[bk]], [Tv[gi]])
        if kind == "lat":
            P.dma("sp", qT_s[:, :, t0:t0 + 512].rearrange("c p t -> p c t"), qst[nb], f"st_q{nb}", [Tqst[nb]], [TqS])
            P.dma("sp", uT_s[:, :, LC + t0:LC + t0 + 512].rearrange("c p t -> p c t"), ust[nb], f"st_u{nb}",
                  [Tust[nb]], [TuS])
        else:
            P.dma("sp", uT_s[:, :, 0:LC].rearrange("c p t -> p c t"), ust[nb][:, :, 0:LC], f"st_u{nb}", [Tust[nb]], [TuS])
            P.dma("sp", uT_s[:, :, LC + L:UEXT].rearrange("c p t -> p c t"), ust[nb][:, :, 0:LC], f"st_u{nb}",
                  [Tust[nb]], [TuS])
    if debug:
        P.dma("sp", kT_o.rearrange("c p t -> p c t"), kT, "dbg", TkT, [Tdbg])
        P.dma("sp", v_o.rearrange("n p f -> p n f"), vtok, "dbg", Tv, [Tdbg])
    P.barrier()
    A.release(m1)
    if stop_after == "pA":
        return finish(P, nc, [TqS, TuS, Tdbg])

    PI = math.pi
    s_ar = A.alloc([32], F32)
    s_ai = A.alloc([32], F32)
    s_nai = A.alloc([32], F32)
    a16r = A.alloc([32], F32)
    a16i = A.alloc([32], F32)
    Wt = A.alloc([2, 4, 2, 128], BF16)
    CwT = A.alloc([32, 2, 32], BF16)
    Eb = A.alloc([2, NCH, 32], F32)
    hst = [[A.alloc([32], F32) for _ in range(2)] for _ in range(2)]
    ct = [A.alloc([32], F32) for _ in range(4)]
    Tprm, TWt, TCw, TE = T("s5prm"), T("Wt"), T("CwT"), T("E")
    TyT = [[T(f"yT{d}{c}") for c in range(4)] for d in range(2)]
    mS1 = A.mark()
    L32 = A.alloc([3, 128], F32)
    ldt = A.alloc([2], F32)
    prm = A.alloc([3, 32], F32)
    w32 = [A.alloc([32], F32) for _ in range(12)]
    negpi = A.alloc([1], F32)
    qint = A.alloc([32], I32)
    breT = A.alloc([32, 16], F32)
    bimT = A.alloc([32, 16], F32)
    bbr = A.alloc([32, 16], F32)
    bbi = A.alloc([32, 16], F32)
    btmp = A.alloc([32, 16], F32)
    srcw = [A.alloc([128], F32) for _ in range(8)]
    Cblk1 = A.alloc([32, 128], F32)
    Cblk = [Cblk1, Cblk1]
    TCb1 = T("Cb")
    TL32, Tbb, Tsrc, TCb = T("L32"), T("bb"), [T(f"srcw{i}") for i in range(8)], [TCb1, TCb1]
    P.dma("sp", L32[0:32, 0, :], lre_d, "ld_s5", [], [TL32])
    P.dma("sp", L32[0:32, 1, :], lim_d, "ld_s5", [], [TL32])
    P.dma("sp", ldt[0:32, :], ldt_d, "ld_s5", [], [TL32])
    P.dma("sp", breT, bre_d.rearrange("d (j g) p c -> (g p) (d j) c", g=2), "ld_s5b", [], [Tbb])
    P.dma("sp", bimT, bim_d.rearrange("d (j g) p c -> (g p) (d j) c", g=2), "ld_s5b", [], [Tbb])
    def load_cblk(ri):
        cd = (cre_d, cim_d)[ri]
        if ri == 0:
            P.memset("pool", Cblk1, 0.0, [TCb1])
        cv_ = cd.rearrange("d (j g) h p -> g h (d j) p", g=2)
        for g2 in range(2):
            P.dma("sp", Cblk1[g2 * 16:(g2 + 1) * 16, :, g2 * 64:(g2 + 1) * 64], cv_[g2], "ld_s5c", [], [TCb1])

    load_cblk(0)
    P.memset("pool", negpi, -PI, [Tprm])
    for i in range(8):
        P.memset("pool", srcw[i], 0.0, [Tsrc[i]])
    P.copy("dve", L32[0:32, 2, :].rearrange("p (g q) -> p g q", g=2), ldt[0:32, :].unsqueeze(2).to_broadcast([32, 2, 64]),
           [TL32], [TL32])
    for i in range(3):
        P.mm(bank(0, 32, i * 32), L32[0:32, i, :], ident_f[0:32, 0:32], True, True, [TL32, Tconst], [Tb[0]])
    P.copy("dve", prm.rearrange("p a b -> p (a b)"), bank(0, 96), [Tb[0]], [Tprm])
    lre, lim, ldtT = prm[:, 0, :], prm[:, 1, :], prm[:, 2, :]
    dt_, tmp_, mag, ang, rs_, sin_, cos_, den, rden, nr, fr, fi = w32
    RW = ([Tprm], [Tprm])
    P.act(dt_, ldtT, AF.Exp, *RW)
    P.tt("dve", tmp_, lre, dt_, ALU.mult, *RW)
    P.act(mag, tmp_, AF.Exp, *RW)
    P.tt("dve", ang, lim, dt_, ALU.mult, *RW)
    def sin_of(dst, shift):
        P.ts("dve", rs_, ang, shift, 1.0 / (2 * PI), ALU.add, ALU.mult, *RW)
        P.copy("dve", qint, rs_, *RW)
        P.copy("dve", rden, qint, *RW)
        P.ts("dve", rs_, ang, shift, None, ALU.add, None, *RW)
        P.stt("dve", rs_, rden, -2 * PI, rs_, ALU.mult, ALU.add, *RW)
        P.ts("dve", rden, rs_, PI, None, ALU.is_gt, None, *RW)
        P.stt("dve", rs_, rden, -2 * PI, rs_, ALU.mult, ALU.add, *RW)
        P.ts("dve", rden, rs_, -PI, None, ALU.is_lt, None, *RW)
        P.stt("dve", rs_, rden, 2 * PI, rs_, ALU.mult, ALU.add, *RW)
        P.act(dst, rs_, AF.Sin, *RW)

    sin_of(sin_, 0.0)
    sin_of(cos_, 0.5 * PI)
    P.tt("dve", s_ar, mag, cos_, ALU.mult, *RW)
    P.tt("dve", s_ai, mag, sin_, ALU.mult, *RW)
    P.ts("dve", s_nai, s_ai, -1.0, None, ALU.mult, None, *RW)
    P.tt("dve", den, lre, lre, ALU.mult, *RW)
    P.tt("dve", tmp_, lim, lim, ALU.mult, *RW)
    P.tt("dve", den, den, tmp_, ALU.add, *RW)
    P.emit("dve", lambda e: e.reciprocal(out=rden, in_=den), *RW)
    P.ts("dve", nr, s_ar, -1.0, None, ALU.add, None, *RW)
    P.tt("dve", fr, nr, lre, ALU.mult, *RW)
    P.tt("dve", tmp_, s_ai, lim, ALU.mult, *RW)
    P.tt("dve", fr, fr, tmp_, ALU.add, *RW)
    P.tt("dve", fr, fr, rden, ALU.mult, *RW)
    P.tt("dve", fi, s_ai, lre, ALU.mult, *RW)
    P.tt("dve", tmp_, nr, lim, ALU.mult, *RW)
    P.tt("dve", fi, fi, tmp_, ALU.subtract, *RW)
    P.tt("dve", fi, fi, rden, ALU.mult, *RW)
    P.copy("dve", a16r, s_ar, *RW)
    P.copy("dve", a16i, s_ai, *RW)
    for _ in range(4):
        P.tt("dve", tmp_, a16r, a16r, ALU.mult, *RW)
        P.tt("dve", den, a16i, a16i, ALU.mult, *RW)
        P.tt("dve", nr, a16r, a16i, ALU.mult, *RW)
        P.tt("dve", a16r, tmp_, den, ALU.subtract, *RW)
        P.ts("dve", a16i, nr, 2.0, None, ALU.mult, None, *RW)
    frb = fr.unsqueeze(2).to_broadcast([128, 32, 16])
    fib = fi.unsqueeze(2).to_broadcast([128, 32, 16])
    Rb = [Tprm, Tbb]
    P.tt("dve", bbr, breT, frb, ALU.mult, Rb, [Tbb])
    P.tt("dve", btmp, bimT, fib, ALU.mult, Rb, [Tbb])
    P.tt("dve", bbr, bbr, btmp, ALU.subtract, Rb, [Tbb])
    P.tt("dve", bbi, bimT, frb, ALU.mult, Rb, [Tbb])
    P.tt("dve", btmp, breT, fib, ALU.mult, Rb, [Tbb])
    P.tt("dve", bbi, bbi, btmp, ALU.add, Rb, [Tbb])
    n = 0
    for d in range(2):
        for j in range(16):
            dj = d * 16 + j
            cj, jm = j // 4, j % 4
            for ri, bb_ in enumerate((bbr, bbi)):
                sw = srcw[ri * 4 + jm]
                Ts_ = Tsrc[ri * 4 + jm]
                for g2 in range(2):
                    P.copy("dve", sw[g2 * 64:(g2 + 1) * 64, 32 * jm + 16 * g2:32 * jm + 16 * g2 + 16],
                           bb_[g2 * 64:(g2 + 1) * 64, dj, :], [Tbb], [Ts_])
                bk = 1 + n % 2
                n += 1
                P.mm(bank(bk, 128), sw, ident_f, True, True, [Ts_, Tconst], [Tb[bk]])
                P.copy("act", Wt[32 * jm:32 * jm + 32, d, cj, ri, :], bank(bk, 128)[32 * jm:32 * jm + 32, :], [Tb[bk]], [TWt])
    for ri in range(2):
        if ri == 1:
            load_cblk(1)
        for dj in range(32):
            bk = 1 + n % 2
            n += 1
            P.mm(bank(bk, 32), Cblk1[0:32, dj, :], ident_f[0:32, 0:32], True, True, [TCb1, Tconst], [Tb[bk]])
            if ri == 0:
                P.copy("act", CwT[:, dj, 0, :], bank(bk, 32), [Tb[bk]], [TCw])
            else:
                P.act(CwT[:, dj, 1, :], bank(bk, 32), AF.Copy, [Tb[bk]], [TCw], scale=-1.0)
    P.barrier()
    A.release(mS1)

    mP = A.mark()

    def useq(u, jm, d, i):
        if d == 0:
            return u[32 * jm:32 * jm + 32, i:NTOK:TCH]
        return u[32 * jm:32 * jm + 32, UEXT - 1 - i:LC - 1:-TCH]

    def scan_pass(phase2, yT=None):
        uch = [A.alloc([UEXT], BF16) for _ in range(2)]
        Tuch = [T("uch0"), T("uch1")]
        Hs = [[A.alloc([NCH], F32) for _ in range(2)] for _ in range(4)]
        THs = [T(f"H{i}") for i in range(4)]
        tm = [[A.alloc([NCH], F32) for _ in range(2)] for _ in range(2)]
        Ttm = [T("tm0"), T("tm1")]
        Hb = [[[A.alloc([NCH], BF16) for _ in range(2)] for _ in range(4)] for _ in range(2)]
        THb = [[T(f"Hb{b}{i}") for i in range(4)] for b in range(2)]
        steps = [(cj, d, i, jm) for cj in range(4) for d in range(2) for i in range(TCH) for jm in range(4)]
        N = len(steps)

        def load_u(cj):
            ub = cj % 2
            P.dma("sp", uch[ub], uT_s[cj], f"ld_u{ub}_{int(phase2)}", [TuS], [Tuch[ub]])

        def emitBU(n):
            cj, d, i, jm = steps[n]
            ub = cj % 2
            pb = n % 2
            rhs = useq(uch[ub], jm, d, i)
            for ri in range(2):
                P.mm(bank(pb * 2 + ri, NCH), Wt[32 * jm:32 * jm + 32, d, cj, ri, :], rhs, True, True, [TWt, Tuch[ub]],
                     [Tb[pb * 2]], tp=(32 * jm, 0))

        load_u(0)
        emitBU(0)
        for n in range(N):
            cj, d, i, jm = steps[n]
            if d == 0 and i == 0 and jm == 0 and cj + 1 < 4:
                load_u(cj + 1)
            if n + 1 < N:
                emitBU(n + 1)
            stepno = n // 4
            ybk = 4 + stepno % 2
            dj = d * 16 + cj * 4 + jm
            pb = n % 2
            BUr, BUi = bank(pb * 2, NCH), bank(pb * 2 + 1, NCH)
            TB = Tb[pb * 2]
            Hr, Hi = Hs[jm]
            TH = THs[jm]
            Er, Ei = Eb[:, 0, :, dj], Eb[:, 1, :, dj]
            if i == 0 and not phase2:
                P.copy("dve", Hr, BUr, [TB], [TH])
                P.copy("dve", Hi, BUi, [TB], [TH])
                continue
            if i == 0:
                inr, ini, Rin = Er, Ei, [TE]
            else:
                inr, ini, Rin = Hr, Hi, [TH]
            last1 = (i == TCH - 1) and not phase2
            outr, outi, Wout = (Er, Ei, [TE]) if last1 else (Hr, Hi, [TH])
            t1, t2 = tm[jm % 2]
            Tt = Ttm[jm % 2]
            P.stt("dve", t1, ini, s_nai[:, dj:dj + 1], BUr, ALU.mult, ALU.add, Rin + [TB, Tprm], [Tt])
            P.stt("dve", t2, inr, s_ai[:, dj:dj + 1], BUi, ALU.mult, ALU.add, Rin + [TB, Tprm], [Tt])
            P.stt("dve", outr, inr, s_ar[:, dj:dj + 1], t1, ALU.mult, ALU.add, Rin + [Tt, Tprm], Wout)
            P.stt("dve", outi, ini, s_ar[:, dj:dj + 1], t2, ALU.mult, ALU.add, Rin + [Tt, Tprm], Wout)
            if phase2:
                hb = stepno % 2
                Hbr, Hbi = Hb[hb][jm]
                P.copy("act", Hbr, Hr, [TH], [THb[hb][jm]])
                P.copy("act", Hbi, Hi, [TH], [THb[hb][jm]])
                yps = bank(ybk, NCH)[32 * jm:32 * jm + 32, :]
                P.mm(yps, CwT[:, dj, 0, :], Hbr, True, False, [TCw, THb[hb][jm]], [Tb[ybk]], tp=(0, 32 * jm))
                P.mm(yps, CwT[:, dj, 1, :], Hbi, False, True, [TCw, THb[hb][jm]], [Tb[ybk]], tp=(0, 32 * jm))
                if jm == 3:
                    if d == 0:
                        dst = yT[:, 0, cj, i:L:TCH]
                    else:
                        dst = yT[:, 1, cj, L - 1 - i:None:-TCH]
                    P.copy("act", dst, bank(ybk, NCH)[:, LC // TCH:NCH], [Tb[ybk]], [TyT[d][cj]])

    scan_pass(False)
    P.barrier()
    A.release(mP)
    if stop_after == "pS1":
        return finish(P, nc, [TqS, TuS, Tdbg])
    Th_ = [T("ch0"), T("ch1")]
    Tc_ = [T(f"cc{i}") for i in range(4)]
    P.memset("pool", hst[0][0], 0.0, [Th_[0]])
    P.memset("pool", hst[0][1], 0.0, [Th_[0]])
    for s_ in range(NCH):
        hr, hi = hst[s_ % 2]
        nhr, nhi = hst[(s_ + 1) % 2]
        Tho, Thn_ = Th_[s_ % 2], Th_[(s_ + 1) % 2]
        P.tt("pool", ct[0], a16r, hr, ALU.mult, [Tho, Tprm], [Tc_[0]])
        P.tt("pool", ct[1], a16i, hi, ALU.mult, [Tho, Tprm], [Tc_[1]])
        P.tt("pool", ct[2], a16r, hi, ALU.mult, [Tho, Tprm], [Tc_[2]])
        P.tt("pool", ct[3], a16i, hr, ALU.mult, [Tho, Tprm], [Tc_[3]])
        P.tt("pool", ct[0], ct[0], ct[1], ALU.subtract, [Tc_[0], Tc_[1]], [Tc_[0]])
        P.tt("pool", ct[2], ct[2], ct[3], ALU.add, [Tc_[2], Tc_[3]], [Tc_[2]])
        P.tt("pool", nhr, ct[0], Eb[:, 0, s_, :], ALU.add, [Tc_[0], TE], [Thn_])
        P.tt("pool", nhi, ct[2], Eb[:, 1, s_, :], ALU.add, [Tc_[2], TE], [Thn_])
        P.copy("pool", Eb[:, 0, s_, :], hr, [Tho], [TE])
        P.copy("pool", Eb[:, 1, s_, :], hi, [Tho], [TE])
    if stop_after == "pCH":
        return finish(P, nc, [TqS, TuS, Tdbg])
    mB = A.mark()
    tbias = A.alloc([4, 15, 64], F32)
    mk = A.alloc([64], F32)
    Ttb = T("tbias")
    P.dma("act", tbias, tb_d, "ld_tb", [], [Ttb])
    P.dma("act", mk, mk_d, "ld_tb", [], [Ttb])
    for j in range(4):
        P.tt("dve", tbias[:, j], tbias[:, j], mk.unsqueeze(1).to_broadcast([128, 15, 64]), ALU.add, [Ttb], [Ttb])
    NQB = 3
    qbd = [A.alloc([4, 128], BF16) for _ in range(NQB)]
    Tqbd = [T(f"qbd{i}") for i in range(NQB)]
    for i in range(NQB):
        P.memset("dve", qbd[i], 0.0, [Tqbd[i]])
    sl = [A.alloc([512], F32) for _ in range(2)]
    Tsl = [T("sl0"), T("sl1")]
    Pt = [A.alloc([896], BF16) for _ in range(2)]
    TPt = [T("Pt0"), T("Pt1")]
    for i in range(2):
        P.memset("dve", Pt[i], 0.0, [TPt[i]])
    PTs = [A.alloc([896], BF16) for _ in range(2)]
    TPTs = [T("PTs0"), T("PTs1")]
    On = [A.alloc([128], BF16) for _ in range(2)]
    TOn = [T("On0"), T("On1")]
    sm = [A.alloc([8], F32) for _ in range(2)]
    Tsm = [T("sm0"), T("sm1")]
    ast = [A.alloc([4, 512], BF16) for _ in range(2)]
    Tast = [T("ast0"), T("ast1")]
    TaS = T("aT_s")
    TSl = [Tb[0], Tb[1]]
    TSc = [T("Sc0"), T("Sc1")]
    TPT = [Tb[3], Tb[4]]
    TO = [T("O0"), T("O1")]
    TOT = [T("OT0"), T("OT1")]
    it = 0
    for r in range(64):
        rs = min(max(r - 4, 0), 56)
        d0 = rs - r + 7
        qi = r % NQB
        c0 = r * 64
        for hh in range(2):
            P.dma("sp", qbd[qi][hh * 64:(hh + 1) * 64, :, hh * 64:(hh + 1) * 64],
                  qT_s[:, hh * 64:(hh + 1) * 64, c0:c0 + 64].rearrange("c p t -> p c t"), f"ld_q{qi}", [TqS], [Tqbd[qi]])
        g0 = 1 + (rs * 64) // 512
        g1 = 1 + (rs * 64 + 511) // 512
        Rk = [TkT[0], TkT[g0]] + ([TkT[g1]] if g1 != g0 else [])
        Rv = [Tv[0], Tv[g0]] + ([Tv[g1]] if g1 != g0 else [])
        odd = rs % 2 == 1
        gsel = (r // 8) % 2
        for j in range(4):
            b2 = it % 2
            it += 1
            S_l = bank(b2)
            S_c = bank(2, 256, b2 * 256)
            P.mm(S_l, qbd[qi][:, j, :], kT[:, j, rs * 64:rs * 64 + 512], True, True, [Tqbd[qi]] + Rk, [TSl[b2]])
            P.mm(S_c, qbd[qi][:, j, :], kT[:, j, L:L + LC], True, True, [Tqbd[qi]] + Rk, [TSc[b2]])
            smt = sm[b2]
            P.stt("dve", sl[b2], S_l, 0.125, tbias[:, j, d0:d0 + 8, :].rearrange("p a b -> p (a b)"), ALU.mult, ALU.add,
                  [TSl[b2], Ttb], [Tsl[b2]])
            P.emit("dve", lambda e, o=smt[:, 0:1], i=sl[b2]: e.reduce_max(out=o, in_=i, axis=AX.X), [Tsl[b2]], [Tsm[b2]])
            P.emit("dve", lambda e, o=smt[:, 1:2], i=S_c: e.reduce_max(out=o, in_=i, axis=AX.X), [TSc[b2]], [Tsm[b2]])
            P.stt("dve", smt[:, 2:3], smt[:, 1:2], 0.125, smt[:, 0:1], ALU.mult, ALU.max, [Tsm[b2]], [Tsm[b2]])
            P.ts("dve", smt[:, 3:4], smt[:, 2:3], -1.0, None, ALU.mult, None, [Tsm[b2]], [Tsm[b2]])
            P.act(Pt[b2][:, 64:576], sl[b2], AF.Exp, [Tsl[b2], Tsm[b2]], [TPt[b2], Tsm[b2]], bias=smt[:, 3:4],
                  accum=smt[:, 4:5])
            P.act(Pt[b2][:, 640:896], S_c, AF.Exp, [TSc[b2], Tsm[b2]], [TPt[b2], Tsm[b2]], bias=smt[:, 3:4], scale=0.125,
                  accum=smt[:, 5:6])
            P.tt("dve", smt[:, 6:7], smt[:, 4:5], smt[:, 5:6], ALU.add, [Tsm[b2]], [Tsm[b2]])
            P.emit("dve", lambda e, o=smt[:, 7:8], i=smt[:, 6:7]: e.reciprocal(out=o, in_=i), [Tsm[b2]], [Tsm[b2]])
            PTp = bank_bf(3 + b2)
            if not odd:
                blocks = [(64 + 128 * b, 0, 128, rs // 2 + b) for b in range(4)]
            else:
                vb = (rs - 1) // 2
                blocks = [(0, 64, 128, vb)] + [(128 * b, 0, 128, vb + b) for b in (1, 2, 3)] + [(512, 0, 64, vb + 4)]
            blocks += [(640, 0, 128, 32), (768, 0, 128, 33)]
            nblk = len(blocks)
            for bi, (col, p0, p1, vt) in enumerate(blocks):
                P.tr(PTp[:, bi * 128:(bi + 1) * 128], Pt[b2][:, col:col + 128], ident_b, [TPt[b2], Tconst], [TPT[b2]])
            P.copy("act", PTs[b2][:, 0:nblk * 128], PTp[:, 0:nblk * 128], [TPT[b2]], [TPTs[b2]])
            O = bank(5, 128, b2 * 128)
            for bi, (col, p0, p1, vt) in enumerate(blocks):
                P.mm(O, PTs[b2][p0:p1, bi * 128:(bi + 1) * 128], vtok[p0:p1, vt, j * 128:(j + 1) * 128], bi == 0,
                     bi == nblk - 1, [TPTs[b2]] + Rv, [TO[b2]])
            P.ts("dve", On[b2], O, smt[:, 7:8], None, ALU.mult, None, [TO[b2], Tsm[b2]], [TOn[b2]])
            OT = bank_bf(6)[:, b2 * 128:(b2 + 1) * 128]
            P.tr(OT, On[b2], ident_b, [TOn[b2], Tconst], [TOT[b2]])
            cc = (r % 8) * 64
            P.copy("act", ast[gsel][0:64, j, cc:cc + 64], OT[0:64, 0:64], [TOT[b2]], [Tast[gsel]])
            P.copy("act", ast[gsel][64:128, j, cc:cc + 64], OT[64:128, 64:128], [TOT[b2]], [Tast[gsel]])
        if r % 8 == 7:
            g = r // 8
            P.dma("sp", aT_s[:, :, g * 512:(g + 1) * 512].rearrange("c p t -> p c t"), ast[gsel], f"st_a{gsel}",
                  [Tast[gsel]], [TaS])
    P.barrier()
    A.release(mB)
    if stop_after == "pB":
        return finish(P, nc, [TqS, TuS, TaS, Tdbg])

    cur_ = A.off
    A.off = mA
    yT = A.alloc([2, 4, L], BF16)
    assert A.off <= m1
    A.off = cur_
    scan_pass(True, yT)
    TyS = T("y_s")
    for d in range(2):
        P.dma("sp", y_o[d].rearrange("c p t -> p c t"), yT[:, d], "st_y", TyT[d], [TyS])
    P.barrier()
    A.release(mA)
    if stop_after == "pS":
        return finish(P, nc, [TqS, TuS, TaS, TyS, Tdbg])

    wG = A.alloc([8, 2048], BF16)
    wBA = A.alloc([4, 1024], BF16)
    wBS = A.alloc([4, 1024], BF16)
    wGLU = A.alloc([4, 1024], BF16)
    wO = A.alloc([8, 1024], BF16)
    dsk = A.alloc([4], F32)
    TwC = T("wC")
    wg_v = wgate_d.rearrange("(k p) m -> p k m", p=128)
    for c4 in range(4):
        P.dma("pool", wG[:, :, c4 * 512:(c4 + 1) * 512], wg_v[:, :, c4 * 512:(c4 + 1) * 512], "ld_wC", [], [TwC])
    P.dma("pool", wGLU, wglu_d.rearrange("(k p) m -> p k m", p=128), "ld_wC", [], [TwC])
    P.dma("pool", wBA, wba_d.rearrange("(k p) m -> p k m", p=128), "ld_wC", [], [TwC])
    P.dma("pool", wBS, wbs_d.rearrange("(k p) m -> p k m", p=128), "ld_wC", [], [TwC])
    wo_v = wout_d.rearrange("(k p) m -> p k m", p=128)
    for c2 in range(2):
        P.dma("pool", wO[:, :, c2 * 512:(c2 + 1) * 512], wo_v[:, :, c2 * 512:(c2 + 1) * 512], "ld_wC", [], [TwC])
    P.dma("act", dsk, dsk_d, "ld_dsk", [], [TwC])
    NX = 6
    xt = [A.alloc([D], F32) for _ in range(NX)]
    Txt = [T(f"cxt{i}") for i in range(NX)]
    junk = A.alloc([D], BF16)
    Tjunk = T("cjunk")
    xn = [A.alloc([D], BF16) for _ in range(2)]
    Txn = [T(f"cxn{i}") for i in range(2)]
    st = [A.alloc([4], F32) for _ in range(NX)]
    Tst = [T(f"cst{i}") for i in range(NX)]
    nT = [A.alloc([8, 512], BF16) for _ in range(2)]
    TnT = [T("cnT0"), T("cnT1")]
    aTg = [A.alloc([4, 512], BF16) for _ in range(2)]
    uTg = [A.alloc([4, 512], BF16) for _ in range(2)]
    yg = [A.alloc([2, 4, 512], BF16) for _ in range(2)]
    Tag = [T("aTg0"), T("aTg1")]
    Tug = [T("uTg0"), T("uTg1")]
    Tyg = [T("yg0"), T("yg1")]
    zT = A.alloc([4, 512], BF16)
    sT = A.alloc([4, 512], BF16)
    mT = A.alloc([8, 512], BF16)
    TzT, TsT, TmT = T("zT"), T("sT"), T("mT")
    ft = [A.alloc([512], F32) for _ in range(10)]
    Tft = [T(f"ft{i}") for i in range(10)]
    ThS = T("h_s")
    bcyc = [0]

    def nb_():
        b = 2 + bcyc[0] % 6
        bcyc[0] += 1
        return b

    xcnt = 0
    for g in range(8):
        nb = g % 2
        t0 = g * 512
        P.dma("act", aTg[nb], aT_s[:, :, t0:t0 + 512].rearrange("c p t -> p c t"), f"ld_ag{nb}", [TaS], [Tag[nb]])
        P.dma("act", uTg[nb], uT_s[:, :, LC + t0:LC + t0 + 512].rearrange("c p t -> p c t"), f"ld_ug{nb}", [TuS], [Tug[nb]])
        for d in range(2):
            P.dma("act", yg[nb][:, d], y_o[d, :, :, t0:t0 + 512].rearrange("c p t -> p c t"), f"ld_yg{nb}", [TyS], [Tyg[nb]])
        xis = []
        for half in range(2):
            pT = ps[:, 0:1024].bitcast(BF16).rearrange("p (c t) -> p c t", c=8)
            for tl in range(2):
                ti = half * 2 + tl
                xi = xcnt % NX
                ni = xcnt % 2
                xcnt += 1
                xis.append(xi)
                P.dma("sp", xt[xi], x_d[t0 + ti * 128:t0 + (ti + 1) * 128, :], f"ld_cx{xi}", [], [Txt[xi]])
                P.act(junk, xt[xi], AF.Square, [Txt[xi]], [Tjunk, Tst[xi]], accum=st[xi][:, 0:1])
                rstd_ops(P, st[xi], Tst[xi])
                P.act(xn[ni], xt[xi], AF.Copy, [Txt[xi], Tst[xi]], [Txn[ni]], scale=st[xi][:, 2:3])
                for c in range(8):
                    P.tr(pT[:, c, tl * 128:(tl + 1) * 128], xn[ni][:, c * 128:(c + 1) * 128], ident_b, [Txn[ni], Tconst], [Tb[0]])
            for c in range(8):
                P.ts("dve", nT[nb][:, c, half * 256:(half + 1) * 256], pT[:, c, :], A1l[:, c:c + 1], B1l[:, c:c + 1],
                     ALU.mult, ALU.add, [Tb[0], Tmod], [TnT[nb]])
        for c in range(4):
            f0, f1 = ft[0], ft[1]
            P.tt("dve", f0, yg[nb][:, 0, c, :], yg[nb][:, 1, c, :], ALU.add, [Tyg[nb]], [Tft[0]])
            P.stt("dve", f0, uTg[nb][:, c, :], dsk[:, c:c + 1], f0, ALU.mult, ALU.add, [Tug[nb], TwC, Tft[0]], [Tft[0]])
            P.tt("dve", f1, f0, f0, ALU.mult, [Tft[0]], [Tft[1]])
            P.ts("dve", f1, f1, 0.044715, 1.0, ALU.mult, ALU.add, [Tft[1]], [Tft[1]])
            P.tt("dve", f1, f1, f0, ALU.mult, [Tft[0], Tft[1]], [Tft[1]])
            P.act(f1, f1, AF.Sigmoid, [Tft[1]], [Tft[1]], scale=1.5957691216057308)
            P.tt("dve", zT[:, c, :], f0, f1, ALU.mult, [Tft[0], Tft[1]], [TzT])
        for m in range(4):
            bv, bg = nb_(), nb_()
            for k in range(4):
                P.mm(bank(bv), wGLU[:, k, m * 128:(m + 1) * 128], zT[:, k, :], k == 0, k == 3, [TwC, TzT], [Tb[bv]])
            for k in range(4):
                P.mm(bank(bg), wGLU[:, k, 512 + m * 128:512 + (m + 1) * 128], zT[:, k, :], k == 0, k == 3, [TwC, TzT], [Tb[bg]])
            P.act(ft[2], bank(bg), AF.Sigmoid, [Tb[bg]], [Tft[2]])
            P.tt("dve", sT[:, m, :], bank(bv), ft[2], ALU.mult, [Tb[bv], Tft[2]], [TsT])
        un = 0
        for m in range(8):
            prods = []
            for which in range(2):
                bg, bb = nb_(), nb_()
                gc = which * 1024 + m * 128
                for k in range(8):
                    P.mm(bank(bg), wG[:, k, gc:gc + 128], nT[nb][:, k, :], k == 0, k == 7, [TwC, TnT[nb]], [Tb[bg]])
                if which == 0:
                    for k in range(4):
                        P.mm(bank(bb), wBA[:, k, m * 128:(m + 1) * 128], aTg[nb][:, k, :], k == 0, k == 3, [TwC, Tag[nb]], [Tb[bb]])
                else:
                    for k in range(4):
                        P.mm(bank(bb), wBS[:, k, m * 128:(m + 1) * 128], sT[:, k, :], k == 0, k == 3, [TwC, TsT], [Tb[bb]])
                si = 3 + un % 2
                pi = 5 + un % 4
                un += 1
                P.act(ft[si], bank(bg), AF.Sigmoid, [Tb[bg]], [Tft[si]])
                P.tt("dve", ft[pi], bank(bb), ft[si], ALU.mult, [Tb[bb], Tft[si]], [Tft[pi]])
                prods.append(pi)
            P.tt("pool", mT[:, m, :], ft[prods[0]], ft[prods[1]], ALU.add, [Tft[prods[0]], Tft[prods[1]]], [TmT])
        for ti in range(4):
            xi = xis[ti]
            for nh in range(2):
                bo = nb_()
                for k in range(8):
                    P.mm(bank(bo), mT[:, k, ti * 128:(ti + 1) * 128], wO[:, k, nh * 512:(nh + 1) * 512], k == 0, k == 7,
                         [TmT, TwC], [Tb[bo]])
                gi_ = 9 if nh else 2
                P.tt("dve", ft[gi_], bank(bo), g1b[:, nh * 512:(nh + 1) * 512], ALU.mult, [Tb[bo], Tgb], [Tft[gi_]])
                P.tt("pool", xt[xi][:, nh * 512:(nh + 1) * 512], xt[xi][:, nh * 512:(nh + 1) * 512], ft[gi_], ALU.add,
                     [Txt[xi], Tft[gi_]], [Txt[xi]])
            P.dma("sp", h_s[t0 + ti * 128:t0 + (ti + 1) * 128, :], xt[xi], f"st_h{xi}", [Txt[xi]], [ThS])
    P.barrier()
    A.release(mA)
    if stop_after == "pC1":
        return finish(P, nc, [ThS, Tdbg])

    NHC = FH // 128
    wFI = A.alloc([8, 2 * FH], BF16)
    wFO = A.alloc([NHC, D], BF16)
    TwF = T("wF")
    wfi_v = wfi_d.rearrange("(k p) m -> p k m", p=128)
    for c11 in range(11):
        P.dma("pool", wFI[:, :, c11 * 512:(c11 + 1) * 512], wfi_v[:, :, c11 * 512:(c11 + 1) * 512], "ld_wF", [], [TwF])
    wfo_v = wfo_d.rearrange("(k p) m -> p k m", p=128)
    for c2 in range(2):
        P.dma("pool", wFO[:, 11 * c2:11 * (c2 + 1), :], wfo_v[:, 11 * c2:11 * (c2 + 1), :],
              "ld_wF", [], [TwF])
    NX = 4
    ht = [A.alloc([D], F32) for _ in range(NX)]
    Tht = [T(f"ht{i}") for i in range(NX)]
    hn = [A.alloc([D], BF16) for _ in range(2)]
    Thn = [T("hn0"), T("hn1")]
    st = [A.alloc([8], F32) for _ in range(NX)]
    Tst = [T(f"fst{i}") for i in range(NX)]
    n2T = A.alloc([8, 512], BF16)
    Tn2 = T("n2T")
    actT = A.alloc([NHC, 512], BF16)
    TaT = T("actT")
    fs = [A.alloc([512], F32) for _ in range(4)]
    Tfs = [T(f"fs{i}") for i in range(4)]
    Tout = T("out")
    hcnt = 0
    for g in range(8):
        t0 = g * 512
        his = []
        for half in range(2):
            pT = ps[:, 0:1024].bitcast(BF16).rearrange("p (c t) -> p c t", c=8)
            for tl in range(2):
                ti = half * 2 + tl
                hi_ = hcnt % NX
                ni = hcnt % 2
                hcnt += 1
                his.append(hi_)
                P.dma("sp", ht[hi_], h_s[t0 + ti * 128:t0 + (ti + 1) * 128, :], f"ld_h{hi_}", [ThS], [Tht[hi_]])
                jk = actT[:, 0:2, :].rearrange("p a b -> p (a b)")
                P.act(jk, ht[hi_], AF.Square, [Tht[hi_]], [TaT, Tst[hi_]], accum=st[hi_][:, 0:1])
                rstd_ops(P, st[hi_], Tst[hi_])
                P.act(hn[ni], ht[hi_], AF.Copy, [Tht[hi_], Tst[hi_]], [Thn[ni]], scale=st[hi_][:, 2:3])
                for c in range(8):
                    P.tr(pT[:, c, tl * 128:(tl + 1) * 128], hn[ni][:, c * 128:(c + 1) * 128], ident_b, [Thn[ni], Tconst], [Tb[0]])
            for c in range(8):
                P.ts("dve", n2T[:, c, half * 256:(half + 1) * 256], pT[:, c, :], A2[:, c:c + 1], B2[:, c:c + 1],
                     ALU.mult, ALU.add, [Tb[0], Tmod], [Tn2])
        for hc in range(NHC):
            ba_, bb_ = nb_(), nb_()
            for k in range(8):
                P.mm(bank(ba_), wFI[:, k, hc * 128:(hc + 1) * 128], n2T[:, k, :], k == 0, k == 7, [TwF, Tn2], [Tb[ba_]])
            for k in range(8):
                P.mm(bank(bb_), wFI[:, k, FH + hc * 128:FH + (hc + 1) * 128], n2T[:, k, :], k == 0, k == 7, [TwF, Tn2], [Tb[bb_]])
            fi_ = hc % 2
            P.act(fs[fi_], bank(ba_), AF.Silu, [Tb[ba_]], [Tfs[fi_]])
            P.tt("dve", actT[:, hc, :], bank(bb_), fs[fi_], ALU.mult, [Tb[bb_], Tfs[fi_]], [TaT])
        for ti in range(4):
            hi_ = his[ti]
            for nh in range(2):
                bo = nb_()
                for hc in range(NHC):
                    P.mm(bank(bo), actT[:, hc, ti * 128:(ti + 1) * 128], wFO[:, hc, nh * 512:(nh + 1) * 512], hc == 0,
                         hc == NHC - 1, [TaT, TwF], [Tb[bo]])
                P.tt("dve", fs[2 + nh], bank(bo), g2b[:, nh * 512:(nh + 1) * 512], ALU.mult, [Tb[bo], Tgb], [Tfs[2 + nh]])
                P.tt("pool", ht[hi_][:, nh * 512:(nh + 1) * 512], ht[hi_][:, nh * 512:(nh + 1) * 512], fs[2 + nh], ALU.add,
                     [Tht[hi_], Tfs[2 + nh]], [Tht[hi_]])
            sti = st[hi_]
            P.act(hn[ti % 2], ht[hi_], AF.Square, [Tht[hi_]], [Thn[ti % 2], Tst[hi_]], accum=sti[:, 4:5])
            P.ts("dve", sti[:, 5:6], sti[:, 4:5], 1.0 / D, EPS, ALU.mult, ALU.add, [Tst[hi_]], [Tst[hi_]])
            P.act(sti[:, 6:7], sti[:, 5:6], AF.Sqrt, [Tst[hi_]], [Tst[hi_]])
            P.emit("dve", lambda e, o=sti[:, 7:8], i=sti[:, 6:7]: e.reciprocal(out=o, in_=i), [Tst[hi_]], [Tst[hi_]])
            P.stt("dve", ht[hi_], ht[hi_], sti[:, 7:8], fgb, ALU.mult, ALU.mult, [Tht[hi_], Tst[hi_], Tgb], [Tht[hi_]])
            P.dma("sp", out_d[t0 + ti * 128:t0 + (ti + 1) * 128, :], ht[hi_], f"st_o{hi_}", [Tht[hi_]], [Tout])
    return finish(P, nc, [Tout, Tdbg])


def rstd_ops(P, st, Tst):
    P.ts("dve", st[:, 1:2], st[:, 0:1], 1.0 / D, EPS, ALU.mult, ALU.add, [Tst], [Tst])
    P.act(st[:, 3:4], st[:, 1:2], AF.Sqrt, [Tst], [Tst])
    P.emit("dve", lambda e: e.reciprocal(out=st[:, 2:3], in_=st[:, 3:4]), [Tst], [Tst])


def finish(P, nc, tiles=()):
    P.barrier()
    nc._marks = P.marks
    P.final_wait("sp", list(tiles))
    P.build()
    return nc


def _pcol(v, nchunk):
    return np.ascontiguousarray(v.reshape(nchunk, 128).T)


def host_shared(inp):
    f32 = np.float32
    sh = {}
    w_in = inp["w_in"][0]
    idx = np.concatenate([np.arange(16, 32), np.arange(0, 16), np.arange(48, 64), np.arange(32, 48)])
    swap = np.concatenate([h * 64 + idx for h in range(NH)])
    sh["w_in_ext"] = np.ascontiguousarray(np.concatenate([w_in[:, :2048], w_in[:, swap], w_in[:, 512 + swap]], axis=1))
    sh["w_gate"] = np.ascontiguousarray(w_in[:, 2048:4096])
    sh["w_mod"] = np.ascontiguousarray(inp["w_mod"][0])
    sh["b_mod"] = _pcol(inp["b_mod"][0], 48)
    sh["gA"] = _pcol(inp["attn_norm_g"][0], 8)
    sh["gF"] = _pcol(inp["ffn_norm_g"][0], 8)
    sh["gFin"] = _pcol(inp["final_norm_g"], 8)
    t = np.arange(L)
    inv = (10000.0 ** (-np.arange(16, dtype=f32) / 16)).astype(f32)
    cos = np.zeros((64, L), f32)
    sin = np.zeros((64, L), f32)
    for half, pos in ((0, t // GW), (1, t % GW)):
        ang = pos.astype(f32)[None, :] * inv[:, None]
        c_, s_ = np.cos(ang).astype(f32), np.sin(ang).astype(f32)
        cos[half * 32:half * 32 + 16] = c_
        cos[half * 32 + 16:half * 32 + 32] = c_
        sin[half * 32:half * 32 + 16] = -s_
        sin[half * 32 + 16:half * 32 + 32] = s_
    sh["rope_cos"] = np.ascontiguousarray(np.concatenate([cos, cos], 0))
    sh["rope_sin"] = np.ascontiguousarray(np.concatenate([sin, sin], 0))
    rpb = inp["rel_pos_bias"][0]
    q = np.arange(64)[:, None]
    k = np.arange(64)[None, :]
    dc = np.clip(k - q + 15, 0, 30)
    tb = np.zeros((128, 4, 15, 64), f32)
    for j in range(4):
        for hh in range(2):
            tb[hh * 64:(hh + 1) * 64, j] = np.transpose(rpb[2 * j + hh][:, dc], (1, 0, 2))
    sh["bias_tb"] = tb
    cs = np.clip(q - 8, 0, 48)
    inw = (k >= cs) & (k < cs + 16)
    mk = np.where(inw, 0.0, -1e30).astype(f32)
    sh["bias_mask"] = np.ascontiguousarray(np.concatenate([mk, mk], 0))
    sh["lam_re"] = np.ascontiguousarray(inp["ssm_lambda_re"][0].reshape(32, 128))
    sh["lam_im"] = np.ascontiguousarray(inp["ssm_lambda_im"][0].reshape(32, 128))
    sh["log_dt"] = np.ascontiguousarray(inp["ssm_log_dt"][0].reshape(32, 2))
    sh["b_re"] = np.ascontiguousarray(inp["ssm_b_re"][0])
    sh["b_im"] = np.ascontiguousarray(inp["ssm_b_im"][0])
    sh["c_re"] = np.ascontiguousarray(inp["ssm_c_re"][0])
    sh["c_im"] = np.ascontiguousarray(inp["ssm_c_im"][0])
    sh["d_skip"] = _pcol(inp["ssm_d"][0], 4)
    sh["w_glu"] = np.ascontiguousarray(inp["w_glu"][0])
    sh["w_ba"] = np.ascontiguousarray(inp["w_branch_attn"][0])
    sh["w_bs"] = np.ascontiguousarray(inp["w_branch_ssm"][0])
    sh["w_out"] = np.ascontiguousarray(inp["w_out"][0])
    sh["w_ffn_in"] = np.ascontiguousarray(inp["w_ffn_in"][0])
    sh["w_ffn_out"] = np.ascontiguousarray(inp["w_ffn_out"][0])
    return sh


def host_core(inp, b):
    cv = np.stack([inp["c"][b], inp["c_ctx"]], axis=1)
    return {
        "x": np.ascontiguousarray(inp["x"][b]),
        "ctx": np.ascontiguousarray(inp["ctx"][b]),
        "cvec": np.ascontiguousarray(cv.reshape(8, 128, 2).transpose(1, 0, 2)),
    }


_NC_CACHE = {}


def kernel(**inputs):
    inp = {k: np.asarray(v) for k, v in inputs.items()}
    if "nc" not in _NC_CACHE:
        _NC_CACHE["nc"] = build_program()
    nc = _NC_CACHE["nc"]
    sh = host_shared(inp)
    in_maps = [{**sh, **host_core(inp, b)} for b in range(8)]
    res = run_bass_kernel_spmd(nc, in_maps, core_ids=list(range(8)))
    return np.stack([np.asarray(r["out"]) for r in res.results], axis=0).astype(np.float32)
```

```python
import math
from contextlib import ExitStack

import numpy as np
import concourse.bass as bass
import concourse.mybir as mybir
from concourse.bass_utils import run_bass_kernel_spmd

F32 = mybir.dt.float32
BF16 = mybir.dt.bfloat16
U8 = mybir.dt.uint8
I32 = mybir.dt.int32
AF = mybir.ActivationFunctionType
ALU = mybir.AluOpType
AX = mybir.AxisListType
DTSZ = {F32: 4, BF16: 2, U8: 1, I32: 4}

L = 4096
D = 1024
LC = 256
NTOK = L + LC
GW = 64
NH = 8
HD = 64
AW = 512
SW = 512
NG = 32
NP = 64
FH = 2816
EPS = 1e-6
TCH = 16
NCH = NTOK // TCH
UEXT = L + 2 * LC

ENGS = ("pe", "act", "dve", "pool", "sp")


class T:
    __slots__ = ("name", "w", "r")

    def __init__(self, name):
        self.name = name
        self.w = None
        self.r = {}


class Prog:
    def __init__(self, nc):
        self.nc = nc
        self.es = ExitStack()
        self.ops = {e: [] for e in ENGS}
        self.cnt = {}
        self.sems = {}
        self.seen = {e: {} for e in ENGS}
        self.marks = []
        for e in ENGS:
            self._sem("E_" + e)

    def _sem(self, key):
        if key not in self.sems:
            self.sems[key] = self.es.enter_context(self.nc.semaphore(key))
            self.cnt[key] = 0
        return self.sems[key]

    def sbuf(self, name, shape, dt):
        return self.es.enter_context(self.nc.sbuf_tensor(name, list(shape), dt))

    def psum(self, name, shape, dt):
        return self.es.enter_context(self.nc.psum_tensor(name, list(shape), dt))

    def emit(self, eng, fn, reads=(), writes=(), dma_sem=None):
        own = "E_" + eng
        deps = {}

        def add(ev, is_raw):
            if ev is None:
                return
            k, v = ev
            if k == own and (eng == "pe" or not is_raw):
                return
            if deps.get(k, 0) < v:
                deps[k] = v

        for t in reads:
            add(t.w, True)
        for t in writes:
            add(t.w, False)
            for k, v in t.r.items():
                add((k, v), False)
        waits = []
        seen = self.seen[eng]
        for k, v in deps.items():
            if seen.get(k, 0) < v:
                seen[k] = v
                waits.append((k, v))
        if dma_sem is not None:
            self._sem(dma_sem)
            self.cnt[dma_sem] += 16
            ev = (dma_sem, self.cnt[dma_sem])
            inc = (dma_sem, 16)
        else:
            self.cnt[own] += 1
            ev = (own, self.cnt[own])
            inc = (own, 1)
        self.ops[eng].append((waits, fn, inc))
        for t in reads:
            if t.r.get(ev[0], 0) < ev[1]:
                t.r[ev[0]] = ev[1]
        for t in writes:
            t.w = ev
            t.r = {}
        return ev

    def final_wait(self, eng, tiles):
        waits = []
        for t in tiles:
            if t.w is not None:
                waits.append(t.w)
            for k, v in t.r.items():
                waits.append((k, v))
        self.ops[eng].append((waits, None, None))

    def barrier(self):
        snap = dict(self.cnt)
        self.marks.append(snap)
        for e in ENGS:
            waits = []
            for k, v in snap.items():
                if v > 0 and k != "E_" + e and self.seen[e].get(k, 0) < v:
                    self.seen[e][k] = v
                    waits.append((k, v))
            if waits:
                self.ops[e].append((waits, None, None))

    def build(self):
        nc = self.nc
        needed = {}
        for e in ENGS:
            for waits, fn, inc in self.ops[e]:
                for k, v in waits:
                    if k.startswith("E_"):
                        needed.setdefault(k, set()).add(v)
        rank = {k: {v: i + 1 for i, v in enumerate(sorted(vs))} for k, vs in needed.items()}
        with nc.Block() as block:
            def run(e, handle):
                seq = 0
                own = "E_" + e
                myrank = rank.get(own, {})
                for waits, fn, inc in self.ops[e]:
                    for k, v in waits:
                        if k.startswith("E_"):
                            handle.wait_ge(self.sems[k], rank[k][v])
                        else:
                            handle.wait_ge(self.sems[k], v)
                    if fn is not None:
                        ins = fn(handle)
                        if inc[0] == own:
                            seq += 1
                            if seq in myrank:
                                ins.then_inc(self.sems[own], 1)
                        else:
                            ins.then_inc(self.sems[inc[0]], inc[1])

            @block.tensor
            def _(h):
                run("pe", h)

            @block.scalar
            def _(h):
                run("act", h)

            @block.vector
            def _(h):
                run("dve", h)

            @block.gpsimd
            def _(h):
                run("pool", h)

            @block.sync
            def _(h):
                run("sp", h)
        self.es.close()

    def mm(self, out, lhsT, rhs, start, stop, R, W, tp=None):
        if tp is None:
            self.emit("pe", lambda e: e.matmul(out, lhsT=lhsT, rhs=rhs, start=start, stop=stop), R, W)
        else:
            self.emit("pe", lambda e: e.matmul(out, lhsT=lhsT, rhs=rhs, start=start, stop=stop, tile_position=tp), R, W)

    def tr(self, out, in_, ident, R, W):
        self.emit("pe", lambda e: e.transpose(out=out, in_=in_, identity=ident), R, W)

    def act(self, out, in_, func, R, W, bias=None, scale=None, accum=None):
        kw = {}
        if bias is not None:
            kw["bias"] = bias
        if scale is not None:
            kw["scale"] = scale
        if accum is not None:
            kw["accum_out"] = accum
        self.emit("act", lambda e: e.activation(out=out, in_=in_, func=func, **kw), R, W)

    def tt(self, eng, out, in0, in1, op, R, W):
        self.emit(eng, lambda e: e.tensor_tensor(out=out, in0=in0, in1=in1, op=op), R, W)

    def ts(self, eng, out, in0, s1, s2, op0, op1, R, W, accum=None):
        if s2 is None:
            self.emit(eng, lambda e: e.tensor_scalar(out=out, in0=in0, scalar1=s1, scalar2=None, op0=op0), R, W)
        elif accum is None:
            self.emit(eng, lambda e: e.tensor_scalar(out=out, in0=in0, scalar1=s1, scalar2=s2, op0=op0, op1=op1), R, W)
        else:
            self.emit(eng, lambda e: e.tensor_scalar(out=out, in0=in0, scalar1=s1, scalar2=s2, op0=op0, op1=op1,
                                                     accum_out=accum), R, W)

    def stt(self, eng, out, in0, scalar, in1, op0, op1, R, W):
        self.emit(eng, lambda e: e.scalar_tensor_tensor(out=out, in0=in0, scalar=scalar, in1=in1, op0=op0, op1=op1), R, W)

    def copy(self, eng, out, in_, R, W):
        if eng == "act":
            self.emit("act", lambda e: e.activation(out=out, in_=in_, func=AF.Copy), R, W)
        else:
            self.emit(eng, lambda e: e.tensor_copy(out=out, in_=in_), R, W)

    def memset(self, eng, ap, val, W):
        self.emit(eng, lambda e: e.memset(ap, val), (), W)

    def dma(self, eng, out, in_, sem, R, W):
        self.emit(eng, lambda e: e.dma_start(out=out, in_=in_), R, W, dma_sem=sem)


class Arena:
    def __init__(self, P, nbytes):
        self.t = P.sbuf("arena", [128, nbytes], U8)
        self.n = nbytes
        self.off = 0

    def alloc(self, shape, dt):
        n = int(np.prod(shape)) * DTSZ[dt]
        n_al = (n + 63) // 64 * 64
        assert self.off + n_al <= self.n, f"arena overflow {self.off}+{n_al}>{self.n}"
        v = self.t[:, self.off:self.off + n].bitcast(dt)
        self.off += n_al
        if len(shape) == 2:
            v = v.rearrange("p (a b) -> p a b", a=shape[0])
        elif len(shape) == 3:
            v = v.rearrange("p (a b c) -> p a b c", a=shape[0], b=shape[1])
        elif len(shape) == 4:
            v = v.rearrange("p (a b c d) -> p a b c d", a=shape[0], b=shape[1], c=shape[2])
        return v

    def mark(self):
        return self.off

    def release(self, m):
        self.off = m


ARENA_BYTES = 207 * 1024


def build_program(debug=False, stop_after=None):
    nc = bass.Bass("TRN2", target_bir_lowering=False)
    P = Prog(nc)

    def din(name, shape, dt=F32):
        return nc.dram_tensor(name, list(shape), dt, kind="ExternalInput").ap()

    skind = "ExternalOutput" if debug else "Internal"

    def dscr(name, shape, dt):
        return nc.dram_tensor(name, list(shape), dt, kind=skind).ap()

    x_d = din("x", [L, D])
    ctx_d = din("ctx", [LC, D])
    cvec_d = din("cvec", [128, 8, 2])
    wmod_d = din("w_mod", [D, 6 * D])
    bmod_d = din("b_mod", [128, 48])
    gA_d = din("gA", [128, 8])
    gF_d = din("gF", [128, 8])
    gFin_d = din("gFin", [128, 8])
    win_d = din("w_in_ext", [D, 3072])
    wgate_d = din("w_gate", [D, 2048])
    cos_d = din("rope_cos", [128, L])
    sin_d = din("rope_sin", [128, L])
    tb_d = din("bias_tb", [128, 4, 15, 64])
    mk_d = din("bias_mask", [128, 64])
    lre_d = din("lam_re", [32, 128])
    lim_d = din("lam_im", [32, 128])
    ldt_d = din("log_dt", [32, 2])
    bre_d = din("b_re", [2, 32, 64, 16])
    bim_d = din("b_im", [2, 32, 64, 16])
    cre_d = din("c_re", [2, 32, 16, 64])
    cim_d = din("c_im", [2, 32, 16, 64])
    dsk_d = din("d_skip", [128, 4])
    wglu_d = din("w_glu", [SW, 2 * SW])
    wba_d = din("w_ba", [AW, D])
    wbs_d = din("w_bs", [SW, D])
    wout_d = din("w_out", [D, D])
    wfi_d = din("w_ffn_in", [D, 2 * FH])
    wfo_d = din("w_ffn_out", [FH, D])
    out_d = nc.dram_tensor("out", [L, D], F32, kind="ExternalOutput").ap()

    qT_s = dscr("qT_s", [4, 128, L], BF16)
    uT_s = dscr("uT_s", [4, 128, UEXT], BF16)
    aT_s = dscr("aT_s", [4, 128, L], BF16)
    h_s = dscr("h_s", [L, D], F32)
    y_o = dscr("y_o", [2, 4, 128, L], BF16)
    if debug:
        kT_o = dscr("kT_o", [4, 128, NTOK], BF16)
        v_o = dscr("v_o", [34, 128, 512], BF16)
        mod_o = dscr("mod_o", [128, 96], F32)
        gb_o = dscr("gb_o", [128, 3 * D], F32)

    A = Arena(P, ARENA_BYTES)
    ps = P.psum("ps", [128, 4096], F32)

    def bank(b, n=512, off=0):
        return ps[:, b * 512 + off:b * 512 + off + n]

    def bank_bf(b):
        return ps[:, b * 512:(b + 1) * 512].bitcast(BF16)

    Tb = [T(f"bank{b}") for b in range(8)]

    ident_f = A.alloc([128], F32)
    ident_b = A.alloc([128], BF16)
    ones_f = A.alloc([128], F32)
    modv = A.alloc([48, 2], F32)
    A1l = A.alloc([8], F32)
    A1c = A.alloc([8], F32)
    A2 = A.alloc([8], F32)
    gA = A.alloc([8], F32)
    gF = A.alloc([8], F32)
    gFin = A.alloc([8], F32)
    g1b = A.alloc([D], F32)
    g2b = A.alloc([D], F32)
    fgb = A.alloc([D], F32)
    Tconst = T("const")
    Tdbg = T("dbg")
    Tmod = T("mod")
    Tgb = T("gb")

    P.memset("pool", ident_f, 1.0, [Tconst])
    P.emit("pool", lambda e: e.affine_select(out=ident_f, in_=ident_f, pattern=[[-1, 128]], compare_op=ALU.is_equal,
                                             fill=0.0, base=0, channel_multiplier=1), [Tconst], [Tconst])
    P.copy("pool", ident_b, ident_f, [Tconst], [Tconst])
    P.memset("pool", ones_f, 1.0, [Tconst])

    m0 = A.mark()
    cv = A.alloc([8, 2], F32)
    cvb = A.alloc([8, 2], BF16)
    bmod = A.alloc([48], F32)
    wmb = [A.alloc([8, 1024], BF16) for _ in range(2)]
    Tcv, Tbm = T("cv"), T("bmod")
    Twm = [T("wm0"), T("wm1")]
    P.dma("sp", cv, cvec_d, "ld_cv", [], [Tcv])
    P.dma("sp", bmod, bmod_d, "ld_cv", [], [Tbm])
    P.dma("sp", gA, gA_d, "ld_cv", [], [Tmod])
    P.dma("sp", gF, gF_d, "ld_cv", [], [Tmod])
    P.dma("sp", gFin, gFin_d, "ld_cv", [], [Tmod])
    P.act(cvb, cv, AF.Silu, [Tcv], [Tcv])
    wm_v = wmod_d.rearrange("(k p) m -> p k m", p=128)
    for blk in range(6):
        bi = blk % 2
        P.dma("pool", wmb[bi], wm_v[:, :, blk * 1024:(blk + 1) * 1024], f"ld_wm{bi}", [], [Twm[bi]])
        for ml in range(8):
            m = blk * 8 + ml
            for k in range(8):
                P.mm(bank(0, 2, m * 2), wmb[bi][:, k, ml * 128:(ml + 1) * 128], cvb[:, k, :], k == 0, k == 7,
                     [Twm[bi], Tcv], [Tb[0]])
    P.tt("dve", modv, bank(0, 96).rearrange("p (m j) -> p m j", j=2), bmod.unsqueeze(2).to_broadcast([128, 48, 2]),
         ALU.add, [Tb[0], Tbm], [Tmod])
    for (dst, j) in ((A1l, 0), (A1c, 1)):
        P.ts("dve", dst, modv[:, 8:16, j], 1.0, None, ALU.add, None, [Tmod], [Tmod])
        P.tt("dve", dst, dst, gA, ALU.mult, [Tmod], [Tmod])
    P.ts("dve", A2, modv[:, 32:40, 0], 1.0, None, ALU.add, None, [Tmod], [Tmod])
    P.tt("dve", A2, A2, gF, ALU.mult, [Tmod], [Tmod])
    B1l = modv[:, 0:8, 0]
    B1c = modv[:, 0:8, 1]
    B2 = modv[:, 24:32, 0]
    diag = [A.alloc([128], F32) for _ in range(2)]
    Tdg = [T("dg0"), T("dg1")]
    n = 0
    for (dst, vec) in ((g1b, modv[:, 16:24, 0]), (g2b, modv[:, 40:48, 0]), (fgb, gFin)):
        for k in range(8):
            di = n % 2
            bk = 1 + (n % 2)
            P.ts("dve", diag[di], ident_f, vec[:, k:k + 1], None, ALU.mult, None, [Tmod, Tconst], [Tdg[di]])
            P.mm(bank(bk, 128), ones_f, diag[di], True, True, [Tconst, Tdg[di]], [Tb[bk]])
            P.copy("act", dst[:, k * 128:(k + 1) * 128], bank(bk, 128), [Tb[bk]], [Tgb])
            n += 1
    if debug:
        P.dma("sp", mod_o, modv.rearrange("p m j -> p (m j)"), "dbg", [Tmod], [Tdbg])
        P.dma("sp", gb_o[:, 0:D], g1b, "dbg", [Tgb], [Tdbg])
        P.dma("sp", gb_o[:, D:2 * D], g2b, "dbg", [Tgb], [Tdbg])
        P.dma("sp", gb_o[:, 2 * D:3 * D], fgb, "dbg", [Tgb], [Tdbg])
    P.barrier()
    A.release(m0)
    if stop_after == "p0":
        return finish(P, nc, [Tdbg])

    mA = A.mark()
    kT = A.alloc([4, NTOK], BF16)
    vtok = A.alloc([34, 512], BF16)
    TkT = [T(f"kT{g}") for g in range(9)]
    Tv = [T(f"v{g}") for g in range(9)]
    m1 = A.mark()
    wA = A.alloc([8, 3072], BF16)
    TwA = T("wA")
    win_v = win_d.rearrange("(k p) m -> p k m", p=128)
    for c6 in range(6):
        P.dma("pool", wA[:, :, c6 * 512:(c6 + 1) * 512], win_v[:, :, c6 * 512:(c6 + 1) * 512], "ld_wA", [], [TwA])
    xt = [A.alloc([D], F32) for _ in range(3)]
    Txt = [T(f"xt{i}") for i in range(3)]
    junk = A.alloc([D], BF16)
    Tjunk = T("junk")
    xn = [A.alloc([D], BF16) for _ in range(2)]
    Txn = [T(f"xn{i}") for i in range(2)]
    st = [A.alloc([4], F32) for _ in range(3)]
    Tst = [T(f"st{i}") for i in range(3)]
    nT = [A.alloc([8, 512], BF16) for _ in range(2)]
    TnT = [T("nT0"), T("nT1")]
    ropec = [A.alloc([512], F32) for _ in range(2)]
    ropes = [A.alloc([512], F32) for _ in range(2)]
    Trope = [T("rope0"), T("rope1")]
    rt = [A.alloc([512], F32) for _ in range(4)]
    Trt = [T(f"rt{i}") for i in range(4)]
    qst = [A.alloc([4, 512], BF16) for _ in range(2)]
    Tqst = [T("qst0"), T("qst1")]
    ust = [A.alloc([4, 512], BF16) for _ in range(2)]
    Tust = [T("ust0"), T("ust1")]
    TqS, TuS = T("qT_s"), T("uT_s")

    groups = [("ctx", 0, LC)] + [("lat", g * 512, 512) for g in range(8)]
    tile_no = 0
    mmbank = 0
    for gi, (kind, t0, ntok) in enumerate(groups):
        nb = gi % 2
        ntile = ntok // 128
        src = ctx_d if kind == "ctx" else x_d
        Asc, Bsc = (A1c, B1c) if kind == "ctx" else (A1l, B1l)
        if kind == "lat":
            P.dma("act", ropec[nb], cos_d[:, t0:t0 + 512], f"ld_rope{nb}", [], [Trope[nb]])
            P.dma("act", ropes[nb], sin_d[:, t0:t0 + 512], f"ld_rope{nb}", [], [Trope[nb]])
        for half in range((ntile + 1) // 2):
            tb_ = 0 if (gi * 2 + half) % 2 == 0 else 2
            pT = ps[:, tb_ * 512:(tb_ + 2) * 512].bitcast(BF16).rearrange("p (c t) -> p c t", c=8)
            Tp = Tb[tb_]
            for tl in range(2):
                ti = half * 2 + tl
                xi = tile_no % 3
                si = tile_no % 3
                ni = tile_no % 2
                tile_no += 1
                P.dma("sp", xt[xi], src[t0 + ti * 128:t0 + (ti + 1) * 128, :], f"ld_x{xi}", [], [Txt[xi]])
                P.act(junk, xt[xi], AF.Square, [Txt[xi]], [Tjunk, Tst[si]], accum=st[si][:, 0:1])
                rstd_ops(P, st[si], Tst[si])
                P.act(xn[ni], xt[xi], AF.Copy, [Txt[xi], Tst[si]], [Txn[ni]], scale=st[si][:, 2:3])
                for c in range(8):
                    P.tr(pT[:, c, tl * 128:(tl + 1) * 128], xn[ni][:, c * 128:(c + 1) * 128], ident_b,
                         [Txn[ni], Tconst], [Tp])
            for c in range(8):
                P.ts("dve", nT[nb][:, c, half * 256:(half + 1) * 256], pT[:, c, :], Asc[:, c:c + 1], Bsc[:, c:c + 1],
                     ALU.mult, ALU.add, [Tp, Tmod], [TnT[nb]])
        cols = slice(0, ntok)

        def proj(mcol, bk):
            for k in range(8):
                P.mm(bank(bk, ntok), wA[:, k, mcol:mcol + 128], nT[nb][:, k, cols], k == 0, k == 7,
                     [TwA, TnT[nb]], [Tb[bk]])

        kcol0 = L if kind == "ctx" else t0
        for j in range(4):
            if kind == "ctx":
                bk = 4 + mmbank % 4
                mmbank += 1
                proj(512 + j * 128, bk)
                P.copy("act", kT[:, j, kcol0:kcol0 + ntok], bank(bk, ntok), [Tb[bk]], [TkT[gi]])
            else:
                for (base, swb, dst, Td, isq) in ((0, 2048, qst[nb], Tqst[nb], True), (512, 2560, None, TkT[gi], False)):
                    bk = 4 + mmbank % 4
                    bk2 = 4 + (mmbank + 1) % 4
                    mmbank += 2
                    proj(base + j * 128, bk)
                    proj(swb + j * 128, bk2)
                    rsel = (mmbank // 2) % 2 * 2
                    r0, r1 = rt[rsel], rt[rsel + 1]
                    Tr0, Tr1 = Trt[rsel], Trt[rsel + 1]
                    P.tt("dve", r0, bank(bk), ropec[nb], ALU.mult, [Tb[bk], Trope[nb]], [Tr0])
                    P.tt("dve", r1, bank(bk2), ropes[nb], ALU.mult, [Tb[bk2], Trope[nb]], [Tr1])
                    o = qst[nb][:, j, :] if isq else kT[:, j, t0:t0 + 512]
                    P.tt("pool", o, r0, r1, ALU.add, [Tr0, Tr1], [Td])
            bk = 4 + mmbank % 4
            mmbank += 1
            proj(1536 + j * 128, bk)
            P.copy("act", ust[nb][:, j, cols], bank(bk, ntok), [Tb[bk]], [Tust[nb]])
        for ti in range(ntile):
            bk = 4 + mmbank % 4
            mmbank += 1
            for k in range(8):
                P.mm(bank(bk), nT[nb][:, k, ti * 128:(ti + 1) * 128], wA[:, k, 1024:1536], k == 0, k == 7,
                     [TwA, TnT[nb]], [Tb[bk]])
            vt = (32 + ti) if kind == "ctx" else (t0 // 128 + ti)
            P.copy("act", vtok[:, vt, :], bank(bk), [Tb[bk]], [Tv[gi]])
        if kind == "lat":
            P.dma("sp", qT_s[:, :, t0:t0 + 512].rearrange("c p t -> p c t"), qst[nb], f"st_q{nb}", [Tqst[nb]], [TqS])
            P.dma("sp", uT_s[:, :, LC + t0:LC + t0 + 512].rearrange("c p t -> p c t"), ust[nb], f"st_u{nb}",
                  [Tust[nb]], [TuS])
        else:
            P.dma("sp", uT_s[:, :, 0:LC].rearrange("c p t -> p c t"), ust[nb][:, :, 0:LC], f"st_u{nb}", [Tust[nb]], [TuS])
            P.dma("sp", uT_s[:, :, LC + L:UEXT].rearrange("c p t -> p c t"), ust[nb][:, :, 0:LC], f"st_u{nb}",
                  [Tust[nb]], [TuS])
    if debug:
        P.dma("sp", kT_o.rearrange("c p t -> p c t"), kT, "dbg", TkT, [Tdbg])
        P.dma("sp", v_o.rearrange("n p f -> p n f"), vtok, "dbg", Tv, [Tdbg])
    P.barrier()
    A.release(m1)
    if stop_after == "pA":
        return finish(P, nc, [TqS, TuS, Tdbg])

    PI = math.pi
    s_ar = A.alloc([32], F32)
    s_ai = A.alloc([32], F32)
    s_nai = A.alloc([32], F32)
    a16r = A.alloc([32], F32)
    a16i = A.alloc([32], F32)
    Wt = A.alloc([2, 4, 2, 128], BF16)
    CwT = A.alloc([32, 2, 32], BF16)
    Eb = A.alloc([2, NCH, 32], F32)
    hst = [[A.alloc([32], F32) for _ in range(2)] for _ in range(2)]
    ct = [A.alloc([32], F32) for _ in range(4)]
    Tprm, TWt, TCw, TE = T("s5prm"), T("Wt"), T("CwT"), T("E")
    TyT = [[T(f"yT{d}{c}") for c in range(4)] for d in range(2)]
    mS1 = A.mark()
    L32 = A.alloc([3, 128], F32)
    ldt = A.alloc([2], F32)
    prm = A.alloc([3, 32], F32)
    w32 = [A.alloc([32], F32) for _ in range(12)]
    negpi = A.alloc([1], F32)
    qint = A.alloc([32], I32)
    breT = A.alloc([32, 16], F32)
    bimT = A.alloc([32, 16], F32)
    bbr = A.alloc([32, 16], F32)
    bbi = A.alloc([32, 16], F32)
    btmp = A.alloc([32, 16], F32)
    srcw = [A.alloc([128], F32) for _ in range(8)]
    Cblk1 = A.alloc([32, 128], F32)
    Cblk = [Cblk1, Cblk1]
    TCb1 = T("Cb")
    TL32, Tbb, Tsrc, TCb = T("L32"), T("bb"), [T(f"srcw{i}") for i in range(8)], [TCb1, TCb1]
    P.dma("sp", L32[0:32, 0, :], lre_d, "ld_s5", [], [TL32])
    P.dma("sp", L32[0:32, 1, :], lim_d, "ld_s5", [], [TL32])
    P.dma("sp", ldt[0:32, :], ldt_d, "ld_s5", [], [TL32])
    P.dma("sp", breT, bre_d.rearrange("d (j g) p c -> (g p) (d j) c", g=2), "ld_s5b", [], [Tbb])
    P.dma("sp", bimT, bim_d.rearrange("d (j g) p c -> (g p) (d j) c", g=2), "ld_s5b", [], [Tbb])
    def load_cblk(ri):
        cd = (cre_d, cim_d)[ri]
        if ri == 0:
            P.memset("pool", Cblk1, 0.0, [TCb1])
        cv_ = cd.rearrange("d (j g) h p -> g h (d j) p", g=2)
        for g2 in range(2):
            P.dma("sp", Cblk1[g2 * 16:(g2 + 1) * 16, :, g2 * 64:(g2 + 1) * 64], cv_[g2], "ld_s5c", [], [TCb1])

    load_cblk(0)
    P.memset("pool", negpi, -PI, [Tprm])
    for i in range(8):
        P.memset("pool", srcw[i], 0.0, [Tsrc[i]])
    P.copy("dve", L32[0:32, 2, :].rearrange("p (g q) -> p g q", g=2), ldt[0:32, :].unsqueeze(2).to_broadcast([32, 2, 64]),
           [TL32], [TL32])
    for i in range(3):
        P.mm(bank(0, 32, i * 32), L32[0:32, i, :], ident_f[0:32, 0:32], True, True, [TL32, Tconst], [Tb[0]])
    P.copy("dve", prm.rearrange("p a b -> p (a b)"), bank(0, 96), [Tb[0]], [Tprm])
    lre, lim, ldtT = prm[:, 0, :], prm[:, 1, :], prm[:, 2, :]
    dt_, tmp_, mag, ang, rs_, sin_, cos_, den, rden, nr, fr, fi = w32
    RW = ([Tprm], [Tprm])
    P.act(dt_, ldtT, AF.Exp, *RW)
    P.tt("dve", tmp_, lre, dt_, ALU.mult, *RW)
    P.act(mag, tmp_, AF.Exp, *RW)
    P.tt("dve", ang, lim, dt_, ALU.mult, *RW)
    def sin_of(dst, shift):
        P.ts("dve", rs_, ang, shift, 1.0 / (2 * PI), ALU.add, ALU.mult, *RW)
        P.copy("dve", qint, rs_, *RW)
        P.copy("dve", rden, qint, *RW)
        P.ts("dve", rs_, ang, shift, None, ALU.add, None, *RW)
        P.stt("dve", rs_, rden, -2 * PI, rs_, ALU.mult, ALU.add, *RW)
        P.ts("dve", rden, rs_, PI, None, ALU.is_gt, None, *RW)
        P.stt("dve", rs_, rden, -2 * PI, rs_, ALU.mult, ALU.add, *RW)
        P.ts("dve", rden, rs_, -PI, None, ALU.is_lt, None, *RW)
        P.stt("dve", rs_, rden, 2 * PI, rs_, ALU.mult, ALU.add, *RW)
        P.act(dst, rs_, AF.Sin, *RW)

    sin_of(sin_, 0.0)
    sin_of(cos_, 0.5 * PI)
    P.tt("dve", s_ar, mag, cos_, ALU.mult, *RW)
    P.tt("dve", s_ai, mag, sin_, ALU.mult, *RW)
    P.ts("dve", s_nai, s_ai, -1.0, None, ALU.mult, None, *RW)
    P.tt("dve", den, lre, lre, ALU.mult, *RW)
    P.tt("dve", tmp_, lim, lim, ALU.mult, *RW)
    P.tt("dve", den, den, tmp_, ALU.add, *RW)
    P.emit("dve", lambda e: e.reciprocal(out=rden, in_=den), *RW)
    P.ts("dve", nr, s_ar, -1.0, None, ALU.add, None, *RW)
    P.tt("dve", fr, nr, lre, ALU.mult, *RW)
    P.tt("dve", tmp_, s_ai, lim, ALU.mult, *RW)
    P.tt("dve", fr, fr, tmp_, ALU.add, *RW)
    P.tt("dve", fr, fr, rden, ALU.mult, *RW)
    P.tt("dve", fi, s_ai, lre, ALU.mult, *RW)
    P.tt("dve", tmp_, nr, lim, ALU.mult, *RW)
    P.tt("dve", fi, fi, tmp_, ALU.subtract, *RW)
    P.tt("dve", fi, fi, rden, ALU.mult, *RW)
    P.copy("dve", a16r, s_ar, *RW)
    P.copy("dve", a16i, s_ai, *RW)
    for _ in range(4):
        P.tt("dve", tmp_, a16r, a16r, ALU.mult, *RW)
        P.tt("dve", den, a16i, a16i, ALU.mult, *RW)
        P.tt("dve", nr, a16r, a16i, ALU.mult, *RW)
        P.tt("dve", a16r, tmp_, den, ALU.subtract, *RW)
        P.ts("dve", a16i, nr, 2.0, None, ALU.mult, None, *RW)
    frb = fr.unsqueeze(2).to_broadcast([128, 32, 16])
    fib = fi.unsqueeze(2).to_broadcast([128, 32, 16])
    Rb = [Tprm, Tbb]
    P.tt("dve", bbr, breT, frb, ALU.mult, Rb, [Tbb])
    P.tt("dve", btmp, bimT, fib, ALU.mult, Rb, [Tbb])
    P.tt("dve", bbr, bbr, btmp, ALU.subtract, Rb, [Tbb])
    P.tt("dve", bbi, bimT, frb, ALU.mult, Rb, [Tbb])
    P.tt("dve", btmp, breT, fib, ALU.mult, Rb, [Tbb])
    P.tt("dve", bbi, bbi, btmp, ALU.add, Rb, [Tbb])
    n = 0
    for d in range(2):
        for j in range(16):
            dj = d * 16 + j
            cj, jm = j // 4, j % 4
            for ri, bb_ in enumerate((bbr, bbi)):
                sw = srcw[ri * 4 + jm]
                Ts_ = Tsrc[ri * 4 + jm]
                for g2 in range(2):
                    P.copy("dve", sw[g2 * 64:(g2 + 1) * 64, 32 * jm + 16 * g2:32 * jm + 16 * g2 + 16],
                           bb_[g2 * 64:(g2 + 1) * 64, dj, :], [Tbb], [Ts_])
                bk = 1 + n % 2
                n += 1
                P.mm(bank(bk, 128), sw, ident_f, True, True, [Ts_, Tconst], [Tb[bk]])
                P.copy("act", Wt[32 * jm:32 * jm + 32, d, cj, ri, :], bank(bk, 128)[32 * jm:32 * jm + 32, :], [Tb[bk]], [TWt])
    for ri in range(2):
        if ri == 1:
            load_cblk(1)
        for dj in range(32):
            bk = 1 + n % 2
            n += 1
            P.mm(bank(bk, 32), Cblk1[0:32, dj, :], ident_f[0:32, 0:32], True, True, [TCb1, Tconst], [Tb[bk]])
            if ri == 0:
                P.copy("act", CwT[:, dj, 0, :], bank(bk, 32), [Tb[bk]], [TCw])
            else:
                P.act(CwT[:, dj, 1, :], bank(bk, 32), AF.Copy, [Tb[bk]], [TCw], scale=-1.0)
    P.barrier()
    A.release(mS1)

    mP = A.mark()

    def useq(u, jm, d, i):
        if d == 0:
            return u[32 * jm:32 * jm + 32, i:NTOK:TCH]
        return u[32 * jm:32 * jm + 32, UEXT - 1 - i:LC - 1:-TCH]

    def scan_pass(phase2, yT=None):
        uch = [A.alloc([UEXT], BF16) for _ in range(2)]
        Tuch = [T("uch0"), T("uch1")]
        Hs = [[A.alloc([NCH], F32) for _ in range(2)] for _ in range(4)]
        THs = [T(f"H{i}") for i in range(4)]
        tm = [[A.alloc([NCH], F32) for _ in range(2)] for _ in range(2)]
        Ttm = [T("tm0"), T("tm1")]
        Hb = [[[A.alloc([NCH], BF16) for _ in range(2)] for _ in range(4)] for _ in range(2)]
        THb = [[T(f"Hb{b}{i}") for i in range(4)] for b in range(2)]
        steps = [(cj, d, i, jm) for cj in range(4) for d in range(2) for i in range(TCH) for jm in range(4)]
        N = len(steps)

        def load_u(cj):
            ub = cj % 2
            P.dma("sp", uch[ub], uT_s[cj], f"ld_u{ub}_{int(phase2)}", [TuS], [Tuch[ub]])

        def emitBU(n):
            cj, d, i, jm = steps[n]
            ub = cj % 2
            pb = n % 2
            rhs = useq(uch[ub], jm, d, i)
            for ri in range(2):
                P.mm(bank(pb * 2 + ri, NCH), Wt[32 * jm:32 * jm + 32, d, cj, ri, :], rhs, True, True, [TWt, Tuch[ub]],
                     [Tb[pb * 2]], tp=(32 * jm, 0))

        load_u(0)
        emitBU(0)
        for n in range(N):
            cj, d, i, jm = steps[n]
            if d == 0 and i == 0 and jm == 0 and cj + 1 < 4:
                load_u(cj + 1)
            if n + 1 < N:
                emitBU(n + 1)
            stepno = n // 4
            ybk = 4 + stepno % 2
            dj = d * 16 + cj * 4 + jm
            pb = n % 2
            BUr, BUi = bank(pb * 2, NCH), bank(pb * 2 + 1, NCH)
            TB = Tb[pb * 2]
            Hr, Hi = Hs[jm]
            TH = THs[jm]
            Er, Ei = Eb[:, 0, :, dj], Eb[:, 1, :, dj]
            if i == 0 and not phase2:
                P.copy("dve", Hr, BUr, [TB], [TH])
                P.copy("dve", Hi, BUi, [TB], [TH])
                continue
            if i == 0:
                inr, ini, Rin = Er, Ei, [TE]
            else:
                inr, ini, Rin = Hr, Hi, [TH]
            last1 = (i == TCH - 1) and not phase2
            outr, outi, Wout = (Er, Ei, [TE]) if last1 else (Hr, Hi, [TH])
            t1, t2 = tm[jm % 2]
            Tt = Ttm[jm % 2]
            P.stt("dve", t1, ini, s_nai[:, dj:dj + 1], BUr, ALU.mult, ALU.add, Rin + [TB, Tprm], [Tt])
            P.stt("dve", t2, inr, s_ai[:, dj:dj + 1], BUi, ALU.mult, ALU.add, Rin + [TB, Tprm], [Tt])
            P.stt("dve", outr, inr, s_ar[:, dj:dj + 1], t1, ALU.mult, ALU.add, Rin + [Tt, Tprm], Wout)
            P.stt("dve", outi, ini, s_ar[:, dj:dj + 1], t2, ALU.mult, ALU.add, Rin + [Tt, Tprm], Wout)
            if phase2:
                hb = stepno % 2
                Hbr, Hbi = Hb[hb][jm]
                P.copy("act", Hbr, Hr, [TH], [THb[hb][jm]])
                P.copy("act", Hbi, Hi, [TH], [THb[hb][jm]])
                yps = bank(ybk, NCH)[32 * jm:32 * jm + 32, :]
                P.mm(yps, CwT[:, dj, 0, :], Hbr, True, False, [TCw, THb[hb][jm]], [Tb[ybk]], tp=(0, 32 * jm))
                P.mm(yps, CwT[:, dj, 1, :], Hbi, False, True, [TCw, THb[hb][jm]], [Tb[ybk]], tp=(0, 32 * jm))
                if jm == 3:
                    if d == 0:
                        dst = yT[:, 0, cj, i:L:TCH]
                    else:
                        dst = yT[:, 1, cj, L - 1 - i:None:-TCH]
                    P.copy("act", dst, bank(ybk, NCH)[:, LC // TCH:NCH], [Tb[ybk]], [TyT[d][cj]])

    scan_pass(False)
    P.barrier()
    A.release(mP)
    if stop_after == "pS1":
        return finish(P, nc, [TqS, TuS, Tdbg])
    Th_ = [T("ch0"), T("ch1")]
    Tc_ = [T(f"cc{i}") for i in range(4)]
    P.memset("pool", hst[0][0], 0.0, [Th_[0]])
    P.memset("pool", hst[0][1], 0.0, [Th_[0]])
    for s_ in range(NCH):
        hr, hi = hst[s_ % 2]
        nhr, nhi = hst[(s_ + 1) % 2]
        Tho, Thn_ = Th_[s_ % 2], Th_[(s_ + 1) % 2]
        P.tt("pool", ct[0], a16r, hr, ALU.mult, [Tho, Tprm], [Tc_[0]])
        P.tt("pool", ct[1], a16i, hi, ALU.mult, [Tho, Tprm], [Tc_[1]])
        P.tt("pool", ct[2], a16r, hi, ALU.mult, [Tho, Tprm], [Tc_[2]])
        P.tt("pool", ct[3], a16i, hr, ALU.mult, [Tho, Tprm], [Tc_[3]])
        P.tt("pool", ct[0], ct[0], ct[1], ALU.subtract, [Tc_[0], Tc_[1]], [Tc_[0]])
        P.tt("pool", ct[2], ct[2], ct[3], ALU.add, [Tc_[2], Tc_[3]], [Tc_[2]])
        P.tt("pool", nhr, ct[0], Eb[:, 0, s_, :], ALU.add, [Tc_[0], TE], [Thn_])
        P.tt("pool", nhi, ct[2], Eb[:, 1, s_, :], ALU.add, [Tc_[2], TE], [Thn_])
        P.copy("pool", Eb[:, 0, s_, :], hr, [Tho], [TE])
        P.copy("pool", Eb[:, 1, s_, :], hi, [Tho], [TE])
    if stop_after == "pCH":
        return finish(P, nc, [TqS, TuS, Tdbg])
    mB = A.mark()
    tbias = A.alloc([4, 15, 64], F32)
    mk = A.alloc([64], F32)
    Ttb = T("tbias")
    P.dma("act", tbias, tb_d, "ld_tb", [], [Ttb])
    P.dma("act", mk, mk_d, "ld_tb", [], [Ttb])
    for j in range(4):
        P.tt("dve", tbias[:, j], tbias[:, j], mk.unsqueeze(1).to_broadcast([128, 15, 64]), ALU.add, [Ttb], [Ttb])
    NQB = 3
    qbd = [A.alloc([4, 128], BF16) for _ in range(NQB)]
    Tqbd = [T(f"qbd{i}") for i in range(NQB)]
    for i in range(NQB):
        P.memset("dve", qbd[i], 0.0, [Tqbd[i]])
    sl = [A.alloc([512], F32) for _ in range(2)]
    Tsl = [T("sl0"), T("sl1")]
    Pt = [A.alloc([896], BF16) for _ in range(2)]
    TPt = [T("Pt0"), T("Pt1")]
    for i in range(2):
        P.memset("dve", Pt[i], 0.0, [TPt[i]])
    PTs = [A.alloc([896], BF16) for _ in range(2)]
    TPTs = [T("PTs0"), T("PTs1")]
    On = [A.alloc([128], BF16) for _ in range(2)]
    TOn = [T("On0"), T("On1")]
    sm = [A.alloc([8], F32) for _ in range(2)]
    Tsm = [T("sm0"), T("sm1")]
    ast = [A.alloc([4, 512], BF16) for _ in range(2)]
    Tast = [T("ast0"), T("ast1")]
    TaS = T("aT_s")
    TSl = [Tb[0], Tb[1]]
    TSc = [T("Sc0"), T("Sc1")]
    TPT = [Tb[3], Tb[4]]
    TO = [T("O0"), T("O1")]
    TOT = [T("OT0"), T("OT1")]
    it = 0
    for r in range(64):
        rs = min(max(r - 4, 0), 56)
        d0 = rs - r + 7
        qi = r % NQB
        c0 = r * 64
        for hh in range(2):
            P.dma("sp", qbd[qi][hh * 64:(hh + 1) * 64, :, hh * 64:(hh + 1) * 64],
                  qT_s[:, hh * 64:(hh + 1) * 64, c0:c0 + 64].rearrange("c p t -> p c t"), f"ld_q{qi}", [TqS], [Tqbd[qi]])
        g0 = 1 + (rs * 64) // 512
        g1 = 1 + (rs * 64 + 511) // 512
        Rk = [TkT[0], TkT[g0]] + ([TkT[g1]] if g1 != g0 else [])
        Rv = [Tv[0], Tv[g0]] + ([Tv[g1]] if g1 != g0 else [])
        odd = rs % 2 == 1
        gsel = (r // 8) % 2
        for j in range(4):
            b2 = it % 2
            it += 1
            S_l = bank(b2)
            S_c = bank(2, 256, b2 * 256)
            P.mm(S_l, qbd[qi][:, j, :], kT[:, j, rs * 64:rs * 64 + 512], True, True, [Tqbd[qi]] + Rk, [TSl[b2]])
            P.mm(S_c, qbd[qi][:, j, :], kT[:, j, L:L + LC], True, True, [Tqbd[qi]] + Rk, [TSc[b2]])
            smt = sm[b2]
            P.stt("dve", sl[b2], S_l, 0.125, tbias[:, j, d0:d0 + 8, :].rearrange("p a b -> p (a b)"), ALU.mult, ALU.add,
                  [TSl[b2], Ttb], [Tsl[b2]])
            P.emit("dve", lambda e, o=smt[:, 0:1], i=sl[b2]: e.reduce_max(out=o, in_=i, axis=AX.X), [Tsl[b2]], [Tsm[b2]])
            P.emit("dve", lambda e, o=smt[:, 1:2], i=S_c: e.reduce_max(out=o, in_=i, axis=AX.X), [TSc[b2]], [Tsm[b2]])
            P.stt("dve", smt[:, 2:3], smt[:, 1:2], 0.125, smt[:, 0:1], ALU.mult, ALU.max, [Tsm[b2]], [Tsm[b2]])
            P.ts("dve", smt[:, 3:4], smt[:, 2:3], -1.0, None, ALU.mult, None, [Tsm[b2]], [Tsm[b2]])
            P.act(Pt[b2][:, 64:576], sl[b2], AF.Exp, [Tsl[b2], Tsm[b2]], [TPt[b2], Tsm[b2]], bias=smt[:, 3:4],
                  accum=smt[:, 4:5])
            P.act(Pt[b2][:, 640:896], S_c, AF.Exp, [TSc[b2], Tsm[b2]], [TPt[b2], Tsm[b2]], bias=smt[:, 3:4], scale=0.125,
                  accum=smt[:, 5:6])
            P.tt("dve", smt[:, 6:7], smt[:, 4:5], smt[:, 5:6], ALU.add, [Tsm[b2]], [Tsm[b2]])
            P.emit("dve", lambda e, o=smt[:, 7:8], i=smt[:, 6:7]: e.reciprocal(out=o, in_=i), [Tsm[b2]], [Tsm[b2]])
            PTp = bank_bf(3 + b2)
            if not odd:
                blocks = [(64 + 128 * b, 0, 128, rs // 2 + b) for b in range(4)]
            else:
                vb = (rs - 1) // 2
                blocks = [(0, 64, 128, vb)] + [(128 * b, 0, 128, vb + b) for b in (1, 2, 3)] + [(512, 0, 64, vb + 4)]
            blocks += [(640, 0, 128, 32), (768, 0, 128, 33)]
            nblk = len(blocks)
            for bi, (col, p0, p1, vt) in enumerate(blocks):
                P.tr(PTp[:, bi * 128:(bi + 1) * 128], Pt[b2][:, col:col + 128], ident_b, [TPt[b2], Tconst], [TPT[b2]])
            P.copy("act", PTs[b2][:, 0:nblk * 128], PTp[:, 0:nblk * 128], [TPT[b2]], [TPTs[b2]])
            O = bank(5, 128, b2 * 128)
            for bi, (col, p0, p1, vt) in enumerate(blocks):
                P.mm(O, PTs[b2][p0:p1, bi * 128:(bi + 1) * 128], vtok[p0:p1, vt, j * 128:(j + 1) * 128], bi == 0,
                     bi == nblk - 1, [TPTs[b2]] + Rv, [TO[b2]])
            P.ts("dve", On[b2], O, smt[:, 7:8], None, ALU.mult, None, [TO[b2], Tsm[b2]], [TOn[b2]])
            OT = bank_bf(6)[:, b2 * 128:(b2 + 1) * 128]
            P.tr(OT, On[b2], ident_b, [TOn[b2], Tconst], [TOT[b2]])
            cc = (r % 8) * 64
            P.copy("act", ast[gsel][0:64, j, cc:cc + 64], OT[0:64, 0:64], [TOT[b2]], [Tast[gsel]])
            P.copy("act", ast[gsel][64:128, j, cc:cc + 64], OT[64:128, 64:128], [TOT[b2]], [Tast[gsel]])
        if r % 8 == 7:
            g = r // 8
            P.dma("sp", aT_s[:, :, g * 512:(g + 1) * 512].rearrange("c p t -> p c t"), ast[gsel], f"st_a{gsel}",
                  [Tast[gsel]], [TaS])
    P.barrier()
    A.release(mB)
    if stop_after == "pB":
        return finish(P, nc, [TqS, TuS, TaS, Tdbg])

    cur_ = A.off
    A.off = mA
    yT = A.alloc([2, 4, L], BF16)
    assert A.off <= m1
    A.off = cur_
    scan_pass(True, yT)
    TyS = T("y_s")
    for d in range(2):
        P.dma("sp", y_o[d].rearrange("c p t -> p c t"), yT[:, d], "st_y", TyT[d], [TyS])
    P.barrier()
    A.release(mA)
    if stop_after == "pS":
        return finish(P, nc, [TqS, TuS, TaS, TyS, Tdbg])

    wG = A.alloc([8, 2048], BF16)
    wBA = A.alloc([4, 1024], BF16)
    wBS = A.alloc([4, 1024], BF16)
    wGLU = A.alloc([4, 1024], BF16)
    wO = A.alloc([8, 1024], BF16)
    dsk = A.alloc([4], F32)
    TwC = T("wC")
    wg_v = wgate_d.rearrange("(k p) m -> p k m", p=128)
    for c4 in range(4):
        P.dma("pool", wG[:, :, c4 * 512:(c4 + 1) * 512], wg_v[:, :, c4 * 512:(c4 + 1) * 512], "ld_wC", [], [TwC])
    P.dma("pool", wGLU, wglu_d.rearrange("(k p) m -> p k m", p=128), "ld_wC", [], [TwC])
    P.dma("pool", wBA, wba_d.rearrange("(k p) m -> p k m", p=128), "ld_wC", [], [TwC])
    P.dma("pool", wBS, wbs_d.rearrange("(k p) m -> p k m", p=128), "ld_wC", [], [TwC])
    wo_v = wout_d.rearrange("(k p) m -> p k m", p=128)
    for c2 in range(2):
        P.dma("pool", wO[:, :, c2 * 512:(c2 + 1) * 512], wo_v[:, :, c2 * 512:(c2 + 1) * 512], "ld_wC", [], [TwC])
    P.dma("act", dsk, dsk_d, "ld_dsk", [], [TwC])
    NX = 8
    xt = [A.alloc([D], F32) for _ in range(NX)]
    Txt = [T(f"cxt{i}") for i in range(NX)]
    junk = A.alloc([D], BF16)
    Tjunk = T("cjunk")
    xn = [A.alloc([D], BF16) for _ in range(2)]
    Txn = [T(f"cxn{i}") for i in range(2)]
    st = [A.alloc([4], F32) for _ in range(NX)]
    Tst = [T(f"cst{i}") for i in range(NX)]
    nT = [A.alloc([8, 512], BF16) for _ in range(2)]
    TnT = [T("cnT0"), T("cnT1")]
    aTg = [A.alloc([4, 512], BF16) for _ in range(2)]
    uTg = [A.alloc([4, 512], BF16) for _ in range(2)]
    yg = [A.alloc([2, 4, 512], BF16)] * 2
    Tag = [T("aTg0"), T("aTg1")]
    Tug = [T("uTg0"), T("uTg1")]
    Tyg = [T("yg0")] * 2
    zT = A.alloc([4, 512], BF16)
    sT = A.alloc([4, 512], BF16)
    mT = A.alloc([8, 512], BF16)
    TzT, TsT, TmT = T("zT"), T("sT"), T("mT")
    ft = [A.alloc([512], F32) for _ in range(10)]
    Tft = [T(f"ft{i}") for i in range(10)]
    ThS = T("h_s")
    bcyc = [0]

    def nb_():
        b = 2 + bcyc[0] % 6
        bcyc[0] += 1
        return b

    xcnt_ = [0]
    xis_all = {}

    def c1_front(g):
        nb = g % 2
        t0 = g * 512
        P.dma("act", aTg[nb], aT_s[:, :, t0:t0 + 512].rearrange("c p t -> p c t"), f"ld_ag{nb}", [TaS], [Tag[nb]])
        P.dma("act", uTg[nb], uT_s[:, :, LC + t0:LC + t0 + 512].rearrange("c p t -> p c t"), f"ld_ug{nb}", [TuS], [Tug[nb]])
        for d in range(2):
            P.dma("act", yg[nb][:, d], y_o[d, :, :, t0:t0 + 512].rearrange("c p t -> p c t"), f"ld_yg{nb}", [TyS], [Tyg[nb]])
        xis = []
        for half in range(2):
            pT = ps[:, 0:1024].bitcast(BF16).rearrange("p (c t) -> p c t", c=8)
            for tl in range(2):
                ti = half * 2 + tl
                xi = xcnt_[0] % NX
                ni = xcnt_[0] % 2
                xcnt_[0] += 1
                xis.append(xi)
                P.dma("sp", xt[xi], x_d[t0 + ti * 128:t0 + (ti + 1) * 128, :], f"ld_cx{xi}", [], [Txt[xi]])
                P.act(junk, xt[xi], AF.Square, [Txt[xi]], [Tjunk, Tst[xi]], accum=st[xi][:, 0:1])
                rstd_ops(P, st[xi], Tst[xi])
                P.act(xn[ni], xt[xi], AF.Copy, [Txt[xi], Tst[xi]], [Txn[ni]], scale=st[xi][:, 2:3])
                for c in range(8):
                    P.tr(pT[:, c, tl * 128:(tl + 1) * 128], xn[ni][:, c * 128:(c + 1) * 128], ident_b, [Txn[ni], Tconst], [Tb[0]])
            for c in range(8):
                P.ts("dve", nT[nb][:, c, half * 256:(half + 1) * 256], pT[:, c, :], A1l[:, c:c + 1], B1l[:, c:c + 1],
                     ALU.mult, ALU.add, [Tb[0], Tmod], [TnT[nb]])
        for c in range(4):
            f0, f1 = ft[0], ft[1]
            P.tt("dve", f0, yg[nb][:, 0, c, :], yg[nb][:, 1, c, :], ALU.add, [Tyg[nb]], [Tft[0]])
            P.stt("dve", f0, uTg[nb][:, c, :], dsk[:, c:c + 1], f0, ALU.mult, ALU.add, [Tug[nb], TwC, Tft[0]], [Tft[0]])
            P.tt("dve", f1, f0, f0, ALU.mult, [Tft[0]], [Tft[1]])
            P.ts("dve", f1, f1, 0.044715, 1.0, ALU.mult, ALU.add, [Tft[1]], [Tft[1]])
            P.tt("dve", f1, f1, f0, ALU.mult, [Tft[0], Tft[1]], [Tft[1]])
            P.act(f1, f1, AF.Sigmoid, [Tft[1]], [Tft[1]], scale=1.5957691216057308)
            P.tt("dve", zT[:, c, :], f0, f1, ALU.mult, [Tft[0], Tft[1]], [TzT])
        for m in range(4):
            bv, bg = nb_(), nb_()
            for k in range(4):
                P.mm(bank(bv), wGLU[:, k, m * 128:(m + 1) * 128], zT[:, k, :], k == 0, k == 3, [TwC, TzT], [Tb[bv]])
            for k in range(4):
                P.mm(bank(bg), wGLU[:, k, 512 + m * 128:512 + (m + 1) * 128], zT[:, k, :], k == 0, k == 3, [TwC, TzT], [Tb[bg]])
            P.act(ft[2], bank(bg), AF.Sigmoid, [Tb[bg]], [Tft[2]])
            P.tt("dve", sT[:, m, :], bank(bv), ft[2], ALU.mult, [Tb[bv], Tft[2]], [TsT])
        xis_all[g] = xis

    def c1_stepf(g):
        nb = g % 2
        un = 0
        for m in range(8):
            prods = []
            for which in range(2):
                bg, bb = nb_(), nb_()
                gc = which * 1024 + m * 128
                for k in range(8):
                    P.mm(bank(bg), wG[:, k, gc:gc + 128], nT[nb][:, k, :], k == 0, k == 7, [TwC, TnT[nb]], [Tb[bg]])
                if which == 0:
                    for k in range(4):
                        P.mm(bank(bb), wBA[:, k, m * 128:(m + 1) * 128], aTg[nb][:, k, :], k == 0, k == 3, [TwC, Tag[nb]], [Tb[bb]])
                else:
                    for k in range(4):
                        P.mm(bank(bb), wBS[:, k, m * 128:(m + 1) * 128], sT[:, k, :], k == 0, k == 3, [TwC, TsT], [Tb[bb]])
                si = 3 + un % 2
                pi = 5 + un % 4
                un += 1
                P.act(ft[si], bank(bg), AF.Sigmoid, [Tb[bg]], [Tft[si]])
                P.tt("dve", ft[pi], bank(bb), ft[si], ALU.mult, [Tb[bb], Tft[si]], [Tft[pi]])
                prods.append(pi)
            P.tt("pool", mT[:, m, :], ft[prods[0]], ft[prods[1]], ALU.add, [Tft[prods[0]], Tft[prods[1]]], [TmT])

    def c1_stepg(g):
        nb = g % 2
        t0 = g * 512
        xis = xis_all[g]
        for ti in range(4):
            xi = xis[ti]
            for nh in range(2):
                bo = nb_()
                for k in range(8):
                    P.mm(bank(bo), mT[:, k, ti * 128:(ti + 1) * 128], wO[:, k, nh * 512:(nh + 1) * 512], k == 0, k == 7,
                         [TmT, TwC], [Tb[bo]])
                gi_ = 9 if nh else 2
                P.tt("dve", ft[gi_], bank(bo), g1b[:, nh * 512:(nh + 1) * 512], ALU.mult, [Tb[bo], Tgb], [Tft[gi_]])
                P.tt("pool", xt[xi][:, nh * 512:(nh + 1) * 512], xt[xi][:, nh * 512:(nh + 1) * 512], ft[gi_], ALU.add,
                     [Txt[xi], Tft[gi_]], [Txt[xi]])
            P.dma("sp", h_s[t0 + ti * 128:t0 + (ti + 1) * 128, :], xt[xi], f"st_h{xi}", [Txt[xi]], [ThS])

    c1_front(0)
    for g in range(8):
        c1_stepf(g)
        if g + 1 < 8:
            c1_front(g + 1)
        c1_stepg(g)
    P.barrier()
    A.release(mA)
    if stop_after == "pC1":
        return finish(P, nc, [ThS, Tdbg])

    NHC = FH // 128
    wFI = A.alloc([8, 2 * FH], BF16)
    wFO = A.alloc([NHC, D], BF16)
    TwF = T("wF")
    wfi_v = wfi_d.rearrange("(k p) m -> p k m", p=128)
    for c11 in range(11):
        P.dma("pool", wFI[:, :, c11 * 512:(c11 + 1) * 512], wfi_v[:, :, c11 * 512:(c11 + 1) * 512], "ld_wF", [], [TwF])
    wfo_v = wfo_d.rearrange("(k p) m -> p k m", p=128)
    for c2 in range(2):
        P.dma("pool", wFO[:, 11 * c2:11 * (c2 + 1), :], wfo_v[:, 11 * c2:11 * (c2 + 1), :],
              "ld_wF", [], [TwF])
    NX = 4
    ht = [A.alloc([D], F32) for _ in range(NX)]
    Tht = [T(f"ht{i}") for i in range(NX)]
    hn = [A.alloc([D], BF16) for _ in range(2)]
    Thn = [T("hn0"), T("hn1")]
    st = [A.alloc([8], F32) for _ in range(NX)]
    Tst = [T(f"fst{i}") for i in range(NX)]
    n2T = A.alloc([8, 512], BF16)
    Tn2 = T("n2T")
    actT = A.alloc([NHC, 512], BF16)
    TaT = T("actT")
    fs = [A.alloc([512], F32) for _ in range(4)]
    Tfs = [T(f"fs{i}") for i in range(4)]
    Tout = T("out")
    hcnt = 0
    for g in range(8):
        t0 = g * 512
        his = []
        for half in range(2):
            pT = ps[:, 0:1024].bitcast(BF16).rearrange("p (c t) -> p c t", c=8)
            for tl in range(2):
                ti = half * 2 + tl
                hi_ = hcnt % NX
                ni = hcnt % 2
                hcnt += 1
                his.append(hi_)
                P.dma("sp", ht[hi_], h_s[t0 + ti * 128:t0 + (ti + 1) * 128, :], f"ld_h{hi_}", [ThS], [Tht[hi_]])
                jk = actT[:, 0:2, :].rearrange("p a b -> p (a b)")
                P.act(jk, ht[hi_], AF.Square, [Tht[hi_]], [TaT, Tst[hi_]], accum=st[hi_][:, 0:1])
                rstd_ops(P, st[hi_], Tst[hi_])
                P.act(hn[ni], ht[hi_], AF.Copy, [Tht[hi_], Tst[hi_]], [Thn[ni]], scale=st[hi_][:, 2:3])
                for c in range(8):
                    P.tr(pT[:, c, tl * 128:(tl + 1) * 128], hn[ni][:, c * 128:(c + 1) * 128], ident_b, [Thn[ni], Tconst], [Tb[0]])
            for c in range(8):
                P.ts("dve", n2T[:, c, half * 256:(half + 1) * 256], pT[:, c, :], A2[:, c:c + 1], B2[:, c:c + 1],
                     ALU.mult, ALU.add, [Tb[0], Tmod], [Tn2])
        for hc in range(NHC):
            ba_, bb_ = nb_(), nb_()
            for k in range(8):
                P.mm(bank(ba_), wFI[:, k, hc * 128:(hc + 1) * 128], n2T[:, k, :], k == 0, k == 7, [TwF, Tn2], [Tb[ba_]])
            for k in range(8):
                P.mm(bank(bb_), wFI[:, k, FH + hc * 128:FH + (hc + 1) * 128], n2T[:, k, :], k == 0, k == 7, [TwF, Tn2], [Tb[bb_]])
            fi_ = hc % 2
            P.act(fs[fi_], bank(ba_), AF.Silu, [Tb[ba_]], [Tfs[fi_]])
            P.tt("dve", actT[:, hc, :], bank(bb_), fs[fi_], ALU.mult, [Tb[bb_], Tfs[fi_]], [TaT])
        for ti in range(4):
            hi_ = his[ti]
            for nh in range(2):
                bo = nb_()
                for hc in range(NHC):
                    P.mm(bank(bo), actT[:, hc, ti * 128:(ti + 1) * 128], wFO[:, hc, nh * 512:(nh + 1) * 512], hc == 0,
                         hc == NHC - 1, [TaT, TwF], [Tb[bo]])
                P.tt("dve", fs[2 + nh], bank(bo), g2b[:, nh * 512:(nh + 1) * 512], ALU.mult, [Tb[bo], Tgb], [Tfs[2 + nh]])
                P.tt("pool", ht[hi_][:, nh * 512:(nh + 1) * 512], ht[hi_][:, nh * 512:(nh + 1) * 512], fs[2 + nh], ALU.add,
                     [Tht[hi_], Tfs[2 + nh]], [Tht[hi_]])
            sti = st[hi_]
            P.act(hn[ti % 2], ht[hi_], AF.Square, [Tht[hi_]], [Thn[ti % 2], Tst[hi_]], accum=sti[:, 4:5])
            P.ts("dve", sti[:, 5:6], sti[:, 4:5], 1.0 / D, EPS, ALU.mult, ALU.add, [Tst[hi_]], [Tst[hi_]])
            P.act(sti[:, 6:7], sti[:, 5:6], AF.Sqrt, [Tst[hi_]], [Tst[hi_]])
            P.emit("dve", lambda e, o=sti[:, 7:8], i=sti[:, 6:7]: e.reciprocal(out=o, in_=i), [Tst[hi_]], [Tst[hi_]])
            P.stt("dve", ht[hi_], ht[hi_], sti[:, 7:8], fgb, ALU.mult, ALU.mult, [Tht[hi_], Tst[hi_], Tgb], [Tht[hi_]])
            P.dma("sp", out_d[t0 + ti * 128:t0 + (ti + 1) * 128, :], ht[hi_], f"st_o{hi_}", [Tht[hi_]], [Tout])
    return finish(P, nc, [Tout, Tdbg])


def rstd_ops(P, st, Tst):
    P.ts("dve", st[:, 1:2], st[:, 0:1], 1.0 / D, EPS, ALU.mult, ALU.add, [Tst], [Tst])
    P.act(st[:, 3:4], st[:, 1:2], AF.Sqrt, [Tst], [Tst])
    P.emit("dve", lambda e: e.reciprocal(out=st[:, 2:3], in_=st[:, 3:4]), [Tst], [Tst])


def finish(P, nc, tiles=()):
    P.barrier()
    nc._marks = P.marks
    P.final_wait("sp", list(tiles))
    P.build()
    return nc


def _pcol(v, nchunk):
    return np.ascontiguousarray(v.reshape(nchunk, 128).T)


def host_shared(inp):
    f32 = np.float32
    sh = {}
    w_in = inp["w_in"][0]
    idx = np.concatenate([np.arange(16, 32), np.arange(0, 16), np.arange(48, 64), np.arange(32, 48)])
    swap = np.concatenate([h * 64 + idx for h in range(NH)])
    sh["w_in_ext"] = np.ascontiguousarray(np.concatenate([w_in[:, :2048], w_in[:, swap], w_in[:, 512 + swap]], axis=1))
    sh["w_gate"] = np.ascontiguousarray(w_in[:, 2048:4096])
    sh["w_mod"] = np.ascontiguousarray(inp["w_mod"][0])
    sh["b_mod"] = _pcol(inp["b_mod"][0], 48)
    sh["gA"] = _pcol(inp["attn_norm_g"][0], 8)
    sh["gF"] = _pcol(inp["ffn_norm_g"][0], 8)
    sh["gFin"] = _pcol(inp["final_norm_g"], 8)
    t = np.arange(L)
    inv = (10000.0 ** (-np.arange(16, dtype=f32) / 16)).astype(f32)
    cos = np.zeros((64, L), f32)
    sin = np.zeros((64, L), f32)
    for half, pos in ((0, t // GW), (1, t % GW)):
        ang = pos.astype(f32)[None, :] * inv[:, None]
        c_, s_ = np.cos(ang).astype(f32), np.sin(ang).astype(f32)
        cos[half * 32:half * 32 + 16] = c_
        cos[half * 32 + 16:half * 32 + 32] = c_
        sin[half * 32:half * 32 + 16] = -s_
        sin[half * 32 + 16:half * 32 + 32] = s_
    sh["rope_cos"] = np.ascontiguousarray(np.concatenate([cos, cos], 0))
    sh["rope_sin"] = np.ascontiguousarray(np.concatenate([sin, sin], 0))
    rpb = inp["rel_pos_bias"][0]
    q = np.arange(64)[:, None]
    k = np.arange(64)[None, :]
    dc = np.clip(k - q + 15, 0, 30)
    tb = np.zeros((128, 4, 15, 64), f32)
    for j in range(4):
        for hh in range(2):
            tb[hh * 64:(hh + 1) * 64, j] = np.transpose(rpb[2 * j + hh][:, dc], (1, 0, 2))
    sh["bias_tb"] = tb
    cs = np.clip(q - 8, 0, 48)
    inw = (k >= cs) & (k < cs + 16)
    mk = np.where(inw, 0.0, -1e30).astype(f32)
    sh["bias_mask"] = np.ascontiguousarray(np.concatenate([mk, mk], 0))
    sh["lam_re"] = np.ascontiguousarray(inp["ssm_lambda_re"][0].reshape(32, 128))
    sh["lam_im"] = np.ascontiguousarray(inp["ssm_lambda_im"][0].reshape(32, 128))
    sh["log_dt"] = np.ascontiguousarray(inp["ssm_log_dt"][0].reshape(32, 2))
    sh["b_re"] = np.ascontiguousarray(inp["ssm_b_re"][0])
    sh["b_im"] = np.ascontiguousarray(inp["ssm_b_im"][0])
    sh["c_re"] = np.ascontiguousarray(inp["ssm_c_re"][0])
    sh["c_im"] = np.ascontiguousarray(inp["ssm_c_im"][0])
    sh["d_skip"] = _pcol(inp["ssm_d"][0], 4)
    sh["w_glu"] = np.ascontiguousarray(inp["w_glu"][0])
    sh["w_ba"] = np.ascontiguousarray(inp["w_branch_attn"][0])
    sh["w_bs"] = np.ascontiguousarray(inp["w_branch_ssm"][0])
    sh["w_out"] = np.ascontiguousarray(inp["w_out"][0])
    sh["w_ffn_in"] = np.ascontiguousarray(inp["w_ffn_in"][0])
    sh["w_ffn_out"] = np.ascontiguousarray(inp["w_ffn_out"][0])
    return sh


def host_core(inp, b):
    cv = np.stack([inp["c"][b], inp["c_ctx"]], axis=1)
    return {
        "x": np.ascontiguousarray(inp["x"][b]),
        "ctx": np.ascontiguousarray(inp["ctx"][b]),
        "cvec": np.ascontiguousarray(cv.reshape(8, 128, 2).transpose(1, 0, 2)),
    }


_NC_CACHE = {}


def kernel(**inputs):
    inp = {k: np.asarray(v) for k, v in inputs.items()}
    if "nc" not in _NC_CACHE:
        _NC_CACHE["nc"] = build_program()
    nc = _NC_CACHE["nc"]
    sh = host_shared(inp)
    in_maps = [{**sh, **host_core(inp, b)} for b in range(8)]
    res = run_bass_kernel_spmd(nc, in_maps, core_ids=list(range(8)))
    return np.stack([np.asarray(r["out"]) for r in res.results], axis=0).astype(np.float32)
```

```python
import math
from contextlib import ExitStack

import numpy as np
import concourse.bass as bass
import concourse.mybir as mybir
from concourse.bass_utils import run_bass_kernel_spmd

F32 = mybir.dt.float32
BF16 = mybir.dt.bfloat16
U8 = mybir.dt.uint8
I32 = mybir.dt.int32
AF = mybir.ActivationFunctionType
ALU = mybir.AluOpType
AX = mybir.AxisListType
DTSZ = {F32: 4, BF16: 2, U8: 1, I32: 4}

L = 4096
D = 1024
LC = 256
NTOK = L + LC
GW = 64
NH = 8
HD = 64
AW = 512
SW = 512
NG = 32
NP = 64
FH = 2816
EPS = 1e-6
TCH = 16
NCH = NTOK // TCH
UEXT = L + 2 * LC

ENGS = ("pe", "act", "dve", "pool", "sp")


class T:
    __slots__ = ("name", "w", "r")

    def __init__(self, name):
        self.name = name
        self.w = None
        self.r = {}


class Prog:
    def __init__(self, nc):
        self.nc = nc
        self.es = ExitStack()
        self.ops = {e: [] for e in ENGS}
        self.cnt = {}
        self.sems = {}
        self.seen = {e: {} for e in ENGS}
        self.marks = []
        for e in ENGS:
            self._sem("E_" + e)

    def _sem(self, key):
        if key not in self.sems:
            self.sems[key] = self.es.enter_context(self.nc.semaphore(key))
            self.cnt[key] = 0
        return self.sems[key]

    def sbuf(self, name, shape, dt):
        return self.es.enter_context(self.nc.sbuf_tensor(name, list(shape), dt))

    def psum(self, name, shape, dt):
        return self.es.enter_context(self.nc.psum_tensor(name, list(shape), dt))

    def emit(self, eng, fn, reads=(), writes=(), dma_sem=None):
        own = "E_" + eng
        deps = {}

        def add(ev, is_raw):
            if ev is None:
                return
            k, v = ev
            if k == own and (eng == "pe" or not is_raw):
                return
            if deps.get(k, 0) < v:
                deps[k] = v

        for t in reads:
            add(t.w, True)
        for t in writes:
            add(t.w, False)
            for k, v in t.r.items():
                add((k, v), False)
        waits = []
        seen = self.seen[eng]
        for k, v in deps.items():
            if seen.get(k, 0) < v:
                seen[k] = v
                waits.append((k, v))
        if dma_sem is not None:
            self._sem(dma_sem)
            self.cnt[dma_sem] += 16
            ev = (dma_sem, self.cnt[dma_sem])
            inc = (dma_sem, 16)
        else:
            self.cnt[own] += 1
            ev = (own, self.cnt[own])
            inc = (own, 1)
        self.ops[eng].append((waits, fn, inc))
        for t in reads:
            if t.r.get(ev[0], 0) < ev[1]:
                t.r[ev[0]] = ev[1]
        for t in writes:
            t.w = ev
            t.r = {}
        return ev

    def final_wait(self, eng, tiles):
        waits = []
        for t in tiles:
            if t.w is not None:
                waits.append(t.w)
            for k, v in t.r.items():
                waits.append((k, v))
        self.ops[eng].append((waits, None, None))

    def barrier(self):
        snap = dict(self.cnt)
        self.marks.append(snap)
        for e in ENGS:
            waits = []
            for k, v in snap.items():
                if v > 0 and k != "E_" + e and self.seen[e].get(k, 0) < v:
                    self.seen[e][k] = v
                    waits.append((k, v))
            if waits:
                self.ops[e].append((waits, None, None))

    def build(self):
        nc = self.nc
        needed = {}
        for e in ENGS:
            for waits, fn, inc in self.ops[e]:
                for k, v in waits:
                    if k.startswith("E_"):
                        needed.setdefault(k, set()).add(v)
        rank = {k: {v: i + 1 for i, v in enumerate(sorted(vs))} for k, vs in needed.items()}
        with nc.Block() as block:
            def run(e, handle):
                seq = 0
                own = "E_" + e
                myrank = rank.get(own, {})
                for waits, fn, inc in self.ops[e]:
                    for k, v in waits:
                        if k.startswith("E_"):
                            handle.wait_ge(self.sems[k], rank[k][v])
                        else:
                            handle.wait_ge(self.sems[k], v)
                    if fn is not None:
                        ins = fn(handle)
                        if inc[0] == own:
                            seq += 1
                            if seq in myrank:
                                ins.then_inc(self.sems[own], 1)
                        else:
                            ins.then_inc(self.sems[inc[0]], inc[1])

            @block.tensor
            def _(h):
                run("pe", h)

            @block.scalar
            def _(h):
                run("act", h)

            @block.vector
            def _(h):
                run("dve", h)

            @block.gpsimd
            def _(h):
                run("pool", h)

            @block.sync
            def _(h):
                run("sp", h)
        self.es.close()

    def mm(self, out, lhsT, rhs, start, stop, R, W, tp=None):
        if tp is None:
            self.emit("pe", lambda e: e.matmul(out, lhsT=lhsT, rhs=rhs, start=start, stop=stop), R, W)
        else:
            self.emit("pe", lambda e: e.matmul(out, lhsT=lhsT, rhs=rhs, start=start, stop=stop, tile_position=tp), R, W)

    def tr(self, out, in_, ident, R, W):
        self.emit("pe", lambda e: e.transpose(out=out, in_=in_, identity=ident), R, W)

    def act(self, out, in_, func, R, W, bias=None, scale=None, accum=None):
        kw = {}
        if bias is not None:
            kw["bias"] = bias
        if scale is not None:
            kw["scale"] = scale
        if accum is not None:
            kw["accum_out"] = accum
        self.emit("act", lambda e: e.activation(out=out, in_=in_, func=func, **kw), R, W)

    def tt(self, eng, out, in0, in1, op, R, W):
        self.emit(eng, lambda e: e.tensor_tensor(out=out, in0=in0, in1=in1, op=op), R, W)

    def ts(self, eng, out, in0, s1, s2, op0, op1, R, W, accum=None):
        if s2 is None:
            self.emit(eng, lambda e: e.tensor_scalar(out=out, in0=in0, scalar1=s1, scalar2=None, op0=op0), R, W)
        elif accum is None:
            self.emit(eng, lambda e: e.tensor_scalar(out=out, in0=in0, scalar1=s1, scalar2=s2, op0=op0, op1=op1), R, W)
        else:
            self.emit(eng, lambda e: e.tensor_scalar(out=out, in0=in0, scalar1=s1, scalar2=s2, op0=op0, op1=op1,
                                                     accum_out=accum), R, W)

    def stt(self, eng, out, in0, scalar, in1, op0, op1, R, W):
        self.emit(eng, lambda e: e.scalar_tensor_tensor(out=out, in0=in0, scalar=scalar, in1=in1, op0=op0, op1=op1), R, W)

    def copy(self, eng, out, in_, R, W):
        if eng == "act":
            self.emit("act", lambda e: e.activation(out=out, in_=in_, func=AF.Copy), R, W)
        else:
            self.emit(eng, lambda e: e.tensor_copy(out=out, in_=in_), R, W)

    def memset(self, eng, ap, val, W):
        self.emit(eng, lambda e: e.memset(ap, val), (), W)

    def dma(self, eng, out, in_, sem, R, W):
        self.emit(eng, lambda e: e.dma_start(out=out, in_=in_), R, W, dma_sem=sem)


class Arena:
    def __init__(self, P, nbytes):
        self.t = P.sbuf("arena", [128, nbytes], U8)
        self.n = nbytes
        self.off = 0

    def alloc(self, shape, dt):
        n = int(np.prod(shape)) * DTSZ[dt]
        n_al = (n + 63) // 64 * 64
        assert self.off + n_al <= self.n, f"arena overflow {self.off}+{n_al}>{self.n}"
        v = self.t[:, self.off:self.off + n].bitcast(dt)
        self.off += n_al
        if len(shape) == 2:
            v = v.rearrange("p (a b) -> p a b", a=shape[0])
        elif len(shape) == 3:
            v = v.rearrange("p (a b c) -> p a b c", a=shape[0], b=shape[1])
        elif len(shape) == 4:
            v = v.rearrange("p (a b c d) -> p a b c d", a=shape[0], b=shape[1], c=shape[2])
        return v

    def mark(self):
        return self.off

    def release(self, m):
        self.off = m


ARENA_BYTES = 207 * 1024


def build_program(debug=False, stop_after=None):
    nc = bass.Bass("TRN2", target_bir_lowering=False)
    P = Prog(nc)

    def din(name, shape, dt=F32):
        return nc.dram_tensor(name, list(shape), dt, kind="ExternalInput").ap()

    skind = "ExternalOutput" if debug else "Internal"

    def dscr(name, shape, dt):
        return nc.dram_tensor(name, list(shape), dt, kind=skind).ap()

    x_d = din("x", [L, D])
    ctx_d = din("ctx", [LC, D])
    cvec_d = din("cvec", [128, 8, 2])
    wmod_d = din("w_mod", [D, 6 * D])
    bmod_d = din("b_mod", [128, 48])
    gA_d = din("gA", [128, 8])
    gF_d = din("gF", [128, 8])
    gFin_d = din("gFin", [128, 8])
    win_d = din("w_in_ext", [D, 3072])
    wgate_d = din("w_gate", [D, 2048])
    cos_d = din("rope_cos", [128, L])
    sin_d = din("rope_sin", [128, L])
    tb_d = din("bias_tb", [128, 4, 15, 64])
    mk_d = din("bias_mask", [128, 64])
    lre_d = din("lam_re", [32, 128])
    lim_d = din("lam_im", [32, 128])
    ldt_d = din("log_dt", [32, 2])
    bre_d = din("b_re", [2, 32, 64, 16])
    bim_d = din("b_im", [2, 32, 64, 16])
    cre_d = din("c_re", [2, 32, 16, 64])
    cim_d = din("c_im", [2, 32, 16, 64])
    dsk_d = din("d_skip", [128, 4])
    wglu_d = din("w_glu", [SW, 2 * SW])
    wba_d = din("w_ba", [AW, D])
    wbs_d = din("w_bs", [SW, D])
    wout_d = din("w_out", [D, D])
    wfi_d = din("w_ffn_in", [D, 2 * FH])
    wfo_d = din("w_ffn_out", [FH, D])
    out_d = nc.dram_tensor("out", [L, D], F32, kind="ExternalOutput").ap()

    qT_s = dscr("qT_s", [4, 128, L], BF16)
    uT_s = dscr("uT_s", [4, 128, UEXT], BF16)
    aT_s = dscr("aT_s", [4, 128, L], BF16)
    h_s = dscr("h_s", [L, D], F32)
    y_o = dscr("y_o", [2, 4, 128, L], BF16)
    if debug:
        kT_o = dscr("kT_o", [4, 128, NTOK], BF16)
        v_o = dscr("v_o", [34, 128, 512], BF16)
        mod_o = dscr("mod_o", [128, 96], F32)
        gb_o = dscr("gb_o", [128, 3 * D], F32)

    A = Arena(P, ARENA_BYTES)
    ps = P.psum("ps", [128, 4096], F32)

    def bank(b, n=512, off=0):
        return ps[:, b * 512 + off:b * 512 + off + n]

    def bank_bf(b):
        return ps[:, b * 512:(b + 1) * 512].bitcast(BF16)

    Tb = [T(f"bank{b}") for b in range(8)]

    ident_f = A.alloc([128], F32)
    ident_b = A.alloc([128], BF16)
    ones_f = A.alloc([128], F32)
    modv = A.alloc([48, 2], F32)
    A1l = A.alloc([8], F32)
    A1c = A.alloc([8], F32)
    A2 = A.alloc([8], F32)
    gA = A.alloc([8], F32)
    gF = A.alloc([8], F32)
    gFin = A.alloc([8], F32)
    g1b = A.alloc([D], F32)
    g2b = A.alloc([D], F32)
    fgb = A.alloc([D], F32)
    Tconst = T("const")
    Tdbg = T("dbg")
    Tmod = T("mod")
    Tgb = T("gb")

    P.memset("pool", ident_f, 1.0, [Tconst])
    P.emit("pool", lambda e: e.affine_select(out=ident_f, in_=ident_f, pattern=[[-1, 128]], compare_op=ALU.is_equal,
                                             fill=0.0, base=0, channel_multiplier=1), [Tconst], [Tconst])
    P.copy("pool", ident_b, ident_f, [Tconst], [Tconst])
    P.memset("pool", ones_f, 1.0, [Tconst])

    m0 = A.mark()
    cv = A.alloc([8, 2], F32)
    cvb = A.alloc([8, 2], BF16)
    bmod = A.alloc([48], F32)
    wmb = [A.alloc([8, 1024], BF16) for _ in range(2)]
    Tcv, Tbm = T("cv"), T("bmod")
    Twm = [T("wm0"), T("wm1")]
    P.dma("sp", cv, cvec_d, "ld_cv", [], [Tcv])
    P.dma("sp", bmod, bmod_d, "ld_cv", [], [Tbm])
    P.dma("sp", gA, gA_d, "ld_cv", [], [Tmod])
    P.dma("sp", gF, gF_d, "ld_cv", [], [Tmod])
    P.dma("sp", gFin, gFin_d, "ld_cv", [], [Tmod])
    P.act(cvb, cv, AF.Silu, [Tcv], [Tcv])
    wm_v = wmod_d.rearrange("(k p) m -> p k m", p=128)
    for blk in range(6):
        bi = blk % 2
        P.dma("pool", wmb[bi], wm_v[:, :, blk * 1024:(blk + 1) * 1024], f"ld_wm{bi}", [], [Twm[bi]])
        for ml in range(8):
            m = blk * 8 + ml
            for k in range(8):
                P.mm(bank(0, 2, m * 2), wmb[bi][:, k, ml * 128:(ml + 1) * 128], cvb[:, k, :], k == 0, k == 7,
                     [Twm[bi], Tcv], [Tb[0]])
    P.tt("dve", modv, bank(0, 96).rearrange("p (m j) -> p m j", j=2), bmod.unsqueeze(2).to_broadcast([128, 48, 2]),
         ALU.add, [Tb[0], Tbm], [Tmod])
    for (dst, j) in ((A1l, 0), (A1c, 1)):
        P.ts("dve", dst, modv[:, 8:16, j], 1.0, None, ALU.add, None, [Tmod], [Tmod])
        P.tt("dve", dst, dst, gA, ALU.mult, [Tmod], [Tmod])
    P.ts("dve", A2, modv[:, 32:40, 0], 1.0, None, ALU.add, None, [Tmod], [Tmod])
    P.tt("dve", A2, A2, gF, ALU.mult, [Tmod], [Tmod])
    B1l = modv[:, 0:8, 0]
    B1c = modv[:, 0:8, 1]
    B2 = modv[:, 24:32, 0]
    diag = [A.alloc([128], F32) for _ in range(2)]
    Tdg = [T("dg0"), T("dg1")]
    n = 0
    for (dst, vec) in ((g1b, modv[:, 16:24, 0]), (g2b, modv[:, 40:48, 0]), (fgb, gFin)):
        for k in range(8):
            di = n % 2
            bk = 1 + (n % 2)
            P.ts("dve", diag[di], ident_f, vec[:, k:k + 1], None, ALU.mult, None, [Tmod, Tconst], [Tdg[di]])
            P.mm(bank(bk, 128), ones_f, diag[di], True, True, [Tconst, Tdg[di]], [Tb[bk]])
            P.copy("act", dst[:, k * 128:(k + 1) * 128], bank(bk, 128), [Tb[bk]], [Tgb])
            n += 1
    if debug:
        P.dma("sp", mod_o, modv.rearrange("p m j -> p (m j)"), "dbg", [Tmod], [Tdbg])
        P.dma("sp", gb_o[:, 0:D], g1b, "dbg", [Tgb], [Tdbg])
        P.dma("sp", gb_o[:, D:2 * D], g2b, "dbg", [Tgb], [Tdbg])
        P.dma("sp", gb_o[:, 2 * D:3 * D], fgb, "dbg", [Tgb], [Tdbg])
    P.barrier()
    A.release(m0)
    if stop_after == "p0":
        return finish(P, nc, [Tdbg])

    mA = A.mark()
    kT = A.alloc([4, NTOK], BF16)
    vtok = A.alloc([34, 512], BF16)
    TkT = [T(f"kT{g}") for g in range(9)]
    Tv = [T(f"v{g}") for g in range(9)]
    m1 = A.mark()
    wA = A.alloc([8, 3072], BF16)
    TwA = T("wA")
    win_v = win_d.rearrange("(k p) m -> p k m", p=128)
    for c6 in range(6):
        P.dma("pool", wA[:, :, c6 * 512:(c6 + 1) * 512], win_v[:, :, c6 * 512:(c6 + 1) * 512], "ld_wA", [], [TwA])
    xt = [A.alloc([D], F32) for _ in range(3)]
    Txt = [T(f"xt{i}") for i in range(3)]
    junk = A.alloc([D], BF16)
    Tjunk = T("junk")
    xn = [A.alloc([D], BF16) for _ in range(2)]
    Txn = [T(f"xn{i}") for i in range(2)]
    st = [A.alloc([4], F32) for _ in range(3)]
    Tst = [T(f"st{i}") for i in range(3)]
    nT = [A.alloc([8, 512], BF16) for _ in range(2)]
    TnT = [T("nT0"), T("nT1")]
    ropec = [A.alloc([512], F32) for _ in range(2)]
    ropes = [A.alloc([512], F32) for _ in range(2)]
    Trope = [T("rope0"), T("rope1")]
    rt = [A.alloc([512], F32) for _ in range(4)]
    Trt = [T(f"rt{i}") for i in range(4)]
    qst = [A.alloc([4, 512], BF16) for _ in range(2)]
    Tqst = [T("qst0"), T("qst1")]
    ust = [A.alloc([4, 512], BF16) for _ in range(2)]
    Tust = [T("ust0"), T("ust1")]
    TqS, TuS = T("qT_s"), T("uT_s")

    groups = [("ctx", 0, LC)] + [("lat", g * 512, 512) for g in range(8)]
    tile_no = 0
    mmbank = 0
    for gi, (kind, t0, ntok) in enumerate(groups):
        nb = gi % 2
        ntile = ntok // 128
        src = ctx_d if kind == "ctx" else x_d
        Asc, Bsc = (A1c, B1c) if kind == "ctx" else (A1l, B1l)
        if kind == "lat":
            P.dma("act", ropec[nb], cos_d[:, t0:t0 + 512], f"ld_rope{nb}", [], [Trope[nb]])
            P.dma("act", ropes[nb], sin_d[:, t0:t0 + 512], f"ld_rope{nb}", [], [Trope[nb]])
        for half in range((ntile + 1) // 2):
            tb_ = 0 if (gi * 2 + half) % 2 == 0 else 2
            pT = ps[:, tb_ * 512:(tb_ + 2) * 512].bitcast(BF16).rearrange("p (c t) -> p c t", c=8)
            Tp = Tb[tb_]
            for tl in range(2):
                ti = half * 2 + tl
                xi = tile_no % 3
                si = tile_no % 3
                ni = tile_no % 2
                tile_no += 1
                P.dma("sp", xt[xi], src[t0 + ti * 128:t0 + (ti + 1) * 128, :], f"ld_x{xi}", [], [Txt[xi]])
                P.act(junk, xt[xi], AF.Square, [Txt[xi]], [Tjunk, Tst[si]], accum=st[si][:, 0:1])
                rstd_ops(P, st[si], Tst[si])
                P.act(xn[ni], xt[xi], AF.Copy, [Txt[xi], Tst[si]], [Txn[ni]], scale=st[si][:, 2:3])
                for c in range(8):
                    P.tr(pT[:, c, tl * 128:(tl + 1) * 128], xn[ni][:, c * 128:(c + 1) * 128], ident_b,
                         [Txn[ni], Tconst], [Tp])
            for c in range(8):
                P.ts("dve", nT[nb][:, c, half * 256:(half + 1) * 256], pT[:, c, :], Asc[:, c:c + 1], Bsc[:, c:c + 1],
                     ALU.mult, ALU.add, [Tp, Tmod], [TnT[nb]])
        cols = slice(0, ntok)

        def proj(mcol, bk):
            for k in range(8):
                P.mm(bank(bk, ntok), wA[:, k, mcol:mcol + 128], nT[nb][:, k, cols], k == 0, k == 7,
                     [TwA, TnT[nb]], [Tb[bk]])

        kcol0 = L if kind == "ctx" else t0
        for j in range(4):
            if kind == "ctx":
                bk = 4 + mmbank % 4
                mmbank += 1
                proj(512 + j * 128, bk)
                P.copy("act", kT[:, j, kcol0:kcol0 + ntok], bank(bk, ntok), [Tb[bk]], [TkT[gi]])
            else:
                for (base, swb, dst, Td, isq) in ((0, 2048, qst[nb], Tqst[nb], True), (512, 2560, None, TkT[gi], False)):
                    bk = 4 + mmbank % 4
                    bk2 = 4 + (mmbank + 1) % 4
                    mmbank += 2
                    proj(base + j * 128, bk)
                    proj(swb + j * 128, bk2)
                    rsel = (mmbank // 2) % 2 * 2
                    r0, r1 = rt[rsel], rt[rsel + 1]
                    Tr0, Tr1 = Trt[rsel], Trt[rsel + 1]
                    P.tt("dve", r0, bank(bk), ropec[nb], ALU.mult, [Tb[bk], Trope[nb]], [Tr0])
                    P.tt("dve", r1, bank(bk2), ropes[nb], ALU.mult, [Tb[bk2], Trope[nb]], [Tr1])
                    o = qst[nb][:, j, :] if isq else kT[:, j, t0:t0 + 512]
                    P.tt("pool", o, r0, r1, ALU.add, [Tr0, Tr1], [Td])
            bk = 4 + mmbank % 4
            mmbank += 1
            proj(1536 + j * 128, bk)
            P.copy("act", ust[nb][:, j, cols], bank(bk, ntok), [Tb[bk]], [Tust[nb]])
        for ti in range(ntile):
            bk = 4 + mmbank % 4
            mmbank += 1
            for k in range(8):
                P.mm(bank(bk), nT[nb][:, k, ti * 128:(ti + 1) * 128], wA[:, k, 1024:1536], k == 0, k == 7,
                     [TwA, TnT[nb]], [Tb[bk]])
            vt = (32 + ti) if kind == "ctx" else (t0 // 128 + ti)
            P.copy("act", vtok[:, vt, :], bank(bk), [Tb[bk]], [Tv[gi]])
        if kind == "lat":
            P.dma("sp", qT_s[:, :, t0:t0 + 512].rearrange("c p t -> p c t"), qst[nb], f"st_q{nb}", [Tqst[nb]], [TqS])
            P.dma("sp", uT_s[:, :, LC + t0:LC + t0 + 512].rearrange("c p t -> p c t"), ust[nb], f"st_u{nb}",
                  [Tust[nb]], [TuS])
        else:
            P.dma("sp", uT_s[:, :, 0:LC].rearrange("c p t -> p c t"), ust[nb][:, :, 0:LC], f"st_u{nb}", [Tust[nb]], [TuS])
            P.dma("sp", uT_s[:, :, LC + L:UEXT].rearrange("c p t -> p c t"), ust[nb][:, :, 0:LC], f"st_u{nb}",
                  [Tust[nb]], [TuS])
    if debug:
        P.dma("sp", kT_o.rearrange("c p t -> p c t"), kT, "dbg", TkT, [Tdbg])
        P.dma("sp", v_o.rearrange("n p f -> p n f"), vtok, "dbg", Tv, [Tdbg])
    P.barrier()
    A.release(m1)
    if stop_after == "pA":
        return finish(P, nc, [TqS, TuS, Tdbg])

    PI = math.pi
    s_ar = A.alloc([32], F32)
    s_ai = A.alloc([32], F32)
    s_nai = A.alloc([32], F32)
    a16r = A.alloc([32], F32)
    a16i = A.alloc([32], F32)
    Wt = A.alloc([2, 4, 2, 128], BF16)
    CwT = A.alloc([32, 2, 32], BF16)
    Eb = A.alloc([2, NCH, 32], F32)
    hst = [[A.alloc([32], F32) for _ in range(2)] for _ in range(2)]
    ct = [A.alloc([32], F32) for _ in range(4)]
    Tprm, TWt, TCw, TE = T("s5prm"), T("Wt"), T("CwT"), T("E")
    TyT = [[T(f"yT{d}{c}") for c in range(4)] for d in range(2)]
    mS1 = A.mark()
    L32 = A.alloc([3, 128], F32)
    ldt = A.alloc([2], F32)
    prm = A.alloc([3, 32], F32)
    w32 = [A.alloc([32], F32) for _ in range(12)]
    negpi = A.alloc([1], F32)
    qint = A.alloc([32], I32)
    breT = A.alloc([32, 16], F32)
    bimT = A.alloc([32, 16], F32)
    bbr = A.alloc([32, 16], F32)
    bbi = A.alloc([32, 16], F32)
    btmp = A.alloc([32, 16], F32)
    srcw = [A.alloc([128], F32) for _ in range(8)]
    Cblk1 = A.alloc([32, 128], F32)
    Cblk = [Cblk1, Cblk1]
    TCb1 = T("Cb")
    TL32, Tbb, Tsrc, TCb = T("L32"), T("bb"), [T(f"srcw{i}") for i in range(8)], [TCb1, TCb1]
    P.dma("sp", L32[0:32, 0, :], lre_d, "ld_s5", [], [TL32])
    P.dma("sp", L32[0:32, 1, :], lim_d, "ld_s5", [], [TL32])
    P.dma("sp", ldt[0:32, :], ldt_d, "ld_s5", [], [TL32])
    P.dma("sp", breT, bre_d.rearrange("d (j g) p c -> (g p) (d j) c", g=2), "ld_s5b", [], [Tbb])
    P.dma("sp", bimT, bim_d.rearrange("d (j g) p c -> (g p) (d j) c", g=2), "ld_s5b", [], [Tbb])
    def load_cblk(ri):
        cd = (cre_d, cim_d)[ri]
        if ri == 0:
            P.memset("pool", Cblk1, 0.0, [TCb1])
        cv_ = cd.rearrange("d (j g) h p -> g h (d j) p", g=2)
        for g2 in range(2):
            P.dma("sp", Cblk1[g2 * 16:(g2 + 1) * 16, :, g2 * 64:(g2 + 1) * 64], cv_[g2], "ld_s5c", [], [TCb1])

    load_cblk(0)
    P.memset("pool", negpi, -PI, [Tprm])
    for i in range(8):
        P.memset("pool", srcw[i], 0.0, [Tsrc[i]])
    P.copy("dve", L32[0:32, 2, :].rearrange("p (g q) -> p g q", g=2), ldt[0:32, :].unsqueeze(2).to_broadcast([32, 2, 64]),
           [TL32], [TL32])
    for i in range(3):
        P.mm(bank(0, 32, i * 32), L32[0:32, i, :], ident_f[0:32, 0:32], True, True, [TL32, Tconst], [Tb[0]])
    P.copy("dve", prm.rearrange("p a b -> p (a b)"), bank(0, 96), [Tb[0]], [Tprm])
    lre, lim, ldtT = prm[:, 0, :], prm[:, 1, :], prm[:, 2, :]
    dt_, tmp_, mag, ang, rs_, sin_, cos_, den, rden, nr, fr, fi = w32
    RW = ([Tprm], [Tprm])
    P.act(dt_, ldtT, AF.Exp, *RW)
    P.tt("dve", tmp_, lre, dt_, ALU.mult, *RW)
    P.act(mag, tmp_, AF.Exp, *RW)
    P.tt("dve", ang, lim, dt_, ALU.mult, *RW)
    def sin_of(dst, shift):
        P.ts("dve", rs_, ang, shift, 1.0 / (2 * PI), ALU.add, ALU.mult, *RW)
        P.copy("dve", qint, rs_, *RW)
        P.copy("dve", rden, qint, *RW)
        P.ts("dve", rs_, ang, shift, None, ALU.add, None, *RW)
        P.stt("dve", rs_, rden, -2 * PI, rs_, ALU.mult, ALU.add, *RW)
        P.ts("dve", rden, rs_, PI, None, ALU.is_gt, None, *RW)
        P.stt("dve", rs_, rden, -2 * PI, rs_, ALU.mult, ALU.add, *RW)
        P.ts("dve", rden, rs_, -PI, None, ALU.is_lt, None, *RW)
        P.stt("dve", rs_, rden, 2 * PI, rs_, ALU.mult, ALU.add, *RW)
        P.act(dst, rs_, AF.Sin, *RW)

    sin_of(sin_, 0.0)
    sin_of(cos_, 0.5 * PI)
    P.tt("dve", s_ar, mag, cos_, ALU.mult, *RW)
    P.tt("dve", s_ai, mag, sin_, ALU.mult, *RW)
    P.ts("dve", s_nai, s_ai, -1.0, None, ALU.mult, None, *RW)
    P.tt("dve", den, lre, lre, ALU.mult, *RW)
    P.tt("dve", tmp_, lim, lim, ALU.mult, *RW)
    P.tt("dve", den, den, tmp_, ALU.add, *RW)
    P.emit("dve", lambda e: e.reciprocal(out=rden, in_=den), *RW)
    P.ts("dve", nr, s_ar, -1.0, None, ALU.add, None, *RW)
    P.tt("dve", fr, nr, lre, ALU.mult, *RW)
    P.tt("dve", tmp_, s_ai, lim, ALU.mult, *RW)
    P.tt("dve", fr, fr, tmp_, ALU.add, *RW)
    P.tt("dve", fr, fr, rden, ALU.mult, *RW)
    P.tt("dve", fi, s_ai, lre, ALU.mult, *RW)
    P.tt("dve", tmp_, nr, lim, ALU.mult, *RW)
    P.tt("dve", fi, fi, tmp_, ALU.subtract, *RW)
    P.tt("dve", fi, fi, rden, ALU.mult, *RW)
    P.copy("dve", a16r, s_ar, *RW)
    P.copy("dve", a16i, s_ai, *RW)
    for _ in range(4):
        P.tt("dve", tmp_, a16r, a16r, ALU.mult, *RW)
        P.tt("dve", den, a16i, a16i, ALU.mult, *RW)
        P.tt("dve", nr, a16r, a16i, ALU.mult, *RW)
        P.tt("dve", a16r, tmp_, den, ALU.subtract, *RW)
        P.ts("dve", a16i, nr, 2.0, None, ALU.mult, None, *RW)
    frb = fr.unsqueeze(2).to_broadcast([128, 32, 16])
    fib = fi.unsqueeze(2).to_broadcast([128, 32, 16])
    Rb = [Tprm, Tbb]
    P.tt("dve", bbr, breT, frb, ALU.mult, Rb, [Tbb])
    P.tt("dve", btmp, bimT, fib, ALU.mult, Rb, [Tbb])
    P.tt("dve", bbr, bbr, btmp, ALU.subtract, Rb, [Tbb])
    P.tt("dve", bbi, bimT, frb, ALU.mult, Rb, [Tbb])
    P.tt("dve", btmp, breT, fib, ALU.mult, Rb, [Tbb])
    P.tt("dve", bbi, bbi, btmp, ALU.add, Rb, [Tbb])
    n = 0
    for d in range(2):
        for j in range(16):
            dj = d * 16 + j
            cj, jm = j // 4, j % 4
            for ri, bb_ in enumerate((bbr, bbi)):
                sw = srcw[ri * 4 + jm]
                Ts_ = Tsrc[ri * 4 + jm]
                for g2 in range(2):
                    P.copy("dve", sw[g2 * 64:(g2 + 1) * 64, 32 * jm + 16 * g2:32 * jm + 16 * g2 + 16],
                           bb_[g2 * 64:(g2 + 1) * 64, dj, :], [Tbb], [Ts_])
                bk = 1 + n % 2
                n += 1
                P.mm(bank(bk, 128), sw, ident_f, True, True, [Ts_, Tconst], [Tb[bk]])
                P.copy("act", Wt[32 * jm:32 * jm + 32, d, cj, ri, :], bank(bk, 128)[32 * jm:32 * jm + 32, :], [Tb[bk]], [TWt])
    for ri in range(2):
        if ri == 1:
            load_cblk(1)
        for dj in range(32):
            bk = 1 + n % 2
            n += 1
            P.mm(bank(bk, 32), Cblk1[0:32, dj, :], ident_f[0:32, 0:32], True, True, [TCb1, Tconst], [Tb[bk]])
            if ri == 0:
                P.copy("act", CwT[:, dj, 0, :], bank(bk, 32), [Tb[bk]], [TCw])
            else:
                P.act(CwT[:, dj, 1, :], bank(bk, 32), AF.Copy, [Tb[bk]], [TCw], scale=-1.0)
    P.barrier()
    A.release(mS1)

    mP = A.mark()

    def useq(u, jm, d, i):
        if d == 0:
            return u[32 * jm:32 * jm + 32, i:NTOK:TCH]
        return u[32 * jm:32 * jm + 32, UEXT - 1 - i:LC - 1:-TCH]

    def scan_pass(phase2, yT=None):
        uch = [A.alloc([UEXT], BF16) for _ in range(2)]
        Tuch = [T("uch0"), T("uch1")]
        Hs = [[A.alloc([NCH], F32) for _ in range(2)] for _ in range(4)]
        THs = [T(f"H{i}") for i in range(4)]
        tm = [[A.alloc([NCH], F32) for _ in range(2)] for _ in range(2)]
        Ttm = [T("tm0"), T("tm1")]
        Hb = [[[A.alloc([NCH], BF16) for _ in range(2)] for _ in range(4)] for _ in range(2)]
        THb = [[T(f"Hb{b}{i}") for i in range(4)] for b in range(2)]
        steps = [(cj, d, i, jm) for cj in range(4) for d in range(2) for i in range(TCH) for jm in range(4)]
        N = len(steps)

        def load_u(cj):
            ub = cj % 2
            P.dma("sp", uch[ub], uT_s[cj], f"ld_u{ub}_{int(phase2)}", [TuS], [Tuch[ub]])

        NSET = 3 if phase2 else 4

        def bu_banks(n):
            sidx = n % NSET
            return bank(2 * sidx, NCH), bank(2 * sidx + 1, NCH), Tb[2 * sidx]

        def emitBU(n):
            cj, d, i, jm = steps[n]
            ub = cj % 2
            rhs = useq(uch[ub], jm, d, i)
            BUr, BUi, TB = bu_banks(n)
            for ri, bk_ in enumerate((BUr, BUi)):
                P.mm(bk_, Wt[32 * jm:32 * jm + 32, d, cj, ri, :], rhs, True, True, [TWt, Tuch[ub]], [TB], tp=(32 * jm, 0))

        def t_ops(n):
            cj, d, i, jm = steps[n]
            dj = d * 16 + cj * 4 + jm
            BUr, BUi, TB = bu_banks(n)
            Hr, Hi = Hs[jm]
            TH = THs[jm]
            Er, Ei = Eb[:, 0, :, dj], Eb[:, 1, :, dj]
            if i == 0 and not phase2:
                P.copy("dve", Hr, BUr, [TB], [TH])
                P.copy("dve", Hi, BUi, [TB], [TH])
                return
            if i == 0:
                inr, ini, Rin = Er, Ei, [TE]
            else:
                inr, ini, Rin = Hr, Hi, [TH]
            t1, t2 = tm[jm % 2]
            Tt = Ttm[jm % 2]
            P.stt("dve", t1, ini, s_nai[:, dj:dj + 1], BUr, ALU.mult, ALU.add, Rin + [TB, Tprm], [Tt])
            P.stt("dve", t2, inr, s_ai[:, dj:dj + 1], BUi, ALU.mult, ALU.add, Rin + [TB, Tprm], [Tt])

        def o_ops(n):
            cj, d, i, jm = steps[n]
            dj = d * 16 + cj * 4 + jm
            stepno = n // 4
            ybk = 6 + stepno % 2
            Hr, Hi = Hs[jm]
            TH = THs[jm]
            Er, Ei = Eb[:, 0, :, dj], Eb[:, 1, :, dj]
            if i == 0 and not phase2:
                return
            if i == 0:
                inr, ini, Rin = Er, Ei, [TE]
            else:
                inr, ini, Rin = Hr, Hi, [TH]
            last1 = (i == TCH - 1) and not phase2
            outr, outi, Wout = (Er, Ei, [TE]) if last1 else (Hr, Hi, [TH])
            t1, t2 = tm[jm % 2]
            Tt = Ttm[jm % 2]
            P.stt("dve", outr, inr, s_ar[:, dj:dj + 1], t1, ALU.mult, ALU.add, Rin + [Tt, Tprm], Wout)
            P.stt("dve", outi, ini, s_ar[:, dj:dj + 1], t2, ALU.mult, ALU.add, Rin + [Tt, Tprm], Wout)
            if phase2:
                hb = stepno % 2
                Hbr, Hbi = Hb[hb][jm]
                P.copy("act", Hbr, Hr, [TH], [THb[hb][jm]])
                P.copy("act", Hbi, Hi, [TH], [THb[hb][jm]])
                yps = bank(ybk, NCH)[32 * jm:32 * jm + 32, :]
                P.mm(yps, CwT[:, dj, 0, :], Hbr, True, False, [TCw, THb[hb][jm]], [Tb[ybk]], tp=(0, 32 * jm))
                P.mm(yps, CwT[:, dj, 1, :], Hbi, False, True, [TCw, THb[hb][jm]], [Tb[ybk]], tp=(0, 32 * jm))
                if jm == 3:
                    if d == 0:
                        dst = yT[:, 0, cj, i:L:TCH]
                    else:
                        dst = yT[:, 1, cj, L - 1 - i:None:-TCH]
                    P.copy("act", dst, bank(ybk, NCH)[:, LC // TCH:NCH], [Tb[ybk]], [TyT[d][cj]])

        load_u(0)
        load_u(1)
        nbu = 0
        while nbu < min(NSET, N):
            emitBU(nbu)
            nbu += 1
        for k in range(N // 2):
            n0 = 2 * k
            cj, d, i, jm = steps[n0]
            if d == 0 and i == 0 and jm == 0 and 1 <= cj and cj + 1 < 4:
                load_u(cj + 1)
            while nbu <= n0 + 1:
                emitBU(nbu)
                nbu += 1
            t_ops(n0)
            t_ops(n0 + 1)
            while nbu < N and nbu <= n0 + 1 + NSET:
                emitBU(nbu)
                nbu += 1
            o_ops(n0)
            o_ops(n0 + 1)

    scan_pass(False)
    P.barrier()
    A.release(mP)
    if stop_after == "pS1":
        return finish(P, nc, [TqS, TuS, Tdbg])
    Th_ = [T("ch0"), T("ch1")]
    Tc_ = [T(f"cc{i}") for i in range(4)]
    P.memset("pool", hst[0][0], 0.0, [Th_[0]])
    P.memset("pool", hst[0][1], 0.0, [Th_[0]])
    for s_ in range(NCH):
        hr, hi = hst[s_ % 2]
        nhr, nhi = hst[(s_ + 1) % 2]
        Tho, Thn_ = Th_[s_ % 2], Th_[(s_ + 1) % 2]
        P.tt("pool", ct[0], a16r, hr, ALU.mult, [Tho, Tprm], [Tc_[0]])
        P.tt("pool", ct[1], a16i, hi, ALU.mult, [Tho, Tprm], [Tc_[1]])
        P.tt("pool", ct[2], a16r, hi, ALU.mult, [Tho, Tprm], [Tc_[2]])
        P.tt("pool", ct[3], a16i, hr, ALU.mult, [Tho, Tprm], [Tc_[3]])
        P.tt("pool", ct[0], ct[0], ct[1], ALU.subtract, [Tc_[0], Tc_[1]], [Tc_[0]])
        P.tt("pool", ct[2], ct[2], ct[3], ALU.add, [Tc_[2], Tc_[3]], [Tc_[2]])
        P.tt("pool", nhr, ct[0], Eb[:, 0, s_, :], ALU.add, [Tc_[0], TE], [Thn_])
        P.tt("pool", nhi, ct[2], Eb[:, 1, s_, :], ALU.add, [Tc_[2], TE], [Thn_])
        P.copy("pool", Eb[:, 0, s_, :], hr, [Tho], [TE])
        P.copy("pool", Eb[:, 1, s_, :], hi, [Tho], [TE])
    if stop_after == "pCH":
        return finish(P, nc, [TqS, TuS, Tdbg])
    mB = A.mark()
    tbias = A.alloc([4, 15, 64], F32)
    mk = A.alloc([64], F32)
    Ttb = T("tbias")
    P.dma("act", tbias, tb_d, "ld_tb", [], [Ttb])
    P.dma("act", mk, mk_d, "ld_tb", [], [Ttb])
    for j in range(4):
        P.tt("dve", tbias[:, j], tbias[:, j], mk.unsqueeze(1).to_broadcast([128, 15, 64]), ALU.add, [Ttb], [Ttb])
    NQB = 3
    qbd = [A.alloc([4, 128], BF16) for _ in range(NQB)]
    Tqbd = [T(f"qbd{i}") for i in range(NQB)]
    for i in range(NQB):
        P.memset("dve", qbd[i], 0.0, [Tqbd[i]])
    sl = [A.alloc([512], F32) for _ in range(2)]
    Tsl = [T("sl0"), T("sl1")]
    Pt = [A.alloc([896], BF16) for _ in range(2)]
    TPt = [T("Pt0"), T("Pt1")]
    for i in range(2):
        P.memset("dve", Pt[i], 0.0, [TPt[i]])
    PTs = [A.alloc([896], BF16) for _ in range(2)]
    TPTs = [T("PTs0"), T("PTs1")]
    On = [A.alloc([128], BF16) for _ in range(2)]
    TOn = [T("On0"), T("On1")]
    sm = [A.alloc([8], F32) for _ in range(2)]
    Tsm = [T("sm0"), T("sm1")]
    ast = [A.alloc([4, 512], BF16) for _ in range(2)]
    Tast = [T("ast0"), T("ast1")]
    TaS = T("aT_s")
    TSl = [Tb[0], Tb[1]]
    TSc = [T("Sc0"), T("Sc1")]
    TPT = [Tb[3], Tb[4]]
    TO = [T("O0"), T("O1")]
    TOT = [T("OT0"), T("OT1")]
    it = 0
    for r in range(64):
        rs = min(max(r - 4, 0), 56)
        d0 = rs - r + 7
        qi = r % NQB
        c0 = r * 64
        for hh in range(2):
            P.dma("sp", qbd[qi][hh * 64:(hh + 1) * 64, :, hh * 64:(hh + 1) * 64],
                  qT_s[:, hh * 64:(hh + 1) * 64, c0:c0 + 64].rearrange("c p t -> p c t"), f"ld_q{qi}", [TqS], [Tqbd[qi]])
        g0 = 1 + (rs * 64) // 512
        g1 = 1 + (rs * 64 + 511) // 512
        Rk = [TkT[0], TkT[g0]] + ([TkT[g1]] if g1 != g0 else [])
        Rv = [Tv[0], Tv[g0]] + ([Tv[g1]] if g1 != g0 else [])
        odd = rs % 2 == 1
        gsel = (r // 8) % 2
        for j in range(4):
            b2 = it % 2
            it += 1
            S_l = bank(b2)
            S_c = bank(2, 256, b2 * 256)
            P.mm(S_l, qbd[qi][:, j, :], kT[:, j, rs * 64:rs * 64 + 512], True, True, [Tqbd[qi]] + Rk, [TSl[b2]])
            P.mm(S_c, qbd[qi][:, j, :], kT[:, j, L:L + LC], True, True, [Tqbd[qi]] + Rk, [TSc[b2]])
            smt = sm[b2]
            P.stt("dve", sl[b2], S_l, 0.125, tbias[:, j, d0:d0 + 8, :].rearrange("p a b -> p (a b)"), ALU.mult, ALU.add,
                  [TSl[b2], Ttb], [Tsl[b2]])
            P.emit("dve", lambda e, o=smt[:, 0:1], i=sl[b2]: e.reduce_max(out=o, in_=i, axis=AX.X), [Tsl[b2]], [Tsm[b2]])
            P.emit("dve", lambda e, o=smt[:, 1:2], i=S_c: e.reduce_max(out=o, in_=i, axis=AX.X), [TSc[b2]], [Tsm[b2]])
            P.stt("dve", smt[:, 2:3], smt[:, 1:2], 0.125, smt[:, 0:1], ALU.mult, ALU.max, [Tsm[b2]], [Tsm[b2]])
            P.ts("dve", smt[:, 3:4], smt[:, 2:3], -1.0, None, ALU.mult, None, [Tsm[b2]], [Tsm[b2]])
            P.act(Pt[b2][:, 64:576], sl[b2], AF.Exp, [Tsl[b2], Tsm[b2]], [TPt[b2], Tsm[b2]], bias=smt[:, 3:4],
                  accum=smt[:, 4:5])
            P.act(Pt[b2][:, 640:896], S_c, AF.Exp, [TSc[b2], Tsm[b2]], [TPt[b2], Tsm[b2]], bias=smt[:, 3:4], scale=0.125,
                  accum=smt[:, 5:6])
            P.tt("dve", smt[:, 6:7], smt[:, 4:5], smt[:, 5:6], ALU.add, [Tsm[b2]], [Tsm[b2]])
            P.emit("dve", lambda e, o=smt[:, 7:8], i=smt[:, 6:7]: e.reciprocal(out=o, in_=i), [Tsm[b2]], [Tsm[b2]])
            PTp = bank_bf(3 + b2)
            if not odd:
                blocks = [(64 + 128 * b, 0, 128, rs // 2 + b) for b in range(4)]
            else:
                vb = (rs - 1) // 2
                blocks = [(0, 64, 128, vb)] + [(128 * b, 0, 128, vb + b) for b in (1, 2, 3)] + [(512, 0, 64, vb + 4)]
            blocks += [(640, 0, 128, 32), (768, 0, 128, 33)]
            nblk = len(blocks)
            for bi, (col, p0, p1, vt) in enumerate(blocks):
                P.tr(PTp[:, bi * 128:(bi + 1) * 128], Pt[b2][:, col:col + 128], ident_b, [TPt[b2], Tconst], [TPT[b2]])
            P.copy("act", PTs[b2][:, 0:nblk * 128], PTp[:, 0:nblk * 128], [TPT[b2]], [TPTs[b2]])
            O = bank(5, 128, b2 * 128)
            for bi, (col, p0, p1, vt) in enumerate(blocks):
                P.mm(O, PTs[b2][p0:p1, bi * 128:(bi + 1) * 128], vtok[p0:p1, vt, j * 128:(j + 1) * 128], bi == 0,
                     bi == nblk - 1, [TPTs[b2]] + Rv, [TO[b2]])
            P.ts("dve", On[b2], O, smt[:, 7:8], None, ALU.mult, None, [TO[b2], Tsm[b2]], [TOn[b2]])
            OT = bank_bf(6)[:, b2 * 128:(b2 + 1) * 128]
            P.tr(OT, On[b2], ident_b, [TOn[b2], Tconst], [TOT[b2]])
            cc = (r % 8) * 64
            P.copy("act", ast[gsel][0:64, j, cc:cc + 64], OT[0:64, 0:64], [TOT[b2]], [Tast[gsel]])
            P.copy("act", ast[gsel][64:128, j, cc:cc + 64], OT[64:128, 64:128], [TOT[b2]], [Tast[gsel]])
        if r % 8 == 7:
            g = r // 8
            P.dma("sp", aT_s[:, :, g * 512:(g + 1) * 512].rearrange("c p t -> p c t"), ast[gsel], f"st_a{gsel}",
                  [Tast[gsel]], [TaS])
    P.barrier()
    A.release(mB)
    if stop_after == "pB":
        return finish(P, nc, [TqS, TuS, TaS, Tdbg])

    cur_ = A.off
    A.off = mA
    yT = A.alloc([2, 4, L], BF16)
    assert A.off <= m1
    A.off = cur_
    scan_pass(True, yT)
    TyS = T("y_s")
    for d in range(2):
        P.dma("sp", y_o[d].rearrange("c p t -> p c t"), yT[:, d], "st_y", TyT[d], [TyS])
    P.barrier()
    A.release(mA)
    if stop_after == "pS":
        return finish(P, nc, [TqS, TuS, TaS, TyS, Tdbg])

    wG = A.alloc([8, 2048], BF16)
    wBA = A.alloc([4, 1024], BF16)
    wBS = A.alloc([4, 1024], BF16)
    wGLU = A.alloc([4, 1024], BF16)
    wO = A.alloc([8, 1024], BF16)
    dsk = A.alloc([4], F32)
    TwC = T("wC")
    wg_v = wgate_d.rearrange("(k p) m -> p k m", p=128)
    for c4 in range(4):
        P.dma("pool", wG[:, :, c4 * 512:(c4 + 1) * 512], wg_v[:, :, c4 * 512:(c4 + 1) * 512], "ld_wC", [], [TwC])
    P.dma("pool", wGLU, wglu_d.rearrange("(k p) m -> p k m", p=128), "ld_wC", [], [TwC])
    P.dma("pool", wBA, wba_d.rearrange("(k p) m -> p k m", p=128), "ld_wC", [], [TwC])
    P.dma("pool", wBS, wbs_d.rearrange("(k p) m -> p k m", p=128), "ld_wC", [], [TwC])
    wo_v = wout_d.rearrange("(k p) m -> p k m", p=128)
    for c2 in range(2):
        P.dma("pool", wO[:, :, c2 * 512:(c2 + 1) * 512], wo_v[:, :, c2 * 512:(c2 + 1) * 512], "ld_wC", [], [TwC])
    P.dma("act", dsk, dsk_d, "ld_dsk", [], [TwC])
    NX = 8
    xt = [A.alloc([D], F32) for _ in range(NX)]
    Txt = [T(f"cxt{i}") for i in range(NX)]
    junk = A.alloc([D], BF16)
    Tjunk = T("cjunk")
    xn = [A.alloc([D], BF16) for _ in range(2)]
    Txn = [T(f"cxn{i}") for i in range(2)]
    st = [A.alloc([4], F32) for _ in range(NX)]
    Tst = [T(f"cst{i}") for i in range(NX)]
    nT = [A.alloc([8, 512], BF16) for _ in range(2)]
    TnT = [T("cnT0"), T("cnT1")]
    aTg = [A.alloc([4, 512], BF16) for _ in range(2)]
    uTg = [A.alloc([4, 512], BF16) for _ in range(2)]
    yg = [A.alloc([2, 4, 512], BF16)] * 2
    Tag = [T("aTg0"), T("aTg1")]
    Tug = [T("uTg0"), T("uTg1")]
    Tyg = [T("yg0")] * 2
    zT = A.alloc([4, 512], BF16)
    sT = A.alloc([4, 512], BF16)
    mT = A.alloc([8, 512], BF16)
    TzT, TsT, TmT = T("zT"), T("sT"), T("mT")
    ft = [A.alloc([512], F32) for _ in range(10)]
    Tft = [T(f"ft{i}") for i in range(10)]
    ThS = T("h_s")
    bcyc = [0]

    def nb_():
        b = 2 + bcyc[0] % 6
        bcyc[0] += 1
        return b

    xcnt_ = [0]
    xis_all = {}

    def c1_front(g):
        nb = g % 2
        t0 = g * 512
        P.dma("act", aTg[nb], aT_s[:, :, t0:t0 + 512].rearrange("c p t -> p c t"), f"ld_ag{nb}", [TaS], [Tag[nb]])
        P.dma("act", uTg[nb], uT_s[:, :, LC + t0:LC + t0 + 512].rearrange("c p t -> p c t"), f"ld_ug{nb}", [TuS], [Tug[nb]])
        for d in range(2):
            P.dma("act", yg[nb][:, d], y_o[d, :, :, t0:t0 + 512].rearrange("c p t -> p c t"), f"ld_yg{nb}", [TyS], [Tyg[nb]])
        xis = []
        for half in range(2):
            pT = ps[:, 0:1024].bitcast(BF16).rearrange("p (c t) -> p c t", c=8)
            for tl in range(2):
                ti = half * 2 + tl
                xi = xcnt_[0] % NX
                ni = xcnt_[0] % 2
                xcnt_[0] += 1
                xis.append(xi)
                P.dma("sp", xt[xi], x_d[t0 + ti * 128:t0 + (ti + 1) * 128, :], f"ld_cx{xi}", [], [Txt[xi]])
                P.act(junk, xt[xi], AF.Square, [Txt[xi]], [Tjunk, Tst[xi]], accum=st[xi][:, 0:1])
                rstd_ops(P, st[xi], Tst[xi])
                P.act(xn[ni], xt[xi], AF.Copy, [Txt[xi], Tst[xi]], [Txn[ni]], scale=st[xi][:, 2:3])
                for c in range(8):
                    P.tr(pT[:, c, tl * 128:(tl + 1) * 128], xn[ni][:, c * 128:(c + 1) * 128], ident_b, [Txn[ni], Tconst], [Tb[0]])
            for c in range(8):
                P.ts("dve", nT[nb][:, c, half * 256:(half + 1) * 256], pT[:, c, :], A1l[:, c:c + 1], B1l[:, c:c + 1],
                     ALU.mult, ALU.add, [Tb[0], Tmod], [TnT[nb]])
        for c in range(4):
            f0, f1 = ft[0], ft[1]
            P.tt("dve", f0, yg[nb][:, 0, c, :], yg[nb][:, 1, c, :], ALU.add, [Tyg[nb]], [Tft[0]])
            P.stt("dve", f0, uTg[nb][:, c, :], dsk[:, c:c + 1], f0, ALU.mult, ALU.add, [Tug[nb], TwC, Tft[0]], [Tft[0]])
            P.tt("dve", f1, f0, f0, ALU.mult, [Tft[0]], [Tft[1]])
            P.ts("dve", f1, f1, 0.044715, 1.0, ALU.mult, ALU.add, [Tft[1]], [Tft[1]])
            P.tt("dve", f1, f1, f0, ALU.mult, [Tft[0], Tft[1]], [Tft[1]])
            P.act(f1, f1, AF.Sigmoid, [Tft[1]], [Tft[1]], scale=1.5957691216057308)
            P.tt("dve", zT[:, c, :], f0, f1, ALU.mult, [Tft[0], Tft[1]], [TzT])
        for m in range(4):
            bv, bg = nb_(), nb_()
            for k in range(4):
                P.mm(bank(bv), wGLU[:, k, m * 128:(m + 1) * 128], zT[:, k, :], k == 0, k == 3, [TwC, TzT], [Tb[bv]])
            for k in range(4):
                P.mm(bank(bg), wGLU[:, k, 512 + m * 128:512 + (m + 1) * 128], zT[:, k, :], k == 0, k == 3, [TwC, TzT], [Tb[bg]])
            P.act(ft[2], bank(bg), AF.Sigmoid, [Tb[bg]], [Tft[2]])
            P.tt("dve", sT[:, m, :], bank(bv), ft[2], ALU.mult, [Tb[bv], Tft[2]], [TsT])
        xis_all[g] = xis

    def c1_stepf(g):
        nb = g % 2
        un = 0
        for m in range(8):
            prods = []
            for which in range(2):
                bg, bb = nb_(), nb_()
                gc = which * 1024 + m * 128
                for k in range(8):
                    P.mm(bank(bg), wG[:, k, gc:gc + 128], nT[nb][:, k, :], k == 0, k == 7, [TwC, TnT[nb]], [Tb[bg]])
                if which == 0:
                    for k in range(4):
                        P.mm(bank(bb), wBA[:, k, m * 128:(m + 1) * 128], aTg[nb][:, k, :], k == 0, k == 3, [TwC, Tag[nb]], [Tb[bb]])
                else:
                    for k in range(4):
                        P.mm(bank(bb), wBS[:, k, m * 128:(m + 1) * 128], sT[:, k, :], k == 0, k == 3, [TwC, TsT], [Tb[bb]])
                si = 3 + un % 2
                pi = 5 + un % 4
                un += 1
                P.act(ft[si], bank(bg), AF.Sigmoid, [Tb[bg]], [Tft[si]])
                P.tt("dve", ft[pi], bank(bb), ft[si], ALU.mult, [Tb[bb], Tft[si]], [Tft[pi]])
                prods.append(pi)
            P.tt("pool", mT[:, m, :], ft[prods[0]], ft[prods[1]], ALU.add, [Tft[prods[0]], Tft[prods[1]]], [TmT])

    def c1_stepg(g):
        nb = g % 2
        t0 = g * 512
        xis = xis_all[g]
        for ti in range(4):
            xi = xis[ti]
            for nh in range(2):
                bo = nb_()
                for k in range(8):
                    P.mm(bank(bo), mT[:, k, ti * 128:(ti + 1) * 128], wO[:, k, nh * 512:(nh + 1) * 512], k == 0, k == 7,
                         [TmT, TwC], [Tb[bo]])
                gi_ = 9 if nh else 2
                P.tt("dve", ft[gi_], bank(bo), g1b[:, nh * 512:(nh + 1) * 512], ALU.mult, [Tb[bo], Tgb], [Tft[gi_]])
                P.tt("pool", xt[xi][:, nh * 512:(nh + 1) * 512], xt[xi][:, nh * 512:(nh + 1) * 512], ft[gi_], ALU.add,
                     [Txt[xi], Tft[gi_]], [Txt[xi]])
            P.dma("sp", h_s[t0 + ti * 128:t0 + (ti + 1) * 128, :], xt[xi], f"st_h{xi}", [Txt[xi]], [ThS])

    c1_front(0)
    for g in range(8):
        c1_stepf(g)
        if g + 1 < 8:
            c1_front(g + 1)
        c1_stepg(g)
    P.barrier()
    A.release(mA)
    if stop_after == "pC1":
        return finish(P, nc, [ThS, Tdbg])

    NHC = FH // 128
    wFI = A.alloc([8, 2 * FH], BF16)
    wFO = A.alloc([NHC, D], BF16)
    TwF = T("wF")
    wfi_v = wfi_d.rearrange("(k p) m -> p k m", p=128)
    for c11 in range(11):
        P.dma("pool", wFI[:, :, c11 * 512:(c11 + 1) * 512], wfi_v[:, :, c11 * 512:(c11 + 1) * 512], "ld_wF", [], [TwF])
    wfo_v = wfo_d.rearrange("(k p) m -> p k m", p=128)
    for c2 in range(2):
        P.dma("pool", wFO[:, 11 * c2:11 * (c2 + 1), :], wfo_v[:, 11 * c2:11 * (c2 + 1), :],
              "ld_wF", [], [TwF])
    NX = 4
    ht = [A.alloc([D], F32) for _ in range(NX)]
    Tht = [T(f"ht{i}") for i in range(NX)]
    hn = [A.alloc([D], BF16) for _ in range(2)]
    Thn = [T("hn0"), T("hn1")]
    st = [A.alloc([8], F32) for _ in range(NX)]
    Tst = [T(f"fst{i}") for i in range(NX)]
    n2T = A.alloc([8, 512], BF16)
    Tn2 = T("n2T")
    actT = A.alloc([NHC, 512], BF16)
    TaT = T("actT")
    fs = [A.alloc([512], F32) for _ in range(4)]
    Tfs = [T(f"fs{i}") for i in range(4)]
    Tout = T("out")
    hcnt = 0
    for g in range(8):
        t0 = g * 512
        his = []
        for half in range(2):
            pT = ps[:, 0:1024].bitcast(BF16).rearrange("p (c t) -> p c t", c=8)
            for tl in range(2):
                ti = half * 2 + tl
                hi_ = hcnt % NX
                ni = hcnt % 2
                hcnt += 1
                his.append(hi_)
                P.dma("sp", ht[hi_], h_s[t0 + ti * 128:t0 + (ti + 1) * 128, :], f"ld_h{hi_}", [ThS], [Tht[hi_]])
                jk = actT[:, 0:2, :].rearrange("p a b -> p (a b)")
                P.act(jk, ht[hi_], AF.Square, [Tht[hi_]], [TaT, Tst[hi_]], accum=st[hi_][:, 0:1])
                rstd_ops(P, st[hi_], Tst[hi_])
                P.act(hn[ni], ht[hi_], AF.Copy, [Tht[hi_], Tst[hi_]], [Thn[ni]], scale=st[hi_][:, 2:3])
                for c in range(8):
                    P.tr(pT[:, c, tl * 128:(tl + 1) * 128], hn[ni][:, c * 128:(c + 1) * 128], ident_b, [Thn[ni], Tconst], [Tb[0]])
            for c in range(8):
                P.ts("dve", n2T[:, c, half * 256:(half + 1) * 256], pT[:, c, :], A2[:, c:c + 1], B2[:, c:c + 1],
                     ALU.mult, ALU.add, [Tb[0], Tmod], [Tn2])
        for hc in range(NHC):
            ba_, bb_ = nb_(), nb_()
            for k in range(8):
                P.mm(bank(ba_), wFI[:, k, hc * 128:(hc + 1) * 128], n2T[:, k, :], k == 0, k == 7, [TwF, Tn2], [Tb[ba_]])
            for k in range(8):
                P.mm(bank(bb_), wFI[:, k, FH + hc * 128:FH + (hc + 1) * 128], n2T[:, k, :], k == 0, k == 7, [TwF, Tn2], [Tb[bb_]])
            fi_ = hc % 2
            P.act(fs[fi_], bank(ba_), AF.Silu, [Tb[ba_]], [Tfs[fi_]])
            P.tt("dve", actT[:, hc, :], bank(bb_), fs[fi_], ALU.mult, [Tb[bb_], Tfs[fi_]], [TaT])
        for ti in range(4):
            hi_ = his[ti]
            for nh in range(2):
                bo = nb_()
                for hc in range(NHC):
                    P.mm(bank(bo), actT[:, hc, ti * 128:(ti + 1) * 128], wFO[:, hc, nh * 512:(nh + 1) * 512], hc == 0,
                         hc == NHC - 1, [TaT, TwF], [Tb[bo]])
                P.tt("dve", fs[2 + nh], bank(bo), g2b[:, nh * 512:(nh + 1) * 512], ALU.mult, [Tb[bo], Tgb], [Tfs[2 + nh]])
                P.tt("pool", ht[hi_][:, nh * 512:(nh + 1) * 512], ht[hi_][:, nh * 512:(nh + 1) * 512], fs[2 + nh], ALU.add,
                     [Tht[hi_], Tfs[2 + nh]], [Tht[hi_]])
            sti = st[hi_]
            P.act(hn[ti % 2], ht[hi_], AF.Square, [Tht[hi_]], [Thn[ti % 2], Tst[hi_]], accum=sti[:, 4:5])
            P.ts("dve", sti[:, 5:6], sti[:, 4:5], 1.0 / D, EPS, ALU.mult, ALU.add, [Tst[hi_]], [Tst[hi_]])
            P.act(sti[:, 6:7], sti[:, 5:6], AF.Sqrt, [Tst[hi_]], [Tst[hi_]])
            P.emit("dve", lambda e, o=sti[:, 7:8], i=sti[:, 6:7]: e.reciprocal(out=o, in_=i), [Tst[hi_]], [Tst[hi_]])
            P.stt("dve", ht[hi_], ht[hi_], sti[:, 7:8], fgb, ALU.mult, ALU.mult, [Tht[hi_], Tst[hi_], Tgb], [Tht[hi_]])
            P.dma("sp", out_d[t0 + ti * 128:t0 + (ti + 1) * 128, :], ht[hi_], f"st_o{hi_}", [Tht[hi_]], [Tout])
    return finish(P, nc, [Tout, Tdbg])


def rstd_ops(P, st, Tst):
    P.ts("dve", st[:, 1:2], st[:, 0:1], 1.0 / D, EPS, ALU.mult, ALU.add, [Tst], [Tst])
    P.act(st[:, 3:4], st[:, 1:2], AF.Sqrt, [Tst], [Tst])
    P.emit("dve", lambda e: e.reciprocal(out=st[:, 2:3], in_=st[:, 3:4]), [Tst], [Tst])


def finish(P, nc, tiles=()):
    P.barrier()
    nc._marks = P.marks
    P.final_wait("sp", list(tiles))
    P.build()
    return nc


def _pcol(v, nchunk):
    return np.ascontiguousarray(v.reshape(nchunk, 128).T)


def host_shared(inp):
    f32 = np.float32
    sh = {}
    w_in = inp["w_in"][0]
    idx = np.concatenate([np.arange(16, 32), np.arange(0, 16), np.arange(48, 64), np.arange(32, 48)])
    swap = np.concatenate([h * 64 + idx for h in range(NH)])
    sh["w_in_ext"] = np.ascontiguousarray(np.concatenate([w_in[:, :2048], w_in[:, swap], w_in[:, 512 + swap]], axis=1))
    sh["w_gate"] = np.ascontiguousarray(w_in[:, 2048:4096])
    sh["w_mod"] = np.ascontiguousarray(inp["w_mod"][0])
    sh["b_mod"] = _pcol(inp["b_mod"][0], 48)
    sh["gA"] = _pcol(inp["attn_norm_g"][0], 8)
    sh["gF"] = _pcol(inp["ffn_norm_g"][0], 8)
    sh["gFin"] = _pcol(inp["final_norm_g"], 8)
    t = np.arange(L)
    inv = (10000.0 ** (-np.arange(16, dtype=f32) / 16)).astype(f32)
    cos = np.zeros((64, L), f32)
    sin = np.zeros((64, L), f32)
    for half, pos in ((0, t // GW), (1, t % GW)):
        ang = pos.astype(f32)[None, :] * inv[:, None]
        c_, s_ = np.cos(ang).astype(f32), np.sin(ang).astype(f32)
        cos[half * 32:half * 32 + 16] = c_
        cos[half * 32 + 16:half * 32 + 32] = c_
        sin[half * 32:half * 32 + 16] = -s_
        sin[half * 32 + 16:half * 32 + 32] = s_
    sh["rope_cos"] = np.ascontiguousarray(np.concatenate([cos, cos], 0))
    sh["rope_sin"] = np.ascontiguousarray(np.concatenate([sin, sin], 0))
    rpb = inp["rel_pos_bias"][0]
    q = np.arange(64)[:, None]
    k = np.arange(64)[None, :]
    dc = np.clip(k - q + 15, 0, 30)
    tb = np.zeros((128, 4, 15, 64), f32)
    for j in range(4):
        for hh in range(2):
            tb[hh * 64:(hh + 1) * 64, j] = np.transpose(rpb[2 * j + hh][:, dc], (1, 0, 2))
    sh["bias_tb"] = tb
    cs = np.clip(q - 8, 0, 48)
    inw = (k >= cs) & (k < cs + 16)
    mk = np.where(inw, 0.0, -1e30).astype(f32)
    sh["bias_mask"] = np.ascontiguousarray(np.concatenate([mk, mk], 0))
    sh["lam_re"] = np.ascontiguousarray(inp["ssm_lambda_re"][0].reshape(32, 128))
    sh["lam_im"] = np.ascontiguousarray(inp["ssm_lambda_im"][0].reshape(32, 128))
    sh["log_dt"] = np.ascontiguousarray(inp["ssm_log_dt"][0].reshape(32, 2))
    sh["b_re"] = np.ascontiguousarray(inp["ssm_b_re"][0])
    sh["b_im"] = np.ascontiguousarray(inp["ssm_b_im"][0])
    sh["c_re"] = np.ascontiguousarray(inp["ssm_c_re"][0])
    sh["c_im"] = np.ascontiguousarray(inp["ssm_c_im"][0])
    sh["d_skip"] = _pcol(inp["ssm_d"][0], 4)
    sh["w_glu"] = np.ascontiguousarray(inp["w_glu"][0])
    sh["w_ba"] = np.ascontiguousarray(inp["w_branch_attn"][0])
    sh["w_bs"] = np.ascontiguousarray(inp["w_branch_ssm"][0])
    sh["w_out"] = np.ascontiguousarray(inp["w_out"][0])
    sh["w_ffn_in"] = np.ascontiguousarray(inp["w_ffn_in"][0])
    sh["w_ffn_out"] = np.ascontiguousarray(inp["w_ffn_out"][0])
    return sh


def host_core(inp, b):
    cv = np.stack([inp["c"][b], inp["c_ctx"]], axis=1)
    return {
        "x": np.ascontiguousarray(inp["x"][b]),
        "ctx": np.ascontiguousarray(inp["ctx"][b]),
        "cvec": np.ascontiguousarray(cv.reshape(8, 128, 2).transpose(1, 0, 2)),
    }


_NC_CACHE = {}


def kernel(**inputs):
    inp = {k: np.asarray(v) for k, v in inputs.items()}
    if "nc" not in _NC_CACHE:
        _NC_CACHE["nc"] = build_program()
    nc = _NC_CACHE["nc"]
    sh = host_shared(inp)
    in_maps = [{**sh, **host_core(inp, b)} for b in range(8)]
    res = run_bass_kernel_spmd(nc, in_maps, core_ids=list(range(8)))
    return np.stack([np.asarray(r["out"]) for r in res.results], axis=0).astype(np.float32)
```

```python
import math
from contextlib import ExitStack

import numpy as np
import concourse.bass as bass
import concourse.mybir as mybir
from concourse.bass_utils import run_bass_kernel_spmd

F32 = mybir.dt.float32
BF16 = mybir.dt.bfloat16
U8 = mybir.dt.uint8
I32 = mybir.dt.int32
AF = mybir.ActivationFunctionType
ALU = mybir.AluOpType
AX = mybir.AxisListType
DTSZ = {F32: 4, BF16: 2, U8: 1, I32: 4}

L = 4096
D = 1024
LC = 256
NTOK = L + LC
GW = 64
NH = 8
HD = 64
AW = 512
SW = 512
NG = 32
NP = 64
FH = 2816
EPS = 1e-6
TCH = 16
NCH = NTOK // TCH
UEXT = L + 2 * LC

ENGS = ("pe", "act", "dve", "pool", "sp")


class T:
    __slots__ = ("name", "w", "r")

    def __init__(self, name):
        self.name = name
        self.w = None
        self.r = {}


class Prog:
    def __init__(self, nc):
        self.nc = nc
        self.es = ExitStack()
        self.ops = {e: [] for e in ENGS}
        self.cnt = {}
        self.sems = {}
        self.seen = {e: {} for e in ENGS}
        self.marks = []
        for e in ENGS:
            self._sem("E_" + e)

    def _sem(self, key):
        if key not in self.sems:
            self.sems[key] = self.es.enter_context(self.nc.semaphore(key))
            self.cnt[key] = 0
        return self.sems[key]

    def sbuf(self, name, shape, dt):
        return self.es.enter_context(self.nc.sbuf_tensor(name, list(shape), dt))

    def psum(self, name, shape, dt):
        return self.es.enter_context(self.nc.psum_tensor(name, list(shape), dt))

    def emit(self, eng, fn, reads=(), writes=(), dma_sem=None):
        own = "E_" + eng
        deps = {}

        def add(ev, is_raw):
            if ev is None:
                return
            k, v = ev
            if k == own and (eng == "pe" or not is_raw):
                return
            if deps.get(k, 0) < v:
                deps[k] = v

        for t in reads:
            add(t.w, True)
        for t in writes:
            add(t.w, False)
            for k, v in t.r.items():
                add((k, v), False)
        waits = []
        seen = self.seen[eng]
        for k, v in deps.items():
            if seen.get(k, 0) < v:
                seen[k] = v
                waits.append((k, v))
        if dma_sem is not None:
            self._sem(dma_sem)
            self.cnt[dma_sem] += 16
            ev = (dma_sem, self.cnt[dma_sem])
            inc = (dma_sem, 16)
        else:
            self.cnt[own] += 1
            ev = (own, self.cnt[own])
            inc = (own, 1)
        self.ops[eng].append((waits, fn, inc))
        for t in reads:
            if t.r.get(ev[0], 0) < ev[1]:
                t.r[ev[0]] = ev[1]
        for t in writes:
            t.w = ev
            t.r = {}
        return ev

    def final_wait(self, eng, tiles):
        waits = []
        for t in tiles:
            if t.w is not None:
                waits.append(t.w)
            for k, v in t.r.items():
                waits.append((k, v))
        self.ops[eng].append((waits, None, None))

    def barrier(self):
        snap = dict(self.cnt)
        self.marks.append(snap)
        for e in ENGS:
            waits = []
            for k, v in snap.items():
                if v > 0 and k != "E_" + e and self.seen[e].get(k, 0) < v:
                    self.seen[e][k] = v
                    waits.append((k, v))
            if waits:
                self.ops[e].append((waits, None, None))

    def build(self):
        nc = self.nc
        needed = {}
        for e in ENGS:
            for waits, fn, inc in self.ops[e]:
                for k, v in waits:
                    if k.startswith("E_"):
                        needed.setdefault(k, set()).add(v)
        rank = {k: {v: i + 1 for i, v in enumerate(sorted(vs))} for k, vs in needed.items()}
        with nc.Block() as block:
            def run(e, handle):
                seq = 0
                own = "E_" + e
                myrank = rank.get(own, {})
                for waits, fn, inc in self.ops[e]:
                    for k, v in waits:
                        if k.startswith("E_"):
                            handle.wait_ge(self.sems[k], rank[k][v])
                        else:
                            handle.wait_ge(self.sems[k], v)
                    if fn is not None:
                        ins = fn(handle)
                        if inc[0] == own:
                            seq += 1
                            if seq in myrank:
                                ins.then_inc(self.sems[own], 1)
                        else:
                            ins.then_inc(self.sems[inc[0]], inc[1])

            @block.tensor
            def _(h):
                run("pe", h)

            @block.scalar
            def _(h):
                run("act", h)

            @block.vector
            def _(h):
                run("dve", h)

            @block.gpsimd
            def _(h):
                run("pool", h)

            @block.sync
            def _(h):
                run("sp", h)
        self.es.close()

    def mm(self, out, lhsT, rhs, start, stop, R, W, tp=None):
        if tp is None:
            self.emit("pe", lambda e: e.matmul(out, lhsT=lhsT, rhs=rhs, start=start, stop=stop), R, W)
        else:
            self.emit("pe", lambda e: e.matmul(out, lhsT=lhsT, rhs=rhs, start=start, stop=stop, tile_position=tp), R, W)

    def tr(self, out, in_, ident, R, W):
        self.emit("pe", lambda e: e.transpose(out=out, in_=in_, identity=ident), R, W)

    def act(self, out, in_, func, R, W, bias=None, scale=None, accum=None):
        kw = {}
        if bias is not None:
            kw["bias"] = bias
        if scale is not None:
            kw["scale"] = scale
        if accum is not None:
            kw["accum_out"] = accum
        self.emit("act", lambda e: e.activation(out=out, in_=in_, func=func, **kw), R, W)

    def tt(self, eng, out, in0, in1, op, R, W):
        self.emit(eng, lambda e: e.tensor_tensor(out=out, in0=in0, in1=in1, op=op), R, W)

    def ts(self, eng, out, in0, s1, s2, op0, op1, R, W, accum=None):
        if s2 is None:
            self.emit(eng, lambda e: e.tensor_scalar(out=out, in0=in0, scalar1=s1, scalar2=None, op0=op0), R, W)
        elif accum is None:
            self.emit(eng, lambda e: e.tensor_scalar(out=out, in0=in0, scalar1=s1, scalar2=s2, op0=op0, op1=op1), R, W)
        else:
            self.emit(eng, lambda e: e.tensor_scalar(out=out, in0=in0, scalar1=s1, scalar2=s2, op0=op0, op1=op1,
                                                     accum_out=accum), R, W)

    def stt(self, eng, out, in0, scalar, in1, op0, op1, R, W):
        self.emit(eng, lambda e: e.scalar_tensor_tensor(out=out, in0=in0, scalar=scalar, in1=in1, op0=op0, op1=op1), R, W)

    def copy(self, eng, out, in_, R, W):
        if eng == "act":
            self.emit("act", lambda e: e.activation(out=out, in_=in_, func=AF.Copy), R, W)
        else:
            self.emit(eng, lambda e: e.tensor_copy(out=out, in_=in_), R, W)

    def memset(self, eng, ap, val, W):
        self.emit(eng, lambda e: e.memset(ap, val), (), W)

    def dma(self, eng, out, in_, sem, R, W):
        self.emit(eng, lambda e: e.dma_start(out=out, in_=in_), R, W, dma_sem=sem)


class Arena:
    def __init__(self, P, nbytes):
        self.t = P.sbuf("arena", [128, nbytes], U8)
        self.n = nbytes
        self.off = 0

    def alloc(self, shape, dt):
        n = int(np.prod(shape)) * DTSZ[dt]
        n_al = (n + 63) // 64 * 64
        assert self.off + n_al <= self.n, f"arena overflow {self.off}+{n_al}>{self.n}"
        v = self.t[:, self.off:self.off + n].bitcast(dt)
        self.off += n_al
        if len(shape) == 2:
            v = v.rearrange("p (a b) -> p a b", a=shape[0])
        elif len(shape) == 3:
            v = v.rearrange("p (a b c) -> p a b c", a=shape[0], b=shape[1])
        elif len(shape) == 4:
            v = v.rearrange("p (a b c d) -> p a b c d", a=shape[0], b=shape[1], c=shape[2])
        return v

    def mark(self):
        return self.off

    def release(self, m):
        self.off = m


ARENA_BYTES = 207 * 1024


def build_program(debug=False, stop_after=None):
    nc = bass.Bass("TRN2", target_bir_lowering=False)
    P = Prog(nc)

    def din(name, shape, dt=F32):
        return nc.dram_tensor(name, list(shape), dt, kind="ExternalInput").ap()

    skind = "ExternalOutput" if debug else "Internal"

    def dscr(name, shape, dt):
        return nc.dram_tensor(name, list(shape), dt, kind=skind).ap()

    x_d = din("x", [L, D])
    ctx_d = din("ctx", [LC, D])
    cvec_d = din("cvec", [128, 8, 2])
    wmod_d = din("w_mod", [D, 6 * D])
    bmod_d = din("b_mod", [128, 48])
    gA_d = din("gA", [128, 8])
    gF_d = din("gF", [128, 8])
    gFin_d = din("gFin", [128, 8])
    win_d = din("w_in_ext", [D, 3072])
    wgate_d = din("w_gate", [D, 2048])
    cos_d = din("rope_cos", [128, L])
    sin_d = din("rope_sin", [128, L])
    tb_d = din("bias_tb", [128, 4, 15, 64])
    mk_d = din("bias_mask", [128, 64])
    lre_d = din("lam_re", [32, 128])
    lim_d = din("lam_im", [32, 128])
    ldt_d = din("log_dt", [32, 2])
    bre_d = din("b_re", [2, 32, 64, 16])
    bim_d = din("b_im", [2, 32, 64, 16])
    cre_d = din("c_re", [2, 32, 16, 64])
    cim_d = din("c_im", [2, 32, 16, 64])
    dsk_d = din("d_skip", [128, 4])
    wglu_d = din("w_glu", [SW, 2 * SW])
    wba_d = din("w_ba", [AW, D])
    wbs_d = din("w_bs", [SW, D])
    wout_d = din("w_out", [D, D])
    wfi_d = din("w_ffn_in", [D, 2 * FH])
    wfo_d = din("w_ffn_out", [FH, D])
    out_d = nc.dram_tensor("out", [L, D], F32, kind="ExternalOutput").ap()

    qT_s = dscr("qT_s", [4, 128, L], BF16)
    uT_s = dscr("uT_s", [4, 128, UEXT], BF16)
    aT_s = dscr("aT_s", [4, 128, L], BF16)
    h_s = dscr("h_s", [L, D], F32)
    y_o = dscr("y_o", [2, 4, 128, L], BF16)
    if debug:
        kT_o = dscr("kT_o", [4, 128, NTOK], BF16)
        v_o = dscr("v_o", [34, 128, 512], BF16)
        mod_o = dscr("mod_o", [128, 96], F32)
        gb_o = dscr("gb_o", [128, 3 * D], F32)

    A = Arena(P, ARENA_BYTES)
    ps = P.psum("ps", [128, 4096], F32)

    def bank(b, n=512, off=0):
        return ps[:, b * 512 + off:b * 512 + off + n]

    def bank_bf(b):
        return ps[:, b * 512:(b + 1) * 512].bitcast(BF16)

    Tb = [T(f"bank{b}") for b in range(8)]

    ident_f = A.alloc([128], F32)
    ident_b = A.alloc([128], BF16)
    ones_f = A.alloc([128], F32)
    modv = A.alloc([48, 2], F32)
    A1l = A.alloc([8], F32)
    A1c = A.alloc([8], F32)
    A2 = A.alloc([8], F32)
    gA = A.alloc([8], F32)
    gF = A.alloc([8], F32)
    gFin = A.alloc([8], F32)
    g1b = A.alloc([D], F32)
    g2b = A.alloc([D], F32)
    fgb = A.alloc([D], F32)
    Tconst = T("const")
    Tdbg = T("dbg")
    Tmod = T("mod")
    Tgb = T("gb")

    P.memset("pool", ident_f, 1.0, [Tconst])
    P.emit("pool", lambda e: e.affine_select(out=ident_f, in_=ident_f, pattern=[[-1, 128]], compare_op=ALU.is_equal,
                                             fill=0.0, base=0, channel_multiplier=1), [Tconst], [Tconst])
    P.copy("pool", ident_b, ident_f, [Tconst], [Tconst])
    P.memset("pool", ones_f, 1.0, [Tconst])

    m0 = A.mark()
    cv = A.alloc([8, 2], F32)
    cvb = A.alloc([8, 2], BF16)
    bmod = A.alloc([48], F32)
    wmb = [A.alloc([8, 1024], BF16) for _ in range(2)]
    Tcv, Tbm = T("cv"), T("bmod")
    Twm = [T("wm0"), T("wm1")]
    P.dma("sp", cv, cvec_d, "ld_cv", [], [Tcv])
    P.dma("sp", bmod, bmod_d, "ld_cv", [], [Tbm])
    P.dma("sp", gA, gA_d, "ld_cv", [], [Tmod])
    P.dma("sp", gF, gF_d, "ld_cv", [], [Tmod])
    P.dma("sp", gFin, gFin_d, "ld_cv", [], [Tmod])
    P.act(cvb, cv, AF.Silu, [Tcv], [Tcv])
    wm_v = wmod_d.rearrange("(k p) m -> p k m", p=128)
    for blk in range(6):
        bi = blk % 2
        P.dma("pool", wmb[bi], wm_v[:, :, blk * 1024:(blk + 1) * 1024], f"ld_wm{bi}", [], [Twm[bi]])
        for ml in range(8):
            m = blk * 8 + ml
            for k in range(8):
                P.mm(bank(0, 2, m * 2), wmb[bi][:, k, ml * 128:(ml + 1) * 128], cvb[:, k, :], k == 0, k == 7,
                     [Twm[bi], Tcv], [Tb[0]])
    P.tt("dve", modv, bank(0, 96).rearrange("p (m j) -> p m j", j=2), bmod.unsqueeze(2).to_broadcast([128, 48, 2]),
         ALU.add, [Tb[0], Tbm], [Tmod])
    for (dst, j) in ((A1l, 0), (A1c, 1)):
        P.ts("dve", dst, modv[:, 8:16, j], 1.0, None, ALU.add, None, [Tmod], [Tmod])
        P.tt("dve", dst, dst, gA, ALU.mult, [Tmod], [Tmod])
    P.ts("dve", A2, modv[:, 32:40, 0], 1.0, None, ALU.add, None, [Tmod], [Tmod])
    P.tt("dve", A2, A2, gF, ALU.mult, [Tmod], [Tmod])
    B1l = modv[:, 0:8, 0]
    B1c = modv[:, 0:8, 1]
    B2 = modv[:, 24:32, 0]
    diag = [A.alloc([128], F32) for _ in range(2)]
    Tdg = [T("dg0"), T("dg1")]
    n = 0
    for (dst, vec) in ((g1b, modv[:, 16:24, 0]), (g2b, modv[:, 40:48, 0]), (fgb, gFin)):
        for k in range(8):
            di = n % 2
            bk = 1 + (n % 2)
            P.ts("dve", diag[di], ident_f, vec[:, k:k + 1], None, ALU.mult, None, [Tmod, Tconst], [Tdg[di]])
            P.mm(bank(bk, 128), ones_f, diag[di], True, True, [Tconst, Tdg[di]], [Tb[bk]])
            P.copy("act", dst[:, k * 128:(k + 1) * 128], bank(bk, 128), [Tb[bk]], [Tgb])
            n += 1
    if debug:
        P.dma("sp", mod_o, modv.rearrange("p m j -> p (m j)"), "dbg", [Tmod], [Tdbg])
        P.dma("sp", gb_o[:, 0:D], g1b, "dbg", [Tgb], [Tdbg])
        P.dma("sp", gb_o[:, D:2 * D], g2b, "dbg", [Tgb], [Tdbg])
        P.dma("sp", gb_o[:, 2 * D:3 * D], fgb, "dbg", [Tgb], [Tdbg])
    P.barrier()
    A.release(m0)
    if stop_after == "p0":
        return finish(P, nc, [Tdbg])

    mA = A.mark()
    kT = A.alloc([4, NTOK], BF16)
    vtok = A.alloc([34, 512], BF16)
    TkT = [T(f"kT{g}") for g in range(9)]
    Tv = [T(f"v{g}") for g in range(9)]
    m1 = A.mark()
    wA = A.alloc([8, 3072], BF16)
    TwA = T("wA")
    win_v = win_d.rearrange("(k p) m -> p k m", p=128)
    for c6 in range(6):
        P.dma("pool", wA[:, :, c6 * 512:(c6 + 1) * 512], win_v[:, :, c6 * 512:(c6 + 1) * 512], "ld_wA", [], [TwA])
    xt = [A.alloc([D], F32) for _ in range(3)]
    Txt = [T(f"xt{i}") for i in range(3)]
    junk = A.alloc([D], BF16)
    Tjunk = T("junk")
    xn = [A.alloc([D], BF16) for _ in range(2)]
    Txn = [T(f"xn{i}") for i in range(2)]
    st = [A.alloc([4], F32) for _ in range(3)]
    Tst = [T(f"st{i}") for i in range(3)]
    nT = [A.alloc([8, 512], BF16) for _ in range(2)]
    TnT = [T("nT0"), T("nT1")]
    ropec = [A.alloc([512], F32) for _ in range(2)]
    ropes = [A.alloc([512], F32) for _ in range(2)]
    Trope = [T("rope0"), T("rope1")]
    rt = [A.alloc([512], F32) for _ in range(4)]
    Trt = [T(f"rt{i}") for i in range(4)]
    qst = [A.alloc([4, 512], BF16) for _ in range(2)]
    Tqst = [T("qst0"), T("qst1")]
    ust = [A.alloc([4, 512], BF16) for _ in range(2)]
    Tust = [T("ust0"), T("ust1")]
    TqS, TuS = T("qT_s"), T("uT_s")

    groups = [("ctx", 0, LC)] + [("lat", g * 512, 512) for g in range(8)]
    tile_no = 0
    mmbank = 0
    for gi, (kind, t0, ntok) in enumerate(groups):
        nb = gi % 2
        ntile = ntok // 128
        src = ctx_d if kind == "ctx" else x_d
        Asc, Bsc = (A1c, B1c) if kind == "ctx" else (A1l, B1l)
        if kind == "lat":
            P.dma("act", ropec[nb], cos_d[:, t0:t0 + 512], f"ld_rope{nb}", [], [Trope[nb]])
            P.dma("act", ropes[nb], sin_d[:, t0:t0 + 512], f"ld_rope{nb}", [], [Trope[nb]])
        for half in range((ntile + 1) // 2):
            tb_ = 0 if (gi * 2 + half) % 2 == 0 else 2
            pT = ps[:, tb_ * 512:(tb_ + 2) * 512].bitcast(BF16).rearrange("p (c t) -> p c t", c=8)
            Tp = Tb[tb_]
            for tl in range(2):
                ti = half * 2 + tl
                xi = tile_no % 3
                si = tile_no % 3
                ni = tile_no % 2
                tile_no += 1
                P.dma("sp", xt[xi], src[t0 + ti * 128:t0 + (ti + 1) * 128, :], f"ld_x{xi}", [], [Txt[xi]])
                P.act(junk, xt[xi], AF.Square, [Txt[xi]], [Tjunk, Tst[si]], accum=st[si][:, 0:1])
                rstd_ops(P, st[si], Tst[si])
                P.act(xn[ni], xt[xi], AF.Copy, [Txt[xi], Tst[si]], [Txn[ni]], scale=st[si][:, 2:3])
                for c in range(8):
                    P.tr(pT[:, c, tl * 128:(tl + 1) * 128], xn[ni][:, c * 128:(c + 1) * 128], ident_b,
                         [Txn[ni], Tconst], [Tp])
            for c in range(8):
                P.ts("dve", nT[nb][:, c, half * 256:(half + 1) * 256], pT[:, c, :], Asc[:, c:c + 1], Bsc[:, c:c + 1],
                     ALU.mult, ALU.add, [Tp, Tmod], [TnT[nb]])
        cols = slice(0, ntok)

        def proj(mcol, bk):
            for k in range(8):
                P.mm(bank(bk, ntok), wA[:, k, mcol:mcol + 128], nT[nb][:, k, cols], k == 0, k == 7,
                     [TwA, TnT[nb]], [Tb[bk]])

        kcol0 = L if kind == "ctx" else t0
        for j in range(4):
            if kind == "ctx":
                bk = 4 + mmbank % 4
                mmbank += 1
                proj(512 + j * 128, bk)
                P.copy("act", kT[:, j, kcol0:kcol0 + ntok], bank(bk, ntok), [Tb[bk]], [TkT[gi]])
            else:
                for (base, swb, dst, Td, isq) in ((0, 2048, qst[nb], Tqst[nb], True), (512, 2560, None, TkT[gi], False)):
                    bk = 4 + mmbank % 4
                    bk2 = 4 + (mmbank + 1) % 4
                    mmbank += 2
                    proj(base + j * 128, bk)
                    proj(swb + j * 128, bk2)
                    rsel = (mmbank // 2) % 2 * 2
                    r0, r1 = rt[rsel], rt[rsel + 1]
                    Tr0, Tr1 = Trt[rsel], Trt[rsel + 1]
                    P.tt("dve", r0, bank(bk), ropec[nb], ALU.mult, [Tb[bk], Trope[nb]], [Tr0])
                    P.tt("dve", r1, bank(bk2), ropes[nb], ALU.mult, [Tb[bk2], Trope[nb]], [Tr1])
                    o = qst[nb][:, j, :] if isq else kT[:, j, t0:t0 + 512]
                    P.tt("pool", o, r0, r1, ALU.add, [Tr0, Tr1], [Td])
            bk = 4 + mmbank % 4
            mmbank += 1
            proj(1536 + j * 128, bk)
            P.copy("act", ust[nb][:, j, cols], bank(bk, ntok), [Tb[bk]], [Tust[nb]])
        for ti in range(ntile):
            bk = 4 + mmbank % 4
            mmbank += 1
            for k in range(8):
                P.mm(bank(bk), nT[nb][:, k, ti * 128:(ti + 1) * 128], wA[:, k, 1024:1536], k == 0, k == 7,
                     [TwA, TnT[nb]], [Tb[bk]])
            vt = (32 + ti) if kind == "ctx" else (t0 // 128 + ti)
            P.copy("act", vtok[:, vt, :], bank(bk), [Tb[bk]], [Tv[gi]])
        if kind == "lat":
            P.dma("sp", qT_s[:, :, t0:t0 + 512].rearrange("c p t -> p c t"), qst[nb], f"st_q{nb}", [Tqst[nb]], [TqS])
            P.dma("sp", uT_s[:, :, LC + t0:LC + t0 + 512].rearrange("c p t -> p c t"), ust[nb], f"st_u{nb}",
                  [Tust[nb]], [TuS])
        else:
            P.dma("sp", uT_s[:, :, 0:LC].rearrange("c p t -> p c t"), ust[nb][:, :, 0:LC], f"st_u{nb}", [Tust[nb]], [TuS])
            P.dma("sp", uT_s[:, :, LC + L:UEXT].rearrange("c p t -> p c t"), ust[nb][:, :, 0:LC], f"st_u{nb}",
                  [Tust[nb]], [TuS])
    if debug:
        P.dma("sp", kT_o.rearrange("c p t -> p c t"), kT, "dbg", TkT, [Tdbg])
        P.dma("sp", v_o.rearrange("n p f -> p n f"), vtok, "dbg", Tv, [Tdbg])
    P.barrier()
    A.release(m1)
    if stop_after == "pA":
        return finish(P, nc, [TqS, TuS, Tdbg])

    PI = math.pi
    s_ar = A.alloc([32], F32)
    s_ai = A.alloc([32], F32)
    s_nai = A.alloc([32], F32)
    a16r = A.alloc([32], F32)
    a16i = A.alloc([32], F32)
    Wt = A.alloc([2, 4, 2, 128], BF16)
    CwT = A.alloc([32, 2, 32], BF16)
    Eb = A.alloc([2, NCH, 32], F32)
    hst = [[A.alloc([32], F32) for _ in range(2)] for _ in range(2)]
    ct = [A.alloc([32], F32) for _ in range(4)]
    Tprm, TWt, TCw, TE = T("s5prm"), T("Wt"), T("CwT"), T("E")
    TyT = [[T(f"yT{d}{c}") for c in range(4)] for d in range(2)]
    mS1 = A.mark()
    L32 = A.alloc([3, 128], F32)
    ldt = A.alloc([2], F32)
    prm = A.alloc([3, 32], F32)
    w32 = [A.alloc([32], F32) for _ in range(12)]
    negpi = A.alloc([1], F32)
    qint = A.alloc([32], I32)
    breT = A.alloc([32, 16], F32)
    bimT = A.alloc([32, 16], F32)
    bbr = A.alloc([32, 16], F32)
    bbi = A.alloc([32, 16], F32)
    btmp = A.alloc([32, 16], F32)
    srcw = [A.alloc([128], F32) for _ in range(8)]
    Cblk1 = A.alloc([32, 128], F32)
    Cblk = [Cblk1, Cblk1]
    TCb1 = T("Cb")
    TL32, Tbb, Tsrc, TCb = T("L32"), T("bb"), [T(f"srcw{i}") for i in range(8)], [TCb1, TCb1]
    P.dma("sp", L32[0:32, 0, :], lre_d, "ld_s5", [], [TL32])
    P.dma("sp", L32[0:32, 1, :], lim_d, "ld_s5", [], [TL32])
    P.dma("sp", ldt[0:32, :], ldt_d, "ld_s5", [], [TL32])
    P.dma("sp", breT, bre_d.rearrange("d (j g) p c -> (g p) (d j) c", g=2), "ld_s5b", [], [Tbb])
    P.dma("sp", bimT, bim_d.rearrange("d (j g) p c -> (g p) (d j) c", g=2), "ld_s5b", [], [Tbb])
    def load_cblk(ri):
        cd = (cre_d, cim_d)[ri]
        if ri == 0:
            P.memset("pool", Cblk1, 0.0, [TCb1])
        cv_ = cd.rearrange("d (j g) h p -> g h (d j) p", g=2)
        for g2 in range(2):
            P.dma("sp", Cblk1[g2 * 16:(g2 + 1) * 16, :, g2 * 64:(g2 + 1) * 64], cv_[g2], "ld_s5c", [], [TCb1])

    load_cblk(0)
    P.memset("pool", negpi, -PI, [Tprm])
    for i in range(8):
        P.memset("pool", srcw[i], 0.0, [Tsrc[i]])
    P.copy("dve", L32[0:32, 2, :].rearrange("p (g q) -> p g q", g=2), ldt[0:32, :].unsqueeze(2).to_broadcast([32, 2, 64]),
           [TL32], [TL32])
    for i in range(3):
        P.mm(bank(0, 32, i * 32), L32[0:32, i, :], ident_f[0:32, 0:32], True, True, [TL32, Tconst], [Tb[0]])
    P.copy("dve", prm.rearrange("p a b -> p (a b)"), bank(0, 96), [Tb[0]], [Tprm])
    lre, lim, ldtT = prm[:, 0, :], prm[:, 1, :], prm[:, 2, :]
    dt_, tmp_, mag, ang, rs_, sin_, cos_, den, rden, nr, fr, fi = w32
    RW = ([Tprm], [Tprm])
    P.act(dt_, ldtT, AF.Exp, *RW)
    P.tt("dve", tmp_, lre, dt_, ALU.mult, *RW)
    P.act(mag, tmp_, AF.Exp, *RW)
    P.tt("dve", ang, lim, dt_, ALU.mult, *RW)
    def sin_of(dst, shift):
        P.ts("dve", rs_, ang, shift, 1.0 / (2 * PI), ALU.add, ALU.mult, *RW)
        P.copy("dve", qint, rs_, *RW)
        P.copy("dve", rden, qint, *RW)
        P.ts("dve", rs_, ang, shift, None, ALU.add, None, *RW)
        P.stt("dve", rs_, rden, -2 * PI, rs_, ALU.mult, ALU.add, *RW)
        P.ts("dve", rden, rs_, PI, None, ALU.is_gt, None, *RW)
        P.stt("dve", rs_, rden, -2 * PI, rs_, ALU.mult, ALU.add, *RW)
        P.ts("dve", rden, rs_, -PI, None, ALU.is_lt, None, *RW)
        P.stt("dve", rs_, rden, 2 * PI, rs_, ALU.mult, ALU.add, *RW)
        P.act(dst, rs_, AF.Sin, *RW)

    sin_of(sin_, 0.0)
    sin_of(cos_, 0.5 * PI)
    P.tt("dve", s_ar, mag, cos_, ALU.mult, *RW)
    P.tt("dve", s_ai, mag, sin_, ALU.mult, *RW)
    P.ts("dve", s_nai, s_ai, -1.0, None, ALU.mult, None, *RW)
    P.tt("dve", den, lre, lre, ALU.mult, *RW)
    P.tt("dve", tmp_, lim, lim, ALU.mult, *RW)
    P.tt("dve", den, den, tmp_, ALU.add, *RW)
    P.emit("dve", lambda e: e.reciprocal(out=rden, in_=den), *RW)
    P.ts("dve", nr, s_ar, -1.0, None, ALU.add, None, *RW)
    P.tt("dve", fr, nr, lre, ALU.mult, *RW)
    P.tt("dve", tmp_, s_ai, lim, ALU.mult, *RW)
    P.tt("dve", fr, fr, tmp_, ALU.add, *RW)
    P.tt("dve", fr, fr, rden, ALU.mult, *RW)
    P.tt("dve", fi, s_ai, lre, ALU.mult, *RW)
    P.tt("dve", tmp_, nr, lim, ALU.mult, *RW)
    P.tt("dve", fi, fi, tmp_, ALU.subtract, *RW)
    P.tt("dve", fi, fi, rden, ALU.mult, *RW)
    P.copy("dve", a16r, s_ar, *RW)
    P.copy("dve", a16i, s_ai, *RW)
    for _ in range(4):
        P.tt("dve", tmp_, a16r, a16r, ALU.mult, *RW)
        P.tt("dve", den, a16i, a16i, ALU.mult, *RW)
        P.tt("dve", nr, a16r, a16i, ALU.mult, *RW)
        P.tt("dve", a16r, tmp_, den, ALU.subtract, *RW)
        P.ts("dve", a16i, nr, 2.0, None, ALU.mult, None, *RW)
    frb = fr.unsqueeze(2).to_broadcast([128, 32, 16])
    fib = fi.unsqueeze(2).to_broadcast([128, 32, 16])
    Rb = [Tprm, Tbb]
    P.tt("dve", bbr, breT, frb, ALU.mult, Rb, [Tbb])
    P.tt("dve", btmp, bimT, fib, ALU.mult, Rb, [Tbb])
    P.tt("dve", bbr, bbr, btmp, ALU.subtract, Rb, [Tbb])
    P.tt("dve", bbi, bimT, frb, ALU.mult, Rb, [Tbb])
    P.tt("dve", btmp, breT, fib, ALU.mult, Rb, [Tbb])
    P.tt("dve", bbi, bbi, btmp, ALU.add, Rb, [Tbb])
    n = 0
    for d in range(2):
        for j in range(16):
            dj = d * 16 + j
            cj, jm = j // 4, j % 4
            for ri, bb_ in enumerate((bbr, bbi)):
                sw = srcw[ri * 4 + jm]
                Ts_ = Tsrc[ri * 4 + jm]
                for g2 in range(2):
                    P.copy("dve", sw[g2 * 64:(g2 + 1) * 64, 32 * jm + 16 * g2:32 * jm + 16 * g2 + 16],
                           bb_[g2 * 64:(g2 + 1) * 64, dj, :], [Tbb], [Ts_])
                bk = 1 + n % 2
                n += 1
                P.mm(bank(bk, 128), sw, ident_f, True, True, [Ts_, Tconst], [Tb[bk]])
                P.copy("act", Wt[32 * jm:32 * jm + 32, d, cj, ri, :], bank(bk, 128)[32 * jm:32 * jm + 32, :], [Tb[bk]], [TWt])
    for ri in range(2):
        if ri == 1:
            load_cblk(1)
        for dj in range(32):
            bk = 1 + n % 2
            n += 1
            P.mm(bank(bk, 32), Cblk1[0:32, dj, :], ident_f[0:32, 0:32], True, True, [TCb1, Tconst], [Tb[bk]])
            if ri == 0:
                P.copy("act", CwT[:, dj, 0, :], bank(bk, 32), [Tb[bk]], [TCw])
            else:
                P.act(CwT[:, dj, 1, :], bank(bk, 32), AF.Copy, [Tb[bk]], [TCw], scale=-1.0)
    P.barrier()
    A.release(mS1)

    mP = A.mark()

    def useq(u, jm, d, i):
        if d == 0:
            return u[32 * jm:32 * jm + 32, i:NTOK:TCH]
        return u[32 * jm:32 * jm + 32, UEXT - 1 - i:LC - 1:-TCH]

    def scan_pass(phase2, yT=None):
        uch = [A.alloc([UEXT], BF16) for _ in range(2)]
        Tuch = [T("uch0"), T("uch1")]
        Hs = [[A.alloc([NCH], F32) for _ in range(2)] for _ in range(4)]
        THs = [T(f"H{i}") for i in range(4)]
        tm = [[A.alloc([NCH], F32) for _ in range(2)] for _ in range(2)]
        Ttm = [T("tm0"), T("tm1")]
        Hb = [[[A.alloc([NCH], BF16) for _ in range(2)] for _ in range(4)] for _ in range(2)]
        THb = [[T(f"Hb{b}{i}") for i in range(4)] for b in range(2)]
        steps = [(cj, d, i, jm) for cj in range(4) for d in range(2) for i in range(TCH) for jm in range(4)]
        N = len(steps)

        def load_u(cj):
            ub = cj % 2
            P.dma("sp", uch[ub], uT_s[cj], f"ld_u{ub}_{int(phase2)}", [TuS], [Tuch[ub]])

        NSET = 3 if phase2 else 4

        def bu_banks(n):
            sidx = n % NSET
            return bank(2 * sidx, NCH), bank(2 * sidx + 1, NCH), Tb[2 * sidx]

        def emitBU(n):
            cj, d, i, jm = steps[n]
            ub = cj % 2
            rhs = useq(uch[ub], jm, d, i)
            BUr, BUi, TB = bu_banks(n)
            for ri, bk_ in enumerate((BUr, BUi)):
                P.mm(bk_, Wt[32 * jm:32 * jm + 32, d, cj, ri, :], rhs, True, True, [TWt, Tuch[ub]], [TB], tp=(32 * jm, 0))

        def t_ops(n):
            cj, d, i, jm = steps[n]
            dj = d * 16 + cj * 4 + jm
            BUr, BUi, TB = bu_banks(n)
            Hr, Hi = Hs[jm]
            TH = THs[jm]
            Er, Ei = Eb[:, 0, :, dj], Eb[:, 1, :, dj]
            if i == 0 and not phase2:
                P.copy("dve", Hr, BUr, [TB], [TH])
                P.copy("dve", Hi, BUi, [TB], [TH])
                return
            if i == 0:
                inr, ini, Rin = Er, Ei, [TE]
            else:
                inr, ini, Rin = Hr, Hi, [TH]
            t1, t2 = tm[jm % 2]
            Tt = Ttm[jm % 2]
            P.stt("dve", t1, ini, s_nai[:, dj:dj + 1], BUr, ALU.mult, ALU.add, Rin + [TB, Tprm], [Tt])
            P.stt("dve", t2, inr, s_ai[:, dj:dj + 1], BUi, ALU.mult, ALU.add, Rin + [TB, Tprm], [Tt])

        def o_ops(n):
            cj, d, i, jm = steps[n]
            dj = d * 16 + cj * 4 + jm
            stepno = n // 4
            ybk = 6 + stepno % 2
            Hr, Hi = Hs[jm]
            TH = THs[jm]
            Er, Ei = Eb[:, 0, :, dj], Eb[:, 1, :, dj]
            if i == 0 and not phase2:
                return
            if i == 0:
                inr, ini, Rin = Er, Ei, [TE]
            else:
                inr, ini, Rin = Hr, Hi, [TH]
            last1 = (i == TCH - 1) and not phase2
            outr, outi, Wout = (Er, Ei, [TE]) if last1 else (Hr, Hi, [TH])
            t1, t2 = tm[jm % 2]
            Tt = Ttm[jm % 2]
            P.stt("dve", outr, inr, s_ar[:, dj:dj + 1], t1, ALU.mult, ALU.add, Rin + [Tt, Tprm], Wout)
            P.stt("dve", outi, ini, s_ar[:, dj:dj + 1], t2, ALU.mult, ALU.add, Rin + [Tt, Tprm], Wout)
            if phase2:
                hb = stepno % 2
                Hbr, Hbi = Hb[hb][jm]
                P.copy("act", Hbr, Hr, [TH], [THb[hb][jm]])
                P.copy("act", Hbi, Hi, [TH], [THb[hb][jm]])
                yps = bank(ybk, NCH)[32 * jm:32 * jm + 32, :]
                P.mm(yps, CwT[:, dj, 0, :], Hbr, True, False, [TCw, THb[hb][jm]], [Tb[ybk]], tp=(0, 32 * jm))
                P.mm(yps, CwT[:, dj, 1, :], Hbi, False, True, [TCw, THb[hb][jm]], [Tb[ybk]], tp=(0, 32 * jm))
                if jm == 3:
                    if d == 0:
                        dst = yT[:, 0, cj, i:L:TCH]
                    else:
                        dst = yT[:, 1, cj, L - 1 - i:None:-TCH]
                    P.copy("act", dst, bank(ybk, NCH)[:, LC // TCH:NCH], [Tb[ybk]], [TyT[d][cj]])

        load_u(0)
        load_u(1)
        nbu = 0
        while nbu < min(NSET, N):
            emitBU(nbu)
            nbu += 1
        for k in range(N // 2):
            n0 = 2 * k
            cj, d, i, jm = steps[n0]
            if d == 0 and i == 0 and jm == 0 and 1 <= cj and cj + 1 < 4:
                load_u(cj + 1)
            while nbu <= n0 + 1:
                emitBU(nbu)
                nbu += 1
            t_ops(n0)
            t_ops(n0 + 1)
            while nbu < N and nbu <= n0 + 1 + NSET:
                emitBU(nbu)
                nbu += 1
            o_ops(n0)
            o_ops(n0 + 1)

    scan_pass(False)
    P.barrier()
    A.release(mP)
    if stop_after == "pS1":
        return finish(P, nc, [TqS, TuS, Tdbg])
    Th_ = [T("ch0"), T("ch1")]
    Tc_ = [T(f"cc{i}") for i in range(4)]
    P.memset("pool", hst[0][0], 0.0, [Th_[0]])
    P.memset("pool", hst[0][1], 0.0, [Th_[0]])
    for s_ in range(NCH):
        hr, hi = hst[s_ % 2]
        nhr, nhi = hst[(s_ + 1) % 2]
        Tho, Thn_ = Th_[s_ % 2], Th_[(s_ + 1) % 2]
        P.tt("pool", ct[0], a16r, hr, ALU.mult, [Tho, Tprm], [Tc_[0]])
        P.tt("pool", ct[1], a16i, hi, ALU.mult, [Tho, Tprm], [Tc_[1]])
        P.tt("pool", ct[2], a16r, hi, ALU.mult, [Tho, Tprm], [Tc_[2]])
        P.tt("pool", ct[3], a16i, hr, ALU.mult, [Tho, Tprm], [Tc_[3]])
        P.tt("pool", ct[0], ct[0], ct[1], ALU.subtract, [Tc_[0], Tc_[1]], [Tc_[0]])
        P.tt("pool", ct[2], ct[2], ct[3], ALU.add, [Tc_[2], Tc_[3]], [Tc_[2]])
        P.tt("pool", nhr, ct[0], Eb[:, 0, s_, :], ALU.add, [Tc_[0], TE], [Thn_])
        P.tt("pool", nhi, ct[2], Eb[:, 1, s_, :], ALU.add, [Tc_[2], TE], [Thn_])
        P.copy("pool", Eb[:, 0, s_, :], hr, [Tho], [TE])
        P.copy("pool", Eb[:, 1, s_, :], hi, [Tho], [TE])
    if stop_after == "pCH":
        return finish(P, nc, [TqS, TuS, Tdbg])
    mB = A.mark()
    tbias = A.alloc([4, 15, 64], F32)
    mk = A.alloc([64], F32)
    Ttb = T("tbias")
    P.dma("act", tbias, tb_d, "ld_tb", [], [Ttb])
    P.dma("act", mk, mk_d, "ld_tb", [], [Ttb])
    for j in range(4):
        P.tt("dve", tbias[:, j], tbias[:, j], mk.unsqueeze(1).to_broadcast([128, 15, 64]), ALU.add, [Ttb], [Ttb])
    NQB = 3
    qbd = [A.alloc([4, 128], BF16) for _ in range(NQB)]
    Tqbd = [T(f"qbd{i}") for i in range(NQB)]
    for i in range(NQB):
        P.memset("dve", qbd[i], 0.0, [Tqbd[i]])
    NS = 3
    sl = [A.alloc([512], F32) for _ in range(NS)]
    Tsl = [T(f"sl{i}") for i in range(NS)]
    Pt = [A.alloc([896], BF16) for _ in range(NS)]
    TPt = [T(f"Pt{i}") for i in range(NS)]
    for i in range(NS):
        P.memset("dve", Pt[i], 0.0, [TPt[i]])
    PTs = [A.alloc([896], BF16) for _ in range(NS)]
    TPTs = [T(f"PTs{i}") for i in range(NS)]
    On = [A.alloc([128], BF16) for _ in range(NS)]
    TOn = [T(f"On{i}") for i in range(NS)]
    NSM = 6
    sm = [A.alloc([8], F32) for _ in range(NSM)]
    Tsm = [T(f"sm{i}") for i in range(NSM)]
    ast = [A.alloc([4, 512], BF16) for _ in range(2)]
    Tast = [T("ast0"), T("ast1")]
    TaS = T("aT_s")
    TSc = [T(f"Sc{i}") for i in range(4)]
    TO = [T(f"O{i}") for i in range(3)]
    TOT = [T("OT0"), T("OT1")]
    iters = [(r, j) for r in range(64) for j in range(4)]
    NIT = len(iters)

    def ainfo(n):
        r, j = iters[n]
        rs = min(max(r - 4, 0), 56)
        g0 = 1 + (rs * 64) // 512
        g1 = 1 + (rs * 64 + 511) // 512
        Rk = [TkT[0], TkT[g0]] + ([TkT[g1]] if g1 != g0 else [])
        Rv = [Tv[0], Tv[g0]] + ([Tv[g1]] if g1 != g0 else [])
        if rs % 2 == 0:
            blocks = [(64 + 128 * b, 0, 128, rs // 2 + b) for b in range(4)]
        else:
            vb = (rs - 1) // 2
            blocks = [(0, 64, 128, vb)] + [(128 * b, 0, 128, vb + b) for b in (1, 2, 3)] + [(512, 0, 64, vb + 4)]
        blocks = blocks + [(640, 0, 128, 32), (768, 0, 128, 33)]
        return r, j, rs, Rk, Rv, blocks

    def load_q(r):
        qi = r % NQB
        for hh in range(2):
            P.dma("sp", qbd[qi][hh * 64:(hh + 1) * 64, :, hh * 64:(hh + 1) * 64],
                  qT_s[:, hh * 64:(hh + 1) * 64, r * 64:r * 64 + 64].rearrange("c p t -> p c t"), f"ld_q{qi}", [TqS], [Tqbd[qi]])

    def S_c_of(n):
        return bank(2 + n % 2, 256), Tb[2 + n % 2]

    def stA(n):
        r, j, rs, Rk, Rv, blocks = ainfo(n)
        if j == 0 and r + 1 < 64:
            load_q(r + 1)
        qi = r % NQB
        S_c, Tc = S_c_of(n)
        P.mm(bank(n % 2), qbd[qi][:, j, :], kT[:, j, rs * 64:rs * 64 + 512], True, True, [Tqbd[qi]] + Rk, [Tb[n % 2]])
        P.mm(S_c, qbd[qi][:, j, :], kT[:, j, L:L + LC], True, True, [Tqbd[qi]] + Rk, [Tc])

    def stB(n):
        r, j, rs, Rk, Rv, blocks = ainfo(n)
        d0 = rs - r + 7
        S_c, Tc = S_c_of(n)
        smt, Ts = sm[n % NSM], Tsm[n % NSM]
        s_, Tsl_ = sl[n % NS], Tsl[n % NS]
        P.stt("dve", s_, bank(n % 2), 0.125, tbias[:, j, d0:d0 + 8, :].rearrange("p a b -> p (a b)"), ALU.mult, ALU.add,
              [Tb[n % 2], Ttb], [Tsl_])
        P.emit("dve", lambda e, o=smt[:, 0:1], i=s_: e.reduce_max(out=o, in_=i, axis=AX.X), [Tsl_], [Ts])
        P.emit("dve", lambda e, o=smt[:, 1:2], i=S_c: e.reduce_max(out=o, in_=i, axis=AX.X), [Tc], [Ts])
        P.stt("dve", smt[:, 2:3], smt[:, 1:2], 0.125, smt[:, 0:1], ALU.mult, ALU.max, [Ts], [Ts])
        P.ts("dve", smt[:, 3:4], smt[:, 2:3], -1.0, None, ALU.mult, None, [Ts], [Ts])

    def stC(n):
        S_c, Tc = S_c_of(n)
        smt, Ts = sm[n % NSM], Tsm[n % NSM]
        s_, Tsl_ = sl[n % NS], Tsl[n % NS]
        P.act(Pt[n % NS][:, 64:576], s_, AF.Exp, [Tsl_, Ts], [TPt[n % NS], Ts], bias=smt[:, 3:4], accum=smt[:, 4:5])
        P.act(Pt[n % NS][:, 640:896], S_c, AF.Exp, [Tc, Ts], [TPt[n % NS], Ts], bias=smt[:, 3:4], scale=0.125,
              accum=smt[:, 5:6])

    def stD(n):
        r, j, rs, Rk, Rv, blocks = ainfo(n)
        smt, Ts = sm[n % NSM], Tsm[n % NSM]
        P.tt("dve", smt[:, 6:7], smt[:, 4:5], smt[:, 5:6], ALU.add, [Ts], [Ts])
        P.emit("dve", lambda e, o=smt[:, 7:8], i=smt[:, 6:7]: e.reciprocal(out=o, in_=i), [Ts], [Ts])
        PTp = bank_bf(4 + n % 2)
        nblk = len(blocks)
        for bi, (col, p0, p1, vt) in enumerate(blocks):
            P.tr(PTp[:, bi * 128:(bi + 1) * 128], Pt[n % NS][:, col:col + 128], ident_b, [TPt[n % NS], Tconst], [Tb[4 + n % 2]])
        P.copy("act", PTs[n % NS][:, 0:nblk * 128], PTp[:, 0:nblk * 128], [Tb[4 + n % 2]], [TPTs[n % NS]])

    def stE(n):
        r, j, rs, Rk, Rv, blocks = ainfo(n)
        smt, Ts = sm[n % NSM], Tsm[n % NSM]
        O = bank(6, 128)
        nblk = len(blocks)
        for bi, (col, p0, p1, vt) in enumerate(blocks):
            P.mm(O, PTs[n % NS][p0:p1, bi * 128:(bi + 1) * 128], vtok[p0:p1, vt, j * 128:(j + 1) * 128], bi == 0,
                 bi == nblk - 1, [TPTs[n % NS]] + Rv, [Tb[6]])
        P.ts("dve", On[n % NS], O, smt[:, 7:8], None, ALU.mult, None, [Tb[6], Ts], [TOn[n % NS]])

    def stF(n):
        r, j, rs, Rk, Rv, blocks = ainfo(n)
        gsel = (r // 8) % 2
        OT = bank_bf(7)[:, 0:128]
        P.tr(OT, On[n % NS], ident_b, [TOn[n % NS], Tconst], [Tb[7]])
        cc = (r % 8) * 64
        P.copy("act", ast[gsel][0:64, j, cc:cc + 64], OT[0:64, 0:64], [Tb[7]], [Tast[gsel]])
        P.copy("act", ast[gsel][64:128, j, cc:cc + 64], OT[64:128, 64:128], [Tb[7]], [Tast[gsel]])
        if r % 8 == 7 and j == 3:
            g = r // 8
            P.dma("sp", aT_s[:, :, g * 512:(g + 1) * 512].rearrange("c p t -> p c t"), ast[gsel], f"st_a{gsel}",
                  [Tast[gsel]], [TaS])

    load_q(0)
    sched = ((stA, 0), (stB, 0), (stC, 1), (stD, 2), (stE, 3), (stF, 4))
    for t in range(NIT + 4):
        for fn, off in sched:
            if 0 <= t - off < NIT:
                fn(t - off)
    P.barrier()
    A.release(mB)
    if stop_after == "pB":
        return finish(P, nc, [TqS, TuS, TaS, Tdbg])

    cur_ = A.off
    A.off = mA
    yT = A.alloc([2, 4, L], BF16)
    assert A.off <= m1
    A.off = cur_
    scan_pass(True, yT)
    TyS = T("y_s")
    for d in range(2):
        P.dma("sp", y_o[d].rearrange("c p t -> p c t"), yT[:, d], "st_y", TyT[d], [TyS])
    P.barrier()
    A.release(mA)
    if stop_after == "pS":
        return finish(P, nc, [TqS, TuS, TaS, TyS, Tdbg])

    wG = A.alloc([8, 2048], BF16)
    wBA = A.alloc([4, 1024], BF16)
    wBS = A.alloc([4, 1024], BF16)
    wGLU = A.alloc([4, 1024], BF16)
    wO = A.alloc([8, 1024], BF16)
    dsk = A.alloc([4], F32)
    TwC = T("wC")
    wg_v = wgate_d.rearrange("(k p) m -> p k m", p=128)
    for c4 in range(4):
        P.dma("pool", wG[:, :, c4 * 512:(c4 + 1) * 512], wg_v[:, :, c4 * 512:(c4 + 1) * 512], "ld_wC", [], [TwC])
    P.dma("pool", wGLU, wglu_d.rearrange("(k p) m -> p k m", p=128), "ld_wC", [], [TwC])
    P.dma("pool", wBA, wba_d.rearrange("(k p) m -> p k m", p=128), "ld_wC", [], [TwC])
    P.dma("pool", wBS, wbs_d.rearrange("(k p) m -> p k m", p=128), "ld_wC", [], [TwC])
    wo_v = wout_d.rearrange("(k p) m -> p k m", p=128)
    for c2 in range(2):
        P.dma("pool", wO[:, :, c2 * 512:(c2 + 1) * 512], wo_v[:, :, c2 * 512:(c2 + 1) * 512], "ld_wC", [], [TwC])
    P.dma("act", dsk, dsk_d, "ld_dsk", [], [TwC])
    NX = 8
    xt = [A.alloc([D], F32) for _ in range(NX)]
    Txt = [T(f"cxt{i}") for i in range(NX)]
    junk = A.alloc([D], BF16)
    Tjunk = T("cjunk")
    xn = [A.alloc([D], BF16) for _ in range(2)]
    Txn = [T(f"cxn{i}") for i in range(2)]
    st = [A.alloc([4], F32) for _ in range(NX)]
    Tst = [T(f"cst{i}") for i in range(NX)]
    nT = [A.alloc([8, 512], BF16) for _ in range(2)]
    TnT = [T("cnT0"), T("cnT1")]
    aTg = [A.alloc([4, 512], BF16) for _ in range(2)]
    uTg = [A.alloc([4, 512], BF16) for _ in range(2)]
    yg = [A.alloc([2, 4, 512], BF16)] * 2
    Tag = [T("aTg0"), T("aTg1")]
    Tug = [T("uTg0"), T("uTg1")]
    Tyg = [T("yg0")] * 2
    zT = A.alloc([4, 512], BF16)
    sT = A.alloc([4, 512], BF16)
    mT = A.alloc([8, 512], BF16)
    TzT, TsT, TmT = T("zT"), T("sT"), T("mT")
    ft = [A.alloc([512], F32) for _ in range(10)]
    Tft = [T(f"ft{i}") for i in range(10)]
    ThS = T("h_s")
    bcyc = [0]

    def nb_():
        b = 2 + bcyc[0] % 6
        bcyc[0] += 1
        return b

    xcnt_ = [0]
    xis_all = {}

    def c1_front(g):
        nb = g % 2
        t0 = g * 512
        P.dma("act", aTg[nb], aT_s[:, :, t0:t0 + 512].rearrange("c p t -> p c t"), f"ld_ag{nb}", [TaS], [Tag[nb]])
        P.dma("act", uTg[nb], uT_s[:, :, LC + t0:LC + t0 + 512].rearrange("c p t -> p c t"), f"ld_ug{nb}", [TuS], [Tug[nb]])
        for d in range(2):
            P.dma("act", yg[nb][:, d], y_o[d, :, :, t0:t0 + 512].rearrange("c p t -> p c t"), f"ld_yg{nb}", [TyS], [Tyg[nb]])
        xis = []
        for half in range(2):
            pT = ps[:, 0:1024].bitcast(BF16).rearrange("p (c t) -> p c t", c=8)
            for tl in range(2):
                ti = half * 2 + tl
                xi = xcnt_[0] % NX
                ni = xcnt_[0] % 2
                xcnt_[0] += 1
                xis.append(xi)
                P.dma("sp", xt[xi], x_d[t0 + ti * 128:t0 + (ti + 1) * 128, :], f"ld_cx{xi}", [], [Txt[xi]])
                P.act(junk, xt[xi], AF.Square, [Txt[xi]], [Tjunk, Tst[xi]], accum=st[xi][:, 0:1])
                rstd_ops(P, st[xi], Tst[xi])
                P.act(xn[ni], xt[xi], AF.Copy, [Txt[xi], Tst[xi]], [Txn[ni]], scale=st[xi][:, 2:3])
                for c in range(8):
                    P.tr(pT[:, c, tl * 128:(tl + 1) * 128], xn[ni][:, c * 128:(c + 1) * 128], ident_b, [Txn[ni], Tconst], [Tb[0]])
            for c in range(8):
                P.ts("dve", nT[nb][:, c, half * 256:(half + 1) * 256], pT[:, c, :], A1l[:, c:c + 1], B1l[:, c:c + 1],
                     ALU.mult, ALU.add, [Tb[0], Tmod], [TnT[nb]])
        for c in range(4):
            f0, f1 = ft[0], ft[1]
            P.tt("dve", f0, yg[nb][:, 0, c, :], yg[nb][:, 1, c, :], ALU.add, [Tyg[nb]], [Tft[0]])
            P.stt("dve", f0, uTg[nb][:, c, :], dsk[:, c:c + 1], f0, ALU.mult, ALU.add, [Tug[nb], TwC, Tft[0]], [Tft[0]])
            P.tt("dve", f1, f0, f0, ALU.mult, [Tft[0]], [Tft[1]])
            P.ts("dve", f1, f1, 0.044715, 1.0, ALU.mult, ALU.add, [Tft[1]], [Tft[1]])
            P.tt("dve", f1, f1, f0, ALU.mult, [Tft[0], Tft[1]], [Tft[1]])
            P.act(f1, f1, AF.Sigmoid, [Tft[1]], [Tft[1]], scale=1.5957691216057308)
            P.tt("dve", zT[:, c, :], f0, f1, ALU.mult, [Tft[0], Tft[1]], [TzT])
        for m in range(4):
            bv, bg = nb_(), nb_()
            for k in range(4):
                P.mm(bank(bv), wGLU[:, k, m * 128:(m + 1) * 128], zT[:, k, :], k == 0, k == 3, [TwC, TzT], [Tb[bv]])
            for k in range(4):
                P.mm(bank(bg), wGLU[:, k, 512 + m * 128:512 + (m + 1) * 128], zT[:, k, :], k == 0, k == 3, [TwC, TzT], [Tb[bg]])
            P.act(ft[2], bank(bg), AF.Sigmoid, [Tb[bg]], [Tft[2]])
            P.tt("dve", sT[:, m, :], bank(bv), ft[2], ALU.mult, [Tb[bv], Tft[2]], [TsT])
        xis_all[g] = xis

    def c1_stepf(g):
        nb = g % 2
        un = 0
        for m in range(8):
            prods = []
            for which in range(2):
                bg, bb = nb_(), nb_()
                gc = which * 1024 + m * 128
                for k in range(8):
                    P.mm(bank(bg), wG[:, k, gc:gc + 128], nT[nb][:, k, :], k == 0, k == 7, [TwC, TnT[nb]], [Tb[bg]])
                if which == 0:
                    for k in range(4):
                        P.mm(bank(bb), wBA[:, k, m * 128:(m + 1) * 128], aTg[nb][:, k, :], k == 0, k == 3, [TwC, Tag[nb]], [Tb[bb]])
                else:
                    for k in range(4):
                        P.mm(bank(bb), wBS[:, k, m * 128:(m + 1) * 128], sT[:, k, :], k == 0, k == 3, [TwC, TsT], [Tb[bb]])
                si = 3 + un % 2
                pi = 5 + un % 4
                un += 1
                P.act(ft[si], bank(bg), AF.Sigmoid, [Tb[bg]], [Tft[si]])
                P.tt("dve", ft[pi], bank(bb), ft[si], ALU.mult, [Tb[bb], Tft[si]], [Tft[pi]])
                prods.append(pi)
            P.tt("pool", mT[:, m, :], ft[prods[0]], ft[prods[1]], ALU.add, [Tft[prods[0]], Tft[prods[1]]], [TmT])

    def c1_stepg(g):
        nb = g % 2
        t0 = g * 512
        xis = xis_all[g]
        for ti in range(4):
            xi = xis[ti]
            for nh in range(2):
                bo = nb_()
                for k in range(8):
                    P.mm(bank(bo), mT[:, k, ti * 128:(ti + 1) * 128], wO[:, k, nh * 512:(nh + 1) * 512], k == 0, k == 7,
                         [TmT, TwC], [Tb[bo]])
                gi_ = 9 if nh else 2
                P.tt("dve", ft[gi_], bank(bo), g1b[:, nh * 512:(nh + 1) * 512], ALU.mult, [Tb[bo], Tgb], [Tft[gi_]])
                P.tt("pool", xt[xi][:, nh * 512:(nh + 1) * 512], xt[xi][:, nh * 512:(nh + 1) * 512], ft[gi_], ALU.add,
                     [Txt[xi], Tft[gi_]], [Txt[xi]])
            P.dma("sp", h_s[t0 + ti * 128:t0 + (ti + 1) * 128, :], xt[xi], f"st_h{xi}", [Txt[xi]], [ThS])

    c1_front(0)
    for g in range(8):
        c1_stepf(g)
        if g + 1 < 8:
            c1_front(g + 1)
        c1_stepg(g)
    P.barrier()
    A.release(mA)
    if stop_after == "pC1":
        return finish(P, nc, [ThS, Tdbg])

    NHC = FH // 128
    wFI = A.alloc([8, 2 * FH], BF16)
    wFO = A.alloc([NHC, D], BF16)
    TwF = T("wF")
    wfi_v = wfi_d.rearrange("(k p) m -> p k m", p=128)
    for c11 in range(11):
        P.dma("pool", wFI[:, :, c11 * 512:(c11 + 1) * 512], wfi_v[:, :, c11 * 512:(c11 + 1) * 512], "ld_wF", [], [TwF])
    wfo_v = wfo_d.rearrange("(k p) m -> p k m", p=128)
    for c2 in range(2):
        P.dma("pool", wFO[:, 11 * c2:11 * (c2 + 1), :], wfo_v[:, 11 * c2:11 * (c2 + 1), :],
              "ld_wF", [], [TwF])
    NX = 4
    ht = [A.alloc([D], F32) for _ in range(NX)]
    Tht = [T(f"ht{i}") for i in range(NX)]
    hn = [A.alloc([D], BF16) for _ in range(2)]
    Thn = [T("hn0"), T("hn1")]
    st = [A.alloc([8], F32) for _ in range(NX)]
    Tst = [T(f"fst{i}") for i in range(NX)]
    n2T = A.alloc([8, 512], BF16)
    Tn2 = T("n2T")
    actT = A.alloc([NHC, 512], BF16)
    TaT = T("actT")
    fs = [A.alloc([512], F32) for _ in range(4)]
    Tfs = [T(f"fs{i}") for i in range(4)]
    Tout = T("out")
    hcnt = 0
    for g in range(8):
        t0 = g * 512
        his = []
        for half in range(2):
            pT = ps[:, 0:1024].bitcast(BF16).rearrange("p (c t) -> p c t", c=8)
            for tl in range(2):
                ti = half * 2 + tl
                hi_ = hcnt % NX
                ni = hcnt % 2
                hcnt += 1
                his.append(hi_)
                P.dma("sp", ht[hi_], h_s[t0 + ti * 128:t0 + (ti + 1) * 128, :], f"ld_h{hi_}", [ThS], [Tht[hi_]])
                jk = actT[:, 0:2, :].rearrange("p a b -> p (a b)")
                P.act(jk, ht[hi_], AF.Square, [Tht[hi_]], [TaT, Tst[hi_]], accum=st[hi_][:, 0:1])
                rstd_ops(P, st[hi_], Tst[hi_])
                P.act(hn[ni], ht[hi_], AF.Copy, [Tht[hi_], Tst[hi_]], [Thn[ni]], scale=st[hi_][:, 2:3])
                for c in range(8):
                    P.tr(pT[:, c, tl * 128:(tl + 1) * 128], hn[ni][:, c * 128:(c + 1) * 128], ident_b, [Thn[ni], Tconst], [Tb[0]])
            for c in range(8):
                P.ts("dve", n2T[:, c, half * 256:(half + 1) * 256], pT[:, c, :], A2[:, c:c + 1], B2[:, c:c + 1],
                     ALU.mult, ALU.add, [Tb[0], Tmod], [Tn2])
        for hc in range(NHC):
            ba_, bb_ = nb_(), nb_()
            for k in range(8):
                P.mm(bank(ba_), wFI[:, k, hc * 128:(hc + 1) * 128], n2T[:, k, :], k == 0, k == 7, [TwF, Tn2], [Tb[ba_]])
            for k in range(8):
                P.mm(bank(bb_), wFI[:, k, FH + hc * 128:FH + (hc + 1) * 128], n2T[:, k, :], k == 0, k == 7, [TwF, Tn2], [Tb[bb_]])
            fi_ = hc % 2
            P.act(fs[fi_], bank(ba_), AF.Silu, [Tb[ba_]], [Tfs[fi_]])
            P.tt("dve", actT[:, hc, :], bank(bb_), fs[fi_], ALU.mult, [Tb[bb_], Tfs[fi_]], [TaT])
        for ti in range(4):
            hi_ = his[ti]
            for nh in range(2):
                bo = nb_()
                for hc in range(NHC):
                    P.mm(bank(bo), actT[:, hc, ti * 128:(ti + 1) * 128], wFO[:, hc, nh * 512:(nh + 1) * 512], hc == 0,
                         hc == NHC - 1, [TaT, TwF], [Tb[bo]])
                P.tt("dve", fs[2 + nh], bank(bo), g2b[:, nh * 512:(nh + 1) * 512], ALU.mult, [Tb[bo], Tgb], [Tfs[2 + nh]])
                P.tt("pool", ht[hi_][:, nh * 512:(nh + 1) * 512], ht[hi_][:, nh * 512:(nh + 1) * 512], fs[2 + nh], ALU.add,
                     [Tht[hi_], Tfs[2 + nh]], [Tht[hi_]])
            sti = st[hi_]
            P.act(hn[ti % 2], ht[hi_], AF.Square, [Tht[hi_]], [Thn[ti % 2], Tst[hi_]], accum=sti[:, 4:5])
            P.ts("dve", sti[:, 5:6], sti[:, 4:5], 1.0 / D, EPS, ALU.mult, ALU.add, [Tst[hi_]], [Tst[hi_]])
            P.act(sti[:, 6:7], sti[:, 5:6], AF.Sqrt, [Tst[hi_]], [Tst[hi_]])
            P.emit("dve", lambda e, o=sti[:, 7:8], i=sti[:, 6:7]: e.reciprocal(out=o, in_=i), [Tst[hi_]], [Tst[hi_]])
            P.stt("dve", ht[hi_], ht[hi_], sti[:, 7:8], fgb, ALU.mult, ALU.mult, [Tht[hi_], Tst[hi_], Tgb], [Tht[hi_]])
            P.dma("sp", out_d[t0 + ti * 128:t0 + (ti + 1) * 128, :], ht[hi_], f"st_o{hi_}", [Tht[hi_]], [Tout])
    return finish(P, nc, [Tout, Tdbg])


def rstd_ops(P, st, Tst):
    P.ts("dve", st[:, 1:2], st[:, 0:1], 1.0 / D, EPS, ALU.mult, ALU.add, [Tst], [Tst])
    P.act(st[:, 3:4], st[:, 1:2], AF.Sqrt, [Tst], [Tst])
    P.emit("dve", lambda e: e.reciprocal(out=st[:, 2:3], in_=st[:, 3:4]), [Tst], [Tst])


def finish(P, nc, tiles=()):
    P.barrier()
    nc._marks = P.marks
    P.final_wait("sp", list(tiles))
    P.build()
    return nc


def _pcol(v, nchunk):
    return np.ascontiguousarray(v.reshape(nchunk, 128).T)


def host_shared(inp):
    f32 = np.float32
    sh = {}
    w_in = inp["w_in"][0]
    idx = np.concatenate([np.arange(16, 32), np.arange(0, 16), np.arange(48, 64), np.arange(32, 48)])
    swap = np.concatenate([h * 64 + idx for h in range(NH)])
    sh["w_in_ext"] = np.ascontiguousarray(np.concatenate([w_in[:, :2048], w_in[:, swap], w_in[:, 512 + swap]], axis=1))
    sh["w_gate"] = np.ascontiguousarray(w_in[:, 2048:4096])
    sh["w_mod"] = np.ascontiguousarray(inp["w_mod"][0])
    sh["b_mod"] = _pcol(inp["b_mod"][0], 48)
    sh["gA"] = _pcol(inp["attn_norm_g"][0], 8)
    sh["gF"] = _pcol(inp["ffn_norm_g"][0], 8)
    sh["gFin"] = _pcol(inp["final_norm_g"], 8)
    t = np.arange(L)
    inv = (10000.0 ** (-np.arange(16, dtype=f32) / 16)).astype(f32)
    cos = np.zeros((64, L), f32)
    sin = np.zeros((64, L), f32)
    for half, pos in ((0, t // GW), (1, t % GW)):
        ang = pos.astype(f32)[None, :] * inv[:, None]
        c_, s_ = np.cos(ang).astype(f32), np.sin(ang).astype(f32)
        cos[half * 32:half * 32 + 16] = c_
        cos[half * 32 + 16:half * 32 + 32] = c_
        sin[half * 32:half * 32 + 16] = -s_
        sin[half * 32 + 16:half * 32 + 32] = s_
    sh["rope_cos"] = np.ascontiguousarray(np.concatenate([cos, cos], 0))
    sh["rope_sin"] = np.ascontiguousarray(np.concatenate([sin, sin], 0))
    rpb = inp["rel_pos_bias"][0]
    q = np.arange(64)[:, None]
    k = np.arange(64)[None, :]
    dc = np.clip(k - q + 15, 0, 30)
    tb = np.zeros((128, 4, 15, 64), f32)
    for j in range(4):
        for hh in range(2):
            tb[hh * 64:(hh + 1) * 64, j] = np.transpose(rpb[2 * j + hh][:, dc], (1, 0, 2))
    sh["bias_tb"] = tb
    cs = np.clip(q - 8, 0, 48)
    inw = (k >= cs) & (k < cs + 16)
    mk = np.where(inw, 0.0, -1e30).astype(f32)
    sh["bias_mask"] = np.ascontiguousarray(np.concatenate([mk, mk], 0))
    sh["lam_re"] = np.ascontiguousarray(inp["ssm_lambda_re"][0].reshape(32, 128))
    sh["lam_im"] = np.ascontiguousarray(inp["ssm_lambda_im"][0].reshape(32, 128))
    sh["log_dt"] = np.ascontiguousarray(inp["ssm_log_dt"][0].reshape(32, 2))
    sh["b_re"] = np.ascontiguousarray(inp["ssm_b_re"][0])
    sh["b_im"] = np.ascontiguousarray(inp["ssm_b_im"][0])
    sh["c_re"] = np.ascontiguousarray(inp["ssm_c_re"][0])
    sh["c_im"] = np.ascontiguousarray(inp["ssm_c_im"][0])
    sh["d_skip"] = _pcol(inp["ssm_d"][0], 4)
    sh["w_glu"] = np.ascontiguousarray(inp["w_glu"][0])
    sh["w_ba"] = np.ascontiguousarray(inp["w_branch_attn"][0])
    sh["w_bs"] = np.ascontiguousarray(inp["w_branch_ssm"][0])
    sh["w_out"] = np.ascontiguousarray(inp["w_out"][0])
    sh["w_ffn_in"] = np.ascontiguousarray(inp["w_ffn_in"][0])
    sh["w_ffn_out"] = np.ascontiguousarray(inp["w_ffn_out"][0])
    return sh


def host_core(inp, b):
    cv = np.stack([inp["c"][b], inp["c_ctx"]], axis=1)
    return {
        "x": np.ascontiguousarray(inp["x"][b]),
        "ctx": np.ascontiguousarray(inp["ctx"][b]),
        "cvec": np.ascontiguousarray(cv.reshape(8, 128, 2).transpose(1, 0, 2)),
    }


_NC_CACHE = {}


def kernel(**inputs):
    inp = {k: np.asarray(v) for k, v in inputs.items()}
    if "nc" not in _NC_CACHE:
        _NC_CACHE["nc"] = build_program()
    nc = _NC_CACHE["nc"]
    sh = host_shared(inp)
    in_maps = [{**sh, **host_core(inp, b)} for b in range(8)]
    res = run_bass_kernel_spmd(nc, in_maps, core_ids=list(range(8)))
    return np.stack([np.asarray(r["out"]) for r in res.results], axis=0).astype(np.float32)
```
